# Optimizing a Trainium2 kernel written in Bass

```python
import math
import jax, jax.numpy as jnp
from jax import lax
import numpy as np

D_MODEL = 1024
BATCH = 8
SEQ = 4096
DEPTH = 4

CTX_LEN = 256
GRID_W = 64
N_MIXERS = 2
N_ATTN_LAYERS = (DEPTH + 1) // 2
N_SSD_LAYERS = DEPTH // 2
EPS = 1e-6

ATTN_HEAD_DIM = 64
ATTN_HEADS = D_MODEL // ATTN_HEAD_DIM
ATTN_KV_HEADS = 4
ATTN_GROUP = ATTN_HEADS // ATTN_KV_HEADS
ATTN_WIDTH = ATTN_HEADS * ATTN_HEAD_DIM
ATTN_KV_WIDTH = ATTN_KV_HEADS * ATTN_HEAD_DIM
ATTN_IN = 2 * ATTN_WIDTH + 2 * ATTN_KV_WIDTH
WINDOW = 128
BLOCK = 128
ROPE_BASE = 10000.0

SSD_WIDTH = 2 * D_MODEL
SSD_HEAD_DIM = 64
SSD_HEADS = SSD_WIDTH // SSD_HEAD_DIM
SSD_GROUPS = 8
SSD_HEADS_PER_GROUP = SSD_HEADS // SSD_GROUPS
SSD_STATE = 128
SSD_CONV = 5
SSD_CHUNK = 128
SSD_CONV_DIM = SSD_WIDTH + 2 * SSD_GROUPS * SSD_STATE
SSD_IN = SSD_WIDTH + SSD_CONV_DIM + 2 * SSD_HEADS

kernel_name = "hybrid_swa_ssd_diffusion_prefix"


def rms_normalize(x):
    xf = x.astype(jnp.float32)
    return (xf * lax.rsqrt(jnp.mean(xf * xf, axis=-1, keepdims=True) + EPS)).astype(x.dtype)


def adaln(cond, w, b):
    mod = jax.nn.silu(cond) @ w + b
    return jnp.split(mod, 3, axis=-1)


def rope_1d(x, pos):
    half = x.shape[-1]
    inv_freq = ROPE_BASE ** (-jnp.arange(0, half, 2, dtype=jnp.float32) / half)
    ang = pos.astype(jnp.float32)[:, None] * inv_freq[None, :]
    ang = jnp.concatenate([ang, ang], axis=-1)
    shape = (pos.shape[0],) + (1,) * (x.ndim - 3) + (half,)
    cos = jnp.cos(ang).reshape(shape)
    sin = jnp.sin(ang).reshape(shape)
    x1, x2 = jnp.split(x, 2, axis=-1)
    rot = jnp.concatenate([-x2, x1], axis=-1)
    return (x * cos + rot * sin).astype(x.dtype)


def axial_rope(x, row, col):
    half = x.shape[-1] // 2
    return jnp.concatenate([rope_1d(x[..., :half], row), rope_1d(x[..., half:], col)], axis=-1)


def sink_softmax(scores, sink_kg, mask):
    if mask is not None:
        scores = jnp.where(mask, scores, -jnp.inf)
    sink_col = jnp.broadcast_to(sink_kg.astype(jnp.float32)[None, :, :, None, None],
                                scores.shape[:-1] + (1,))
    p = jax.nn.softmax(jnp.concatenate([scores, sink_col], axis=-1), axis=-1)
    return p[..., :-1]


def attention_mixer(h_lat, h_ctx, w_in, sink, w_out, row, col, need_ctx_out):
    bsz, L, _ = h_lat.shape

    def project(h):
        n = h.shape[1]
        q, k, v, g = jnp.split(h @ w_in, [ATTN_WIDTH, ATTN_WIDTH + ATTN_KV_WIDTH,
                                          ATTN_WIDTH + 2 * ATTN_KV_WIDTH], axis=-1)
        q = q.reshape(bsz, n, ATTN_KV_HEADS, ATTN_GROUP, ATTN_HEAD_DIM)
        k = k.reshape(bsz, n, ATTN_KV_HEADS, ATTN_HEAD_DIM)
        v = v.reshape(bsz, n, ATTN_KV_HEADS, ATTN_HEAD_DIM)
        return q, k, v, g

    q_c, k_c, v_c, g_c = project(h_ctx)
    q, k, v, g = project(h_lat)
    q = axial_rope(q, row, col)
    k = axial_rope(k, row, col)
    scale = ATTN_HEAD_DIM ** -0.5
    sink_kg = sink.reshape(ATTN_KV_HEADS, ATTN_GROUP)

    n_blocks = L // BLOCK
    pad = jnp.zeros((bsz, BLOCK, ATTN_KV_HEADS, ATTN_HEAD_DIM), k.dtype)
    k_pad = jnp.concatenate([pad, k, pad], axis=1)
    v_pad = jnp.concatenate([pad, v, pad], axis=1)
    ctx_mask = jnp.ones((BLOCK, CTX_LEN), dtype=bool)

    def block(j):
        start = j * BLOCK
        qb = lax.dynamic_slice_in_dim(q, start, BLOCK, axis=1)
        kb = lax.dynamic_slice_in_dim(k_pad, start, 3 * BLOCK, axis=1)
        vb = lax.dynamic_slice_in_dim(v_pad, start, 3 * BLOCK, axis=1)
        kb = jnp.concatenate([kb, k_c], axis=1)
        vb = jnp.concatenate([vb, v_c], axis=1)
        s = jnp.einsum('bqkgd,bskd->bkgqs', qb, kb).astype(jnp.float32) * scale
        qpos = start + jnp.arange(BLOCK)
        kpos = start - BLOCK + jnp.arange(3 * BLOCK)
        band = ((kpos[None, :] >= 0) & (kpos[None, :] < L)
                & (jnp.abs(qpos[:, None] - kpos[None, :]) <= WINDOW))
        mask = jnp.concatenate([band, ctx_mask], axis=1)
        p = sink_softmax(s, sink_kg, mask).astype(vb.dtype)
        return jnp.einsum('bkgqs,bskd->bqkgd', p, vb)

    o = lax.map(block, jnp.arange(n_blocks))
    o = jnp.moveaxis(o, 0, 1).reshape(bsz, L, ATTN_WIDTH)
    y_lat = ((o * jax.nn.silu(g)) @ w_out).astype(h_lat.dtype)

    y_ctx = None
    if need_ctx_out:
        s = jnp.einsum('bqkgd,bskd->bkgqs', q_c, k_c).astype(jnp.float32) * scale
        p = sink_softmax(s, sink_kg, None).astype(v_c.dtype)
        o_c = jnp.einsum('bkgqs,bskd->bqkgd', p, v_c).reshape(bsz, CTX_LEN, ATTN_WIDTH)
        y_ctx = ((o_c * jax.nn.silu(g_c)) @ w_out).astype(h_ctx.dtype)
    return y_lat, y_ctx


def centred_depthwise_conv(u, w, b):
    out = lax.conv_general_dilated(u, w[:, None, :].astype(u.dtype), window_strides=(1,),
                                   padding=[(SSD_CONV // 2, SSD_CONV // 2)],
                                   dimension_numbers=('NWC', 'WIO', 'NWC'),
                                   feature_group_count=u.shape[-1])
    return out + b


def ssd_chunk_pass(xdt, da, bm, cm, h0, want_y):
    bsz, L = xdt.shape[:2]
    nc = L // SSD_CHUNK
    xdt = xdt.reshape(bsz, nc, SSD_CHUNK, SSD_GROUPS, SSD_HEADS_PER_GROUP, SSD_HEAD_DIM)
    da = da.reshape(bsz, nc, SSD_CHUNK, SSD_GROUPS, SSD_HEADS_PER_GROUP)
    bm = bm.reshape(bsz, nc, SSD_CHUNK, SSD_GROUPS, SSD_STATE)
    cm = cm.reshape(bsz, nc, SSD_CHUNK, SSD_GROUPS, SSD_STATE)
    a_cs = jnp.cumsum(da, axis=2)
    a_last = a_cs[:, :, -1]
    decay_to_end = jnp.exp(a_last[:, :, None] - a_cs)
    states = jnp.einsum('bcsgn,bcsgr,bcsgrp->bcgrpn', bm, decay_to_end, xdt)

    def step(h, inp):
        st, al = inp
        return jnp.exp(al)[..., None, None] * h + st, h

    h_final, h_prev = lax.scan(step, h0, (jnp.moveaxis(states, 1, 0), jnp.moveaxis(a_last, 1, 0)))
    if not want_y:
        return None, h_final
    h_prev = jnp.moveaxis(h_prev, 0, 1)
    diff = a_cs[:, :, :, None] - a_cs[:, :, None, :]
    lower = jnp.tril(jnp.ones((SSD_CHUNK, SSD_CHUNK), dtype=bool))[:, :, None, None]
    lmat = jnp.exp(jnp.where(lower, diff, -jnp.inf))
    cb = jnp.einsum('bcqgn,bcsgn->bcqsg', cm, bm)
    y_diag = jnp.einsum('bcqsg,bcqsgr,bcsgrp->bcqgrp', cb, lmat, xdt)
    y_off = jnp.einsum('bcqgn,bcgrpn,bcqgr->bcqgrp', cm, h_prev, jnp.exp(a_cs))
    y = (y_diag + y_off).reshape(bsz, L, SSD_GROUPS, SSD_HEADS_PER_GROUP, SSD_HEAD_DIM)
    return y, h_final


def ssd_mixer(h_lat, h_ctx, w_in, conv_w, conv_b, dt_bias, a_log, d_skip, norm_w, w_out,
              need_ctx_out):
    bsz = h_lat.shape[0]
    f32 = jnp.float32
    a_dir = -jnp.exp(a_log.astype(f32))
    dt_bias_dir = dt_bias.astype(f32).reshape(2, SSD_GROUPS, SSD_HEADS_PER_GROUP)

    def front(h):
        n = h.shape[1]
        z, xbc, dt_raw = jnp.split(h @ w_in, [SSD_WIDTH, SSD_WIDTH + SSD_CONV_DIM], axis=-1)
        xbc = jax.nn.silu(centred_depthwise_conv(xbc, conv_w, conv_b))
        xs, bm, cm = jnp.split(xbc, [SSD_WIDTH, SSD_WIDTH + SSD_GROUPS * SSD_STATE], axis=-1)
        xs = xs.reshape(bsz, n, SSD_GROUPS, SSD_HEADS_PER_GROUP, SSD_HEAD_DIM).astype(f32)
        bm = bm.reshape(bsz, n, SSD_GROUPS, SSD_STATE).astype(f32)
        cm = cm.reshape(bsz, n, SSD_GROUPS, SSD_STATE).astype(f32)
        dt = jax.nn.softplus(dt_raw.astype(f32).reshape(bsz, n, 2, SSD_GROUPS, SSD_HEADS_PER_GROUP)
                             + dt_bias_dir)
        return z, xs, bm, cm, dt

    def scan_dir(xs, bm, cm, dt, d, h0, want_y):
        dt_d = dt[:, :, d]
        xdt = xs * dt_d[..., None]
        da = dt_d * a_dir[d].reshape(SSD_GROUPS, SSD_HEADS_PER_GROUP)
        if d == 1:
            xdt, da, bm, cm = (jnp.flip(t, axis=1) for t in (xdt, da, bm, cm))
        y, h_fin = ssd_chunk_pass(xdt, da, bm, cm, h0, want_y)
        if want_y and d == 1:
            y = jnp.flip(y, axis=1)
        return y, h_fin

    def back(y_f, y_b, xs, z):
        n = xs.shape[1]
        y = y_f + y_b + d_skip.astype(f32).reshape(SSD_GROUPS, SSD_HEADS_PER_GROUP, 1) * xs
        u = y.reshape(bsz, n, SSD_WIDTH) * jax.nn.silu(z.astype(f32))
        u = rms_normalize(u.reshape(bsz, n, SSD_GROUPS, SSD_WIDTH // SSD_GROUPS))
        return (u.reshape(bsz, n, SSD_WIDTH) * norm_w) @ w_out

    z_c, xs_c, b_c, c_c, dt_c = front(h_ctx)
    z, xs, bm, cm, dt = front(h_lat)
    h0 = jnp.zeros((bsz, SSD_GROUPS, SSD_HEADS_PER_GROUP, SSD_HEAD_DIM, SSD_STATE), f32)
    y_c_f, s_f = scan_dir(xs_c, b_c, c_c, dt_c, 0, h0, need_ctx_out)
    y_c_b, s_b = scan_dir(xs_c, b_c, c_c, dt_c, 1, h0, need_ctx_out)
    y_f, _ = scan_dir(xs, bm, cm, dt, 0, s_f, True)
    y_b, _ = scan_dir(xs, bm, cm, dt, 1, s_b, True)
    y_lat = back(y_f, y_b, xs, z).astype(h_lat.dtype)
    y_ctx = back(y_c_f, y_c_b, xs_c, z_c).astype(h_ctx.dtype) if need_ctx_out else None
    return y_lat, y_ctx


def setup_inputs(seed: int = 0) -> dict:
    key = jax.random.key(seed)
    ks = jax.random.split(key, 20)
    f32 = jnp.float32
    nrm = lambda k, shape, s: jax.random.normal(k, shape, f32) * s
    dt0 = jnp.exp(jax.random.uniform(ks[12], (N_SSD_LAYERS, 2, SSD_HEADS), f32,
                                     math.log(1e-3), math.log(1e-1)))
    return {
        'x': nrm(ks[0], (BATCH, SEQ, D_MODEL), 1.0),
        'c': nrm(ks[1], (BATCH, D_MODEL), 1.0),
        'ctx': nrm(ks[2], (BATCH, CTX_LEN, D_MODEL), 1.0),
        'c_ctx': nrm(ks[3], (D_MODEL,), 1.0),
        'w_ada': nrm(ks[4], (DEPTH, D_MODEL, 3 * D_MODEL), 0.5 * D_MODEL ** -0.5),
        'b_ada': nrm(ks[5], (DEPTH, 3 * D_MODEL), 0.02),
        'attn_w_in': nrm(ks[6], (N_ATTN_LAYERS, D_MODEL, ATTN_IN), D_MODEL ** -0.5),
        'attn_sink': nrm(ks[7], (N_ATTN_LAYERS, ATTN_HEADS), 0.5),
        'attn_w_out': nrm(ks[8], (N_ATTN_LAYERS, ATTN_WIDTH, D_MODEL), ATTN_WIDTH ** -0.5),
        'ssd_w_in': nrm(ks[9], (N_SSD_LAYERS, D_MODEL, SSD_IN), D_MODEL ** -0.5),
        'ssd_conv_w': nrm(ks[10], (N_SSD_LAYERS, SSD_CONV, SSD_CONV_DIM), SSD_CONV ** -0.5),
        'ssd_conv_b': nrm(ks[11], (N_SSD_LAYERS, SSD_CONV_DIM), 0.02),
        'ssd_dt_bias': dt0 + jnp.log(-jnp.expm1(-dt0)),
        'ssd_a_log': jnp.log(jax.random.uniform(ks[13], (N_SSD_LAYERS, 2, SSD_HEADS), f32, 1.0, 16.0)),
        'ssd_d': 1.0 + nrm(ks[14], (N_SSD_LAYERS, SSD_HEADS), 0.1),
        'ssd_norm_w': 1.0 + nrm(ks[15], (N_SSD_LAYERS, SSD_WIDTH), 0.1),
        'ssd_w_out': nrm(ks[16], (N_SSD_LAYERS, SSD_WIDTH, D_MODEL), SSD_WIDTH ** -0.5),
        'final_norm_w': 1.0 + nrm(ks[17], (D_MODEL,), 0.1),
    }


def reference(x, c, ctx, c_ctx, w_ada, b_ada, attn_w_in, attn_sink, attn_w_out, ssd_w_in,
              ssd_conv_w, ssd_conv_b, ssd_dt_bias, ssd_a_log, ssd_d, ssd_norm_w, ssd_w_out,
              final_norm_w):
    L = x.shape[1]
    ROWS = L // GRID_W
    row = jnp.repeat(jnp.arange(ROWS, dtype=jnp.int32), GRID_W)
    col = jnp.tile(jnp.arange(GRID_W, dtype=jnp.int32), ROWS)
    h, hc = x, ctx
    for i in range(DEPTH):
        need_ctx_out = i < DEPTH - 1
        shift, scale, gate = adaln(c, w_ada[i], b_ada[i])
        shift_c, scale_c, gate_c = adaln(c_ctx, w_ada[i], b_ada[i])
        u = rms_normalize(h) * (1.0 + scale[:, None]) + shift[:, None]
        uc = rms_normalize(hc) * (1.0 + scale_c) + shift_c
        j = i // N_MIXERS
        if i % N_MIXERS == 0:
            y, yc = attention_mixer(u, uc, attn_w_in[j], attn_sink[j], attn_w_out[j],
                                    row, col, need_ctx_out)
        else:
            y, yc = ssd_mixer(u, uc, ssd_w_in[j], ssd_conv_w[j], ssd_conv_b[j], ssd_dt_bias[j],
                              ssd_a_log[j], ssd_d[j], ssd_norm_w[j], ssd_w_out[j], need_ctx_out)
        h = h + gate[:, None] * y
        if need_ctx_out:
            hc = hc + gate_c * yc
    return rms_normalize(h) * final_norm_w
```

```python
import math
from contextlib import ExitStack

import numpy as np
import concourse.bass as bass
import concourse.mybir as mybir
from concourse.bass_utils import run_bass_kernel_spmd

F32 = mybir.dt.float32
BF16 = mybir.dt.bfloat16
AF = mybir.ActivationFunctionType
ALU = mybir.AluOpType
AX = mybir.AxisListType

SAME_ENGINE_WAITS = True
DBG = {}


class Q:
    def __init__(self, mk, name, eng, is_pe=False):
        self.mk = mk
        self.name = name
        self.eng = eng
        self.sem = mk.new_sem("q_" + name)
        self.count = 0
        self.seen = {}
        self.is_pe = is_pe

    def wait_tokens(self, toks):
        best = {}
        for tok in toks:
            if tok is None:
                continue
            sem, val, q = tok
            if q is self and (self.is_pe or not SAME_ENGINE_WAITS):
                continue
            k = id(sem)
            if self.seen.get(k, 0) >= val:
                continue
            if k not in best or best[k][1] < val:
                best[k] = (sem, val)
        for k, (sem, val) in best.items():
            self.eng.wait_ge(sem, val)
            self.seen[k] = val


class T:
    def __init__(self, name, ap=None, dsem=None):
        self.name = name
        self.ap = ap
        self.w = []
        self.r = []
        self.dsem = dsem
        self.dcount = 0
        self.psum = False

    def __getitem__(self, k):
        return self.ap[k]


class MK:
    def __init__(self, nc, es):
        self.nc = nc
        self.es = es
        self.nsem = 0
        self.pe = Q(self, "pe", nc.tensor, is_pe=True)
        self.act = Q(self, "act", nc.scalar)
        self.dve = Q(self, "dve", nc.vector)
        self.pool = Q(self, "pool", nc.gpsimd)
        self.sp = Q(self, "sp", nc.sync)
        self.n_inst = 0
        self.dts = []

    def new_sem(self, name):
        self.nsem += 1
        return self.es.enter_context(self.nc.semaphore(f"{name}_{self.nsem}"))

    def sbuf(self, name, shape, dtype, dma=False):
        h = self.es.enter_context(self.nc.sbuf_tensor(name, list(shape), dtype))
        t = T(name, h, self.new_sem("d_" + name) if dma else None)
        if dma:
            self.dts.append(t)
        return t

    def psum(self, name, shape, dtype=F32):
        h = self.es.enter_context(self.nc.psum_tensor(name, list(shape), dtype))
        t = T(name, h)
        t.psum = True
        return t

    def dram(self, name, shape, dtype, kind="Internal"):
        h = self.nc.dram_tensor(name, list(shape), dtype, kind=kind)
        return T(name, h.ap() if hasattr(h, "ap") else h)

    def op(self, q, fn, reads=(), writes=(), inc=True):
        toks = []
        for t in reads:
            toks += t.w
            if t.psum:
                toks += [x for x in t.r if x[2] is not q]
        for t in writes:
            toks += t.w
            toks += t.r
        q.wait_tokens(toks)
        ins = fn(q.eng)
        self.n_inst += 1
        if not inc:
            tok = (q.sem, q.count + 1, q)
        else:
            q.count += 1
            ins.then_inc(q.sem, 1)
            tok = (q.sem, q.count, q)
        for t in writes:
            t.w = [tok]
            t.r = []
        for t in reads:
            if t not in writes:
                t.r.append(tok)
                if len(t.r) > 24:
                    t.r = _prune(t.r)
        return ins

    def dma(self, q, out, in_, side, reads=(), writes=(), n_parts=None, **kw):
        pairs = out if isinstance(out, list) else [(out, in_)]
        toks = []
        for t in reads:
            toks += t.w
        for t in writes:
            toks += t.w
            toks += t.r
        if side.dcount:
            toks.append((side.dsem, side.dcount, None))
        q.wait_tokens(toks)
        for (o, i) in pairs:
            q.eng.dma_start(out=o, in_=i, **kw).then_inc(side.dsem, 16)
            side.dcount += 16
            self.n_inst += 1
        tok = (side.dsem, side.dcount, None)
        for t in writes:
            t.w = [tok]
            t.r = []
        for t in reads:
            if t not in writes:
                t.r.append(tok)
                if len(t.r) > 24:
                    t.r = _prune(t.r)

    def barrier(self):
        toks = [(p.sem, p.count, None) for p in self.queues() if p.count]
        toks += [(t.dsem, t.dcount, None) for t in self.dts if t.dcount]
        for q in self.queues():
            q.wait_tokens(toks)

    def queues(self):
        return [self.pe, self.act, self.dve, self.pool, self.sp]

    def finish(self, outs):
        toks = []
        for t in outs:
            toks += t.w
        self.sp.wait_tokens(toks)


def _prune(toks):
    best = {}
    for (sem, val, q) in toks:
        k = id(sem)
        if k not in best or best[k][1] < val:
            best[k] = (sem, val, q)
    return list(best.values())


D = 1024
KD = 8
TC = 256
TL = 4096
TT = TC + TL
NT = TT // 128
EPS = 1e-6
NEG = -30000.0
GROUPS = [(0, 256)] + [(256 + 512 * i, 512) for i in range(8)]
UCOLS = TT + 8


def ucol(tok):
    return tok + (2 if tok < TC else 6)


class Arena:
    def __init__(self, mk, nbytes):
        self.mk = mk
        self.h = mk.es.enter_context(mk.nc.sbuf_tensor("arena", [128, nbytes // 4], F32))
        self.cap = nbytes // 4
        self.off = 0
        self.cache = {}

    def reset(self, off=0):
        self.off = off

    def alloc(self, key, shape, dtype, dma=False):
        nel = 1
        for s_ in shape[1:]:
            nel *= s_
        nb = nel * (4 if dtype == F32 else 2)
        n4 = (nb + 3) // 4
        n4 = (n4 + 7) // 8 * 8
        assert self.off + n4 <= self.cap, f"arena overflow at {key}: {self.off + n4} > {self.cap}"
        ck = (key, self.off, tuple(shape), str(dtype))
        off = self.off
        self.off += n4
        if ck in self.cache:
            return self.cache[ck]
        ap = self.h[:, off:off + n4]
        if dtype != F32:
            ap = ap.bitcast(dtype)
        ap = ap[:, 0:nel]
        if len(shape) == 3:
            ap = ap.rearrange("p (a b) -> p a b", a=shape[1])
        elif len(shape) == 4:
            ap = ap.rearrange("p (a b c) -> p a b c", a=shape[1], b=shape[2])
        t = T(key, ap, self.mk.new_sem("d_" + key) if dma else None)
        if dma:
            self.mk.dts.append(t)
        self.cache[ck] = t
        return t


class Rot:
    def __init__(self, items):
        self.items = items
        self.i = 0

    def next(self):
        t = self.items[self.i % len(self.items)]
        self.i += 1
        return t


def bview(bank_ap):
    return bank_ap.bitcast(BF16)


class Prog:
    def __init__(self, n_layers=4, dump_h=False):
        self.n_layers = n_layers
        self.dump_h = dump_h
        self.nc = bass.Bass("TRN2", target_bir_lowering=False)
        self.es = ExitStack()

    def inp(self, name, shape):
        return self.nc.dram_tensor(name, list(shape), F32, kind="ExternalInput").ap()

    def build(self):
        nc = self.nc
        with self.es:
            mk = self.mk = MK(nc, self.es)
            self.x_d = self.inp("x", [TL, D])
            self.ctx_d = self.inp("ctx", [TC, D])
            self.c_col2 = self.inp("c_col2", [128, 8, 2])
            self.w_ada = self.inp("w_ada", [4, 128, 8, 3072])
            self.b_col = self.inp("b_col", [128, 4, 16])
            self.b_gate = self.inp("b_gate", [4, 1024])
            self.a_win = self.inp("attn_w_in", [2, 128, 8, 2560])
            self.a_wout = self.inp("attn_w_out", [2, 128, 8, 1024])
            self.a_sink = self.inp("attn_sink", [2, 16])
            self.s_wxbc = self.inp("ssd_w_xbc", [2, 32, 128, 1024])
            self.s_wzdt = self.inp("ssd_w_zdt", [2, 128, 8, 2112])
            self.s_convw = self.inp("ssd_conv_w", [128, 2, 32, 5])
            self.s_convb = self.inp("ssd_conv_b", [128, 2, 32])
            self.s_dtb = self.inp("ssd_dt_bias", [2, 64])
            self.s_alog = self.inp("ssd_a_log", [2, 64])
            self.s_d = self.inp("ssd_d", [2, 32])
            self.s_nw = self.inp("ssd_norm_w", [128, 2, 16])
            self.s_wout = self.inp("ssd_w_out", [2, 128, 16, 1024])
            self.fnw = self.inp("final_norm_w", [1, 1024])
            self.k_ident = self.inp("k_ident", [128, 128])
            self.k_prot = self.inp("k_prot", [128, 128])
            self.k_neglo = self.inp("k_neglo", [128, 512])
            self.k_neghi = self.inp("k_neghi", [128, 512])
            self.k_cos = self.inp("k_cos", [128, TT])
            self.k_sin = self.inp("k_sin", [128, TT])
            self.k_tri = self.inp("k_tri", [128, 4, 128])
            self.k_sel = self.inp("k_sel", [64, 32 * 128])
            if self.dump_h:
                self.out_d = nc.dram_tensor("out", [TT, D], F32, kind="ExternalOutput").ap()
            else:
                self.out_d = nc.dram_tensor("out", [TL, D], F32, kind="ExternalOutput").ap()
            self.h_d = nc.dram_tensor("h_scr", [TT, D], F32).ap()
            self.gates_d = nc.dram_tensor("gates_scr", [8, D], F32).ap()
            self.xs_d = nc.dram_tensor("xs_scr", [TT, 2048], BF16).ap()
            self.bt_d = nc.dram_tensor("btok_scr", [TT, 1024], BF16).ap()
            self.BT_d = nc.dram_tensor("BT_scr", [8, 128, TT], BF16).ap()
            self.CT_d = nc.dram_tensor("CT_scr", [8, 128, TT], BF16).ap()
            self.z_d = nc.dram_tensor("z_scr", [TT, 2048], BF16).ap()
            self.hp_d = nc.dram_tensor("hprev_scr", [2, NT, 128, 2048], BF16).ap()
            self.HT = [T(f"h_tile{t}") for t in range(NT)]
            self.GT = T("gates")
            self.XS = [T(f"xs{t}") for t in range(NT)]
            self.BTK = [T(f"btk{t}") for t in range(NT)]
            self.BTT = T("BT")
            self.CTT = T("CT")
            self.ZT = [T(f"z{t}") for t in range(NT)]
            self.HP = [[T(f"hp{d}_{t}") for t in range(NT)] for d in range(2)]
            self.OUT = [T(f"out{t}") for t in range(NT)]
            self.BTT = [T(f"BT{g}") for g in range(8)]
            self.CTT = [T(f"CT{g}") for g in range(8)]
            self.XIN = T("xin")
            self.ident = mk.sbuf("ident", [128, 128], BF16, dma=True)
            self.prot = mk.sbuf("prot", [128, 128], BF16, dma=True)
            self.neglo = mk.sbuf("neglo", [128, 512], BF16, dma=True)
            self.neghi = mk.sbuf("neghi", [128, 512], BF16, dma=True)
            self.tri = mk.sbuf("tri", [128, 4, 128], F32, dma=True)
            self.onesf = mk.sbuf("onesf", [128, 128], F32)
            self.identf = mk.sbuf("identf", [128, 128], F32, dma=True)
            self.modcol = mk.sbuf("modcol", [128, 4, 16, 2], F32)
            self.banks = [mk.psum(f"bank{i}", [128, 512], F32) for i in range(8)]
            self.arena = Arena(mk, 200 * 1024)
            K = T("consts")
            mk.dma(mk.pool, self.ident[:], self.k_ident, self.ident, reads=[K], writes=[self.ident])
            mk.dma(mk.pool, self.prot[:], self.k_prot, self.prot, reads=[K], writes=[self.prot])
            mk.dma(mk.pool, self.neglo[:], self.k_neglo, self.neglo, reads=[K], writes=[self.neglo])
            mk.dma(mk.pool, self.neghi[:], self.k_neghi, self.neghi, reads=[K], writes=[self.neghi])
            mk.dma(mk.sp, self.tri[:], self.k_tri, self.tri, reads=[K], writes=[self.tri])
            mk.dma(mk.sp, self.identf[:], self.k_ident, self.identf, reads=[K], writes=[self.identf])
            mk.op(mk.dve, lambda e: e.memset(self.onesf[:], 1.0), writes=[self.onesf])

            self.prologue()
            for li in range(self.n_layers):
                mk.barrier()
                if li % 2 == 0:
                    self.attn_layer(li)
                else:
                    self.ssd_layer(li)
            if self.dump_h:
                self.dump()
            mk.barrier()
        return nc

    def mm_group(self, bank_t, out_ap, pairs, reads, start=True, stop=True):
        mk = self.mk
        n = len(pairs)
        for i, (l, r) in enumerate(pairs):
            mk.op(mk.pe, lambda e, l=l, r=r, i=i: e.matmul(out_ap, l, r, start=(start and i == 0), stop=(stop and i == n - 1)),
                  reads=reads, writes=[bank_t], inc=(i == n - 1))

    def h_src(self, li, t):
        if li == 0:
            return (self.ctx_d[t * 128:(t + 1) * 128, :] if t < 2 else self.x_d[(t - 2) * 128:(t - 1) * 128, :]), self.XIN
        return self.h_d[t * 128:(t + 1) * 128, :], self.HT[t]

    def prologue(self):
        mk, A = self.mk, self.arena
        A.reset()
        wadar = Rot([A.alloc(f"wada{i}", [128, 8, 3072], BF16, dma=True) for i in range(2)])
        ccol = A.alloc("ccol", [128, 8, 2], F32, dma=True)
        sc2 = A.alloc("sc2", [128, 8, 2], BF16)
        bcol = A.alloc("bcol", [128, 4, 16], F32, dma=True)
        bg = A.alloc("bg", [128, 1024], F32, dma=True)
        grow = A.alloc("grow", [128, 1024], F32, dma=True)
        K = T("pin")
        mk.dma(mk.sp, ccol[:], self.c_col2, ccol, reads=[K], writes=[ccol])
        mk.dma(mk.sp, bcol[:], self.b_col, bcol, reads=[K], writes=[bcol])
        mk.op(mk.act, lambda e: e.activation(out=sc2[:], in_=ccol[:], func=AF.Silu), reads=[ccol], writes=[sc2])
        wadas = []
        for l in range(4):
            wada = wadar.next()
            if l < 2:
                mk.dma(mk.pool, [(wada[:, k, :], self.w_ada[l, :, k, :]) for k in range(8)], None, wada, reads=[K], writes=[wada], max_dma_last_dim=4096)
            wadas.append(wada)
        for l in range(4):
            wada = wadas[l]
            mk.dma(mk.sp, bg[0:2, :], self.b_gate[l:l + 1, :].to_broadcast([2, 1024]), bg, reads=[K], writes=[bg])
            b0 = self.banks[0]
            pscol = b0[:, 0:32].rearrange("p (f c) -> p f c", c=2)
            for f in range(16):
                self.mm_group(b0, pscol[:, f, :], [(wada[:, k, f * 128:(f + 1) * 128], sc2[:, k, :]) for k in range(8)], [wada, sc2])
            mk.op(mk.dve, lambda e: e.tensor_tensor(out=self.modcol[:, l], in0=pscol,
                                                    in1=bcol[:, l, :].unsqueeze(2).to_broadcast([128, 16, 2]), op=ALU.add),
                  reads=[b0, bcol], writes=[self.modcol])
            mk.op(mk.dve, lambda e: e.tensor_scalar(out=self.modcol[:, l, 8:16, :], in0=self.modcol[:, l, 8:16, :],
                                                    scalar1=1.0, scalar2=None, op0=ALU.add),
                  reads=[self.modcol], writes=[self.modcol])
            for cg in range(2):
                bk = self.banks[1 + cg]
                self.mm_group(bk, bk[0:2, :], [(sc2[:, k, :], wada[:, k, 2048 + cg * 512:2048 + (cg + 1) * 512]) for k in range(8)], [wada, sc2])
                mk.op(mk.dve, lambda e, bk=bk, cg=cg: e.tensor_tensor(out=grow[0:2, cg * 512:(cg + 1) * 512], in0=bk[0:2, :],
                                                                      in1=bg[0:2, cg * 512:(cg + 1) * 512], op=ALU.add),
                      reads=[bk, bg], writes=[grow])
            mk.dma(mk.sp, self.gates_d[2 * l:2 * l + 2, :], grow[0:2, :], grow, reads=[grow], writes=[self.GT])
            if l + 2 < 4:
                mk.dma(mk.pool, [(wada[:, k, :], self.w_ada[l + 2, :, k, :]) for k in range(8)], None, wada, reads=[K], writes=[wada], max_dma_last_dim=4096)

    def dump(self):
        mk, A = self.mk, self.arena
        mk.barrier()
        A.reset()
        bufs = Rot([A.alloc(f"dmp{i}", [128, D], F32, dma=True) for i in range(4)])
        for t in range(NT):
            b = bufs.next()
            mk.dma(mk.sp, b[:], self.h_d[t * 128:(t + 1) * 128, :], b, reads=[self.HT[t]], writes=[b])
            mk.dma(mk.sp, self.out_d[t * 128:(t + 1) * 128, :], b[:], b, reads=[b], writes=[self.OUT[t]])

    def norm_tile(self, li, t, hin, hn, st, dst_T, dst_fn):
        mk = self.mk
        cond = 1 if t < 2 else 0
        mk.op(mk.act, lambda e: e.activation(out=hn[:], in_=hin[:], func=AF.Square, accum_out=st[:, 0:1]),
              reads=[hin], writes=[hn, st])
        mk.op(mk.act, lambda e: e.activation(out=st[:, 1:2], in_=st[:, 0:1], func=AF.Sqrt, scale=1.0 / D, bias=self.epsc[:, 0:1]),
              reads=[st, self.epsc], writes=[st])
        mk.op(mk.dve, lambda e: e.reciprocal(st[:, 2:3], st[:, 1:2]), reads=[st], writes=[st])
        mk.op(mk.dve, lambda e: e.tensor_scalar(out=hn[:], in0=hin[:], scalar1=st[:, 2:3], scalar2=None, op0=ALU.mult),
              reads=[hin, st], writes=[hn])
        if DBG.get("nstop", 9) < 2:
            return
        bk = self.trb.next()
        bv = bview(bk[:]).rearrange("p (a b) -> p a b", a=8)
        for k in range(8):
            mk.op(mk.pe, lambda e, k=k: e.transpose(bv[:, k, :], hn[:, k * 128:(k + 1) * 128], self.ident[:]),
                  reads=[hn, self.ident], writes=[bk], inc=(k == 7))
        if DBG.get("nstop", 9) < 3:
            return
        for k in range(8):
            mk.op(mk.dve, lambda e, k=k: e.tensor_scalar(out=dst_fn(k), in0=bv[:, k, :],
                                                         scalar1=self.modcol[:, li, 8 + k, cond:cond + 1],
                                                         scalar2=self.modcol[:, li, k, cond:cond + 1],
                                                         op0=ALU.mult, op1=ALU.add),
                  reads=[bk, self.modcol], writes=[dst_T])

    def attn_layer(self, li):
        mk, A = self.mk, self.arena
        j = li // 2
        A.reset()
        win = A.alloc("a_win", [128, 8, 2560], BF16, dma=True)
        wout = A.alloc("a_wout", [128, 8, 1024], BF16, dma=True)
        gate = [A.alloc(f"a_gate{i}", [128, D], F32, dma=True) for i in range(2)]
        esink = A.alloc("a_esink", [128, 16], F32, dma=True)
        kT = A.alloc("a_kT", [128, 2, TT], BF16)
        vaug = A.alloc("a_vaug", [128, NT, 4, 65], BF16)
        hinr = Rot([A.alloc(f"a_hin{i}", [128, D], F32, dma=True) for i in range(3)])
        hresr = Rot([A.alloc(f"a_hres{i}", [128, D], F32, dma=True) for i in range(2)])
        hnr = Rot([A.alloc(f"a_hn{i}", [128, D], BF16) for i in range(2)])
        str_ = Rot([A.alloc(f"a_st{i}", [128, 4], F32) for i in range(4)])
        uT = A.alloc("a_uT", [128, 8, 512], BF16)
        qTr = Rot([A.alloc(f"a_qT{i}", [128, 8, 512], BF16) for i in range(2)])
        gsr = Rot([A.alloc(f"a_gs{i}", [128, 4, D], BF16) for i in range(2)])
        cosg = A.alloc("a_cos", [128, 512], F32, dma=True)
        sing = A.alloc("a_sin", [128, 512], F32, dma=True)
        qbr = Rot([A.alloc(f"a_qb{i}", [128, 512], BF16) for i in range(2)])
        t1r = Rot([A.alloc(f"a_t1{i}", [128, 512], F32) for i in range(2)])
        t2r = Rot([A.alloc(f"a_t2{i}", [128, 512], F32) for i in range(1)])
        pTr = Rot([A.alloc(f"a_pT{i}", [128, 5, 512], BF16) for i in range(2)])
        denr = Rot([A.alloc(f"a_den{i}", [128, 8], F32) for i in range(2)])
        otr = Rot([A.alloc(f"a_ot{i}", [128, 256], F32) for i in range(2)])
        og = A.alloc("a_og", [128, D], BF16)
        ogT = A.alloc("a_ogT", [128, 8, 128], BF16)
        ytr = Rot([A.alloc(f"a_yt{i}", [128, 512], F32) for i in range(2)])
        self.epsc = A.alloc("a_eps", [128, 1], F32)
        self.trb = Rot(self.banks[0:2])
        mmb = Rot(self.banks[2:4])
        sb = Rot(self.banks[4:7])
        ob = self.banks[7]
        W = T("wsrc")
        mk.op(mk.dve, lambda e: e.memset(self.epsc[:], EPS), writes=[self.epsc])
        mk.op(mk.dve, lambda e: e.memset(vaug[:, :, :, 64:65], 1.0), writes=[vaug])
        mk.dma(mk.pool, [(win[:, k, :], self.a_win[j, :, k, :]) for k in range(8)], None, win, reads=[W], writes=[win], max_dma_last_dim=4096)
        mk.dma(mk.pool, [(wout[:, k, :], self.a_wout[j, :, k, :]) for k in range(8)], None, wout, reads=[W], writes=[wout], max_dma_last_dim=4096)
        mk.dma(mk.sp, gate[0][:], self.gates_d[2 * li:2 * li + 1, :].to_broadcast([128, D]), gate[0], reads=[self.GT], writes=[gate[0]])
        mk.dma(mk.sp, gate[1][:], self.gates_d[2 * li + 1:2 * li + 2, :].to_broadcast([128, D]), gate[1], reads=[self.GT], writes=[gate[1]])
        mk.dma(mk.sp, esink[:], self.a_sink[j:j + 1, :].to_broadcast([128, 16]), esink, reads=[W], writes=[esink])
        mk.op(mk.act, lambda e: e.activation(out=esink[:], in_=esink[:], func=AF.Exp), reads=[esink], writes=[esink])
        scale = 1.0 / 8.0

        hin_of = {}
        nload = [0]

        def ensure_loaded(upto):
            while nload[0] <= min(upto, NT - 1):
                t = nload[0]
                b = hinr.next()
                src, srcT = self.h_src(li, t)
                mk.dma(mk.sp, b[:], src, b, reads=[srcT], writes=[b])
                hin_of[t] = b
                nload[0] += 1

        def rope_chunk(c, ntok, dst_T, dst_ap):
            pb = mmb.next()
            self.mm_group(pb, pb[:, 0:ntok], [(win[:, k, c * 128:(c + 1) * 128], uT[:, k, 0:ntok]) for k in range(8)], [win, uT])
            qb = qbr.next()
            mk.op(mk.act, lambda e: e.activation(out=qb[:, 0:ntok], in_=pb[:, 0:ntok], func=AF.Identity), reads=[pb], writes=[qb])
            rb = mmb.next()
            self.mm_group(rb, rb[:, 0:ntok], [(self.prot[:], qb[:, 0:ntok])], [self.prot, qb])
            t1 = t1r.next()
            t2 = t2r.next()
            mk.op(mk.dve, lambda e: e.tensor_tensor(out=t1[:, 0:ntok], in0=pb[:, 0:ntok], in1=cosg[:, 0:ntok], op=ALU.mult),
                  reads=[pb, cosg], writes=[t1])
            mk.op(mk.dve, lambda e: e.tensor_tensor(out=t2[:, 0:ntok], in0=rb[:, 0:ntok], in1=sing[:, 0:ntok], op=ALU.mult),
                  reads=[rb, sing], writes=[t2])
            mk.op(mk.dve, lambda e: e.tensor_tensor(out=dst_ap, in0=t1[:, 0:ntok], in1=t2[:, 0:ntok], op=ALU.add),
                  reads=[t1, t2], writes=[dst_T])

        def stage_p(n):
            tok0, ntok = GROUPS[n]
            nt = ntok // 128
            qT = qTr.next()
            gs = gsr.next()

            def part0():
                mk.dma(mk.sp, cosg[:, 0:ntok], self.k_cos[:, tok0:tok0 + ntok], cosg, reads=[W], writes=[cosg])
                mk.dma(mk.sp, sing[:, 0:ntok], self.k_sin[:, tok0:tok0 + ntok], sing, reads=[W], writes=[sing])
                for ti in range(nt):
                    t = tok0 // 128 + ti
                    ensure_loaded(t + 2)
                    self.norm_tile(li, t, hin_of.pop(t), hnr.next(), str_.next(), uT,
                                   lambda k, ti=ti: uT[:, k, ti * 128:(ti + 1) * 128])

            def part1():
                for c in (8, 9):
                    rope_chunk(c, ntok, kT, kT[:, c - 8, tok0:tok0 + ntok])
                for c in range(0, 3):
                    rope_chunk(c, ntok, qT, qT[:, c, 0:ntok])

            def part2():
                for c in range(3, 8):
                    rope_chunk(c, ntok, qT, qT[:, c, 0:ntok])

            def part3():
                for ti in range(nt):
                    t = tok0 // 128 + ti
                    vb = mmb.next()
                    self.mm_group(vb, vb[:, 0:256], [(uT[:, k, ti * 128:(ti + 1) * 128], win[:, k, 1280:1536]) for k in range(8)], [win, uT])
                    mk.op(mk.act, lambda e, vb=vb, t=t: e.activation(out=vaug[:, t, :, 0:64],
                                                                     in_=vb[:, 0:256].rearrange("p (a b) -> p a b", a=4), func=AF.Identity),
                          reads=[vb], writes=[vaug])
                    for cg in range(2):
                        gb = mmb.next()
                        self.mm_group(gb, gb[:, :], [(uT[:, k, ti * 128:(ti + 1) * 128], win[:, k, 1536 + cg * 512:1536 + (cg + 1) * 512])
                                                     for k in range(8)], [win, uT])
                        mk.op(mk.act, lambda e, gb=gb, ti=ti, cg=cg: e.activation(out=gs[:, ti, cg * 512:(cg + 1) * 512], in_=gb[:, :], func=AF.Silu),
                              reads=[gb], writes=[gs])

            return qT, gs, [part0, part1, part2, part3]

        def stage_a(n, qT, gs, parts=()):
            parts = list(parts)
            tok0, ntok = GROUPS[n]
            nta = ntok // 128
            for ti in range(nta):
                t = tok0 // 128 + ti
                while parts and len(parts) > (nta - 1 - ti) * (4 // max(nta, 1)) - 0 and (len(parts) > 4 - (ti + 1) * (4 // nta)):
                    parts.pop(0)()
                if t < 2:
                    kbs = [(0, None), (1, None)]
                else:
                    kbs = []
                    if t - 1 >= 2:
                        kbs.append((t - 1, self.neglo))
                    kbs.append((t, None))
                    if t + 1 < NT:
                        kbs.append((t + 1, self.neghi))
                    kbs += [(0, None), (1, None)]
                hres = hresr.next()
                src, srcT = self.h_src(li, t)
                mk.dma(mk.sp, hres[:], src, hres, reads=[srcT], writes=[hres])
                def qk_stage(kh):
                    pbase = 64 * (kh % 2)
                    kc = kh // 2
                    pT = pTr.next()
                    for b, (kt, msk) in enumerate(kbs):
                        sbk = sb.next()
                        prs = [(kT[pbase:pbase + 64, kc, kt * 128:(kt + 1) * 128],
                                qT[pbase:pbase + 64, 4 * kc:4 * kc + 4, ti * 128:(ti + 1) * 128])]
                        rds = [kT, qT]
                        if msk is not None:
                            prs.append((self.ident[:], msk[:]))
                            rds += [self.ident, msk]
                        self.mm_group(sbk, sbk[:, :], prs, rds)
                        mk.op(mk.act, lambda e, sbk=sbk, b=b: e.activation(out=pT[:, b, :], in_=sbk[:, :], func=AF.Exp, scale=scale),
                              reads=[sbk], writes=[pT])
                    return pT

                pT_next = qk_stage(0)
                for kh in range(4):
                    pT = pT_next
                    if kh + 1 < 4:
                        pT_next = qk_stage(kh + 1)
                    ov = ob[:, :].rearrange("p (a b) -> p a b", a=4)
                    for r in range(4):
                        self.mm_group(ob, ov[:, r, 0:65],
                                      [(pT[:, b, r * 128:(r + 1) * 128], vaug[:, kt, kh, :]) for b, (kt, msk) in enumerate(kbs)],
                                      [pT, vaug])
                    den = denr.next()
                    ot = otr.next()
                    mk.op(mk.dve, lambda e, kh=kh: e.tensor_tensor(out=den[:, 0:4], in0=ov[:, :, 64], in1=esink[:, 4 * kh:4 * kh + 4], op=ALU.add),
                          reads=[ob, esink], writes=[den])
                    mk.op(mk.dve, lambda e: e.reciprocal(den[:, 4:8], den[:, 0:4]), reads=[den], writes=[den])
                    otv = ot[:, :].rearrange("p (a b) -> p a b", a=4)
                    mk.op(mk.dve, lambda e: e.tensor_tensor(out=otv, in0=ov[:, :, 0:64],
                                                            in1=den[:, 4:8].unsqueeze(2).to_broadcast([128, 4, 64]), op=ALU.mult),
                          reads=[ob, den], writes=[ot])
                    mk.op(mk.dve, lambda e, kh=kh: e.tensor_tensor(out=og[:, kh * 256:(kh + 1) * 256], in0=ot[:, :],
                                                                    in1=gs[:, ti, kh * 256:(kh + 1) * 256], op=ALU.mult),
                          reads=[ot, gs], writes=[og])
                bk = self.trb.next()
                bv = bview(bk[:]).rearrange("p (a b) -> p a b", a=8)
                for k in range(8):
                    mk.op(mk.pe, lambda e, k=k: e.transpose(bv[:, k, :], og[:, k * 128:(k + 1) * 128], self.ident[:]),
                          reads=[og, self.ident], writes=[bk], inc=(k == 7))
                mk.op(mk.dve, lambda e: e.tensor_copy(out=ogT[:], in_=bv), reads=[bk], writes=[ogT])
                gt = gate[1] if t < 2 else gate[0]
                for cg in range(2):
                    yb = mmb.next()
                    self.mm_group(yb, yb[:, :], [(ogT[:, k, :], wout[:, k, cg * 512:(cg + 1) * 512]) for k in range(8)], [ogT, wout])
                    yt = ytr.next()
                    mk.op(mk.dve, lambda e, yb=yb, cg=cg: e.tensor_tensor(out=yt[:], in0=yb[:, :], in1=gt[:, cg * 512:(cg + 1) * 512], op=ALU.mult),
                          reads=[yb, gt], writes=[yt])
                    mk.op(mk.dve, lambda e, cg=cg: e.tensor_tensor(out=hres[:, cg * 512:(cg + 1) * 512], in0=hres[:, cg * 512:(cg + 1) * 512],
                                                                    in1=yt[:], op=ALU.add),
                          reads=[yt, hres], writes=[hres])
                mk.dma(mk.sp, self.h_d[t * 128:(t + 1) * 128, :], hres[:], hres, reads=[hres], writes=[self.HT[t]])
            while parts:
                parts.pop(0)()

        ng = len(GROUPS)
        cur = stage_p(0)
        for f in cur[2]:
            f()
        for n in range(ng):
            nxt = stage_p(n + 1) if n + 1 < ng else None
            stage_a(n, cur[0], cur[1], nxt[2] if nxt is not None else ())
            cur = nxt

    def ssd_layer(self, li):
        mk, A = self.mk, self.arena
        j = li // 2
        last = (li == 3)
        need_ctx = not last
        A.reset()
        W = T("wsrc_s")
        gate = [A.alloc(f"s_gate{i}", [128, D], F32, dma=True) for i in range(2)]
        convw = A.alloc("s_convw", [128, 32, 5], F32, dma=True)
        convb = A.alloc("s_convb", [128, 32], F32, dma=True)
        nwc = A.alloc("s_nwc", [128, 16], F32, dma=True)
        dtb = A.alloc("s_dtb", [128, 64], F32, dma=True)
        arow = A.alloc("s_arow", [128, 64], F32, dma=True)
        drow = A.alloc("s_drow", [128, 32], F32, dma=True)
        fnw = A.alloc("s_fnw", [128, D], F32, dma=True)
        dt_all = A.alloc("s_dt", [128, NT, 64], F32)
        self.epsc = A.alloc("s_eps", [128, 1], F32)
        mk.op(mk.dve, lambda e: e.memset(self.epsc[:], EPS), writes=[self.epsc])
        mk.dma(mk.sp, gate[0][:], self.gates_d[2 * li:2 * li + 1, :].to_broadcast([128, D]), gate[0], reads=[self.GT], writes=[gate[0]])
        mk.dma(mk.sp, gate[1][:], self.gates_d[2 * li + 1:2 * li + 2, :].to_broadcast([128, D]), gate[1], reads=[self.GT], writes=[gate[1]])
        mk.dma(mk.sp, convw[:], self.s_convw[:, j], convw, reads=[W], writes=[convw])
        mk.dma(mk.sp, convb[:], self.s_convb[:, j], convb, reads=[W], writes=[convb])
        mk.dma(mk.sp, nwc[:], self.s_nw[:, j], nwc, reads=[W], writes=[nwc])
        mk.dma(mk.sp, dtb[:], self.s_dtb[j:j + 1, :].to_broadcast([128, 64]), dtb, reads=[W], writes=[dtb])
        mk.dma(mk.sp, arow[:], self.s_alog[j:j + 1, :].to_broadcast([128, 64]), arow, reads=[W], writes=[arow])
        mk.dma(mk.sp, drow[:], self.s_d[j:j + 1, :].to_broadcast([128, 32]), drow, reads=[W], writes=[drow])
        mk.dma(mk.sp, fnw[:], self.fnw[0:1, :].to_broadcast([128, D]), fnw, reads=[W], writes=[fnw])
        mk.op(mk.act, lambda e: e.activation(out=arow[:], in_=arow[:], func=AF.Exp), reads=[arow], writes=[arow])
        mk.op(mk.dve, lambda e: e.tensor_scalar(out=arow[:], in0=arow[:], scalar1=-1.0, scalar2=None, op0=ALU.mult), reads=[arow], writes=[arow])
        mark0 = A.off

        uT = A.alloc("s_uT", [128, 8, UCOLS], BF16)
        mark1 = A.off
        hinr = Rot([A.alloc(f"s_hin{i}", [128, D], F32, dma=True) for i in range(3)])
        hnr = Rot([A.alloc(f"s_hn{i}", [128, D], BF16) for i in range(2)])
        str_ = Rot([A.alloc(f"s_st{i}", [128, 4], F32) for i in range(4)])
        wz = A.alloc("s_wz", [128, 8, 2112], BF16, dma=True)
        zstr = Rot([A.alloc(f"s_zst{i}", [128, 2048], BF16, dma=True) for i in range(2)])
        dtmr = Rot([A.alloc(f"s_dtm{i}", [128, 128], F32) for i in range(2)])
        mk.dma(mk.pool, [(wz[:, k, :], self.s_wzdt[j, :, k, :]) for k in range(8)], None, wz, reads=[W], writes=[wz], max_dma_last_dim=4096)
        self.trb = Rot(self.banks[0:2])
        mmb = Rot(self.banks[2:5])
        for (a, b) in ((0, 2), (258, 262), (UCOLS - 2, UCOLS)):
            mk.op(mk.dve, lambda e: e.memset(uT[:, :, a:b], 0.0), writes=[uT])
        hin_of = {}
        nload = [0]

        def ensure_loaded(upto):
            while nload[0] <= min(upto, NT - 1):
                t = nload[0]
                b = hinr.next()
                src, srcT = self.h_src(li, t)
                mk.dma(mk.sp, b[:], src, b, reads=[srcT], writes=[b])
                hin_of[t] = b
                nload[0] += 1

        for t in range(NT):
            ensure_loaded(t + 2)
            c0 = ucol(t * 128)
            uTt = T(f"uT_t{t}", uT.ap)
            self.norm_tile(li, t, hin_of.pop(t), hnr.next(), str_.next(), uTt, lambda k: uT[:, k, c0:c0 + 128])
            zst = zstr.next()
            for cg in range(4):
                bk = mmb.next()
                self.mm_group(bk, bk[:, :], [(uT[:, k, c0:c0 + 128], wz[:, k, cg * 512:(cg + 1) * 512]) for k in range(8)], [uTt, wz])
                mk.op(mk.act, lambda e: e.activation(out=zst[:, cg * 512:(cg + 1) * 512], in_=bk[:, :], func=AF.Silu), reads=[bk], writes=[zst])
            mk.dma(mk.sp, self.z_d[t * 128:(t + 1) * 128, :], zst[:], zst, reads=[zst], writes=[self.ZT[t]])
            bk = mmb.next()
            self.mm_group(bk, bk[:, 0:64], [(uT[:, k, c0:c0 + 128], wz[:, k, 2048:2112]) for k in range(8)], [uTt, wz])
            dtm = dtmr.next()
            mk.op(mk.dve, lambda e: e.tensor_tensor(out=dtm[:, 0:64], in0=bk[:, 0:64], in1=dtb[:], op=ALU.add), reads=[bk, dtb], writes=[dtm])
            mk.op(mk.act, lambda e: e.activation(out=dtm[:, 64:128], in_=dtm[:, 0:64], func=AF.Exp), reads=[dtm], writes=[dtm])
            mk.op(mk.act, lambda e: e.activation(out=dt_all[:, t, :], in_=dtm[:, 64:128], func=AF.Ln, bias=1.0), reads=[dtm], writes=[dt_all])

        mk.barrier()
        A.reset(mark1)
        wccr = Rot([A.alloc(f"s_wcc{i}", [128, 1024], BF16, dma=True) for i in range(3)])
        prer = Rot([A.alloc(f"s_pre{i}", [128, UCOLS], F32) for i in range(2)])
        accr = Rot([A.alloc(f"s_acc{i}", [128, UCOLS - 4], F32) for i in range(2)])
        xor_ = Rot([A.alloc(f"s_xo{i}", [128, UCOLS - 4], BF16, dma=True) for i in range(2)])
        tstr = Rot([A.alloc(f"s_tst{i}", [128, 8, 128], BF16, dma=True) for i in range(3)])
        NX = UCOLS - 4
        colgroups = [(c, min(512, UCOLS - c)) for c in range(0, UCOLS, 512)]
        tilegroups = [list(range(a, min(a + 8, NT))) for a in range(0, NT, 8)]

        def xo_idx(tok):
            return tok if tok < TC else tok + 4

        def s0b_proj(cc):
            wcc = wccr.next()
            mk.dma(mk.pool, wcc[:], self.s_wxbc[j, cc], wcc, reads=[W], writes=[wcc], max_dma_last_dim=4096)
            pre = prer.next()
            for (c0, n) in colgroups:
                bk = mmb.next()
                self.mm_group(bk, bk[:, 0:n], [(wcc[:, k * 128:(k + 1) * 128], uT[:, k, c0:c0 + n]) for k in range(8)], [wcc, uT])
                mk.op(mk.act, lambda e: e.activation(out=pre[:, c0:c0 + n], in_=bk[:, 0:n], func=AF.Identity), reads=[bk], writes=[pre])
            return pre

        def s0b_conv(cc, pre):
            acc = accr.next()
            mk.op(mk.dve, lambda e: e.tensor_scalar(out=acc[:], in0=pre[:, 0:NX], scalar1=convw[:, cc, 0:1], scalar2=None, op0=ALU.mult),
                  reads=[pre, convw], writes=[acc])
            for k in range(1, 5):
                mk.op(mk.dve, lambda e: e.scalar_tensor_tensor(out=acc[:], in0=pre[:, k:k + NX], scalar=convw[:, cc, k:k + 1], in1=acc[:],
                                                               op0=ALU.mult, op1=ALU.add),
                      reads=[pre, convw, acc], writes=[acc])
            xo = xor_.next()
            mk.op(mk.act, lambda e: e.activation(out=xo[:], in_=acc[:], func=AF.Silu, bias=convb[:, cc:cc + 1]),
                  reads=[acc, convb], writes=[xo])
            return xo

        def s0b_out(cc, xo):
            if cc < 24:
                dst_d, dstT, coff = (self.xs_d, self.XS, cc * 128) if cc < 16 else (self.bt_d, self.BTK, (cc - 16) * 128)
                for tg in tilegroups:
                    nu = len(tg)
                    bk = self.trb.next()
                    bv = bview(bk[:]).rearrange("p (a b) -> p a b", a=8)
                    for u, t in enumerate(tg):
                        i0 = xo_idx(t * 128)
                        mk.op(mk.pe, lambda e: e.transpose(bv[:, u, :], xo[:, i0:i0 + 128], self.ident[:]),
                              reads=[xo, self.ident], writes=[bk], inc=(u == nu - 1))
                    ts_ = tstr.next()
                    mk.op(mk.act, lambda e: e.activation(out=ts_[:, 0:nu, :], in_=bv[:, 0:nu, :], func=AF.Identity), reads=[bk], writes=[ts_])
                    mk.dma(mk.sp, dst_d[tg[0] * 128:(tg[0] + nu) * 128, coff:coff + 128].rearrange("(u p) c -> p u c", p=128),
                           ts_[:, 0:nu, :], ts_, reads=[ts_], writes=[dstT[t] for t in tg])
            if cc >= 16:
                g = (cc - 16) % 8
                dd, ddT = (self.BT_d, self.BTT) if cc < 24 else (self.CT_d, self.CTT)
                mk.dma(mk.sp, [(dd[g, :, 0:TC], xo[:, 0:TC]), (dd[g, :, TC:TT], xo[:, TC + 4:TT + 4])], None, xo,
                       reads=[xo], writes=[ddT[g]])

        pre_cur = s0b_proj(0)
        for cc in range(32):
            pre_nxt = s0b_proj(cc + 1) if cc + 1 < 32 else None
            xo = s0b_conv(cc, pre_cur)
            s0b_out(cc, xo)
            pre_cur = pre_nxt

        mk.barrier()
        A.reset(mark0)
        xsr = Rot([A.alloc(f"s_xs{i}", [128, 2048], BF16, dma=True) for i in range(3)])
        btr = Rot([A.alloc(f"s_bt{i}", [128, 1024], BF16, dma=True) for i in range(3)])
        smr = Rot([A.alloc(f"s_sm{i}", [128, 256], F32) for i in range(3)])
        xwr = Rot([A.alloc(f"s_xw{i}", [128, 2048], BF16) for i in range(3)])
        hT = A.alloc("s_hT", [128, 2048], F32)
        hbr = Rot([A.alloc(f"s_hb{i}", [128, 2048], BF16, dma=True) for i in range(2)])
        stb = self.banks[4:8]
        mmb = Rot(self.banks[2:4])
        mark2 = A.off
        for d in range(2):
            order = list(range(NT)) if d == 0 else [1, 0] + list(range(NT - 1, 1, -1))
            mk.op(mk.dve, lambda e: e.memset(hT[:], 0.0), writes=[hT])

            def p1_front(t):
                xs_t = xsr.next()
                bt_t = btr.next()
                mk.dma(mk.sp, xs_t[:], self.xs_d[t * 128:(t + 1) * 128, :], xs_t, reads=[self.XS[t]], writes=[xs_t])
                mk.dma(mk.sp, bt_t[:], self.bt_d[t * 128:(t + 1) * 128, :], bt_t, reads=[self.BTK[t]], writes=[bt_t])
                sm = smr.next()
                dts = dt_all[:, t, d * 32:(d + 1) * 32]
                mk.op(mk.dve, lambda e: e.tensor_tensor(out=sm[:, 0:32], in0=dts, in1=arow[:, d * 32:(d + 1) * 32], op=ALU.mult),
                      reads=[dt_all, arow], writes=[sm])
                bk = mmb.next()
                self.mm_group(bk, bk[:, 0:32], [(self.tri[:, d, :], sm[:, 0:32])], [self.tri, sm])
                self.mm_group(bk, bk[:, 32:64], [(self.onesf[:], sm[:, 0:32])], [self.onesf, sm])
                mk.op(mk.act, lambda e: e.activation(out=sm[:, 32:96], in_=bk[:, 0:64], func=AF.Identity), reads=[bk], writes=[sm])
                mk.op(mk.dve, lambda e: e.tensor_tensor(out=sm[:, 96:128], in0=sm[:, 64:96], in1=sm[:, 32:64], op=ALU.subtract), reads=[sm], writes=[sm])
                mk.op(mk.act, lambda e: e.activation(out=sm[:, 96:128], in_=sm[:, 96:128], func=AF.Exp), reads=[sm], writes=[sm])
                mk.op(mk.act, lambda e: e.activation(out=sm[:, 160:192], in_=sm[:, 64:96], func=AF.Exp), reads=[sm], writes=[sm])
                mk.op(mk.dve, lambda e: e.tensor_tensor(out=sm[:, 128:160], in0=sm[:, 96:128], in1=dts, op=ALU.mult), reads=[sm, dt_all], writes=[sm])
                xw = xwr.next()
                mk.op(mk.dve, lambda e: e.tensor_tensor(out=xw[:].rearrange("p (h c) -> p h c", h=32),
                                                        in0=xs_t[:].rearrange("p (h c) -> p h c", h=32),
                                                        in1=sm[:, 128:160].unsqueeze(2).to_broadcast([128, 32, 64]), op=ALU.mult),
                      reads=[xs_t, sm], writes=[xw])
                return sm, bt_t, xw

            def p1_update(t, fr):
                sm, bt_t, xw = fr
                hb = hbr.next()
                mk.op(mk.act, lambda e: e.activation(out=hb[:], in_=hT[:], func=AF.Identity), reads=[hT], writes=[hb])
                mk.dma(mk.sp, self.hp_d[d, t], hb[:], hb, reads=[hb], writes=[self.HP[d][t]])
                for g in range(8):
                    sbk = stb[g // 2]
                    self.mm_group(sbk, sbk[:, (g % 2) * 256:(g % 2 + 1) * 256], [(bt_t[:, g * 128:(g + 1) * 128], xw[:, g * 256:(g + 1) * 256])], [bt_t, xw])
                hv = hT[:].rearrange("p (h c) -> p h c", h=32)
                mk.op(mk.dve, lambda e: e.tensor_tensor(out=hv, in0=hv, in1=sm[:, 160:192].unsqueeze(2).to_broadcast([128, 32, 64]), op=ALU.mult),
                      reads=[hT, sm], writes=[hT])
                for i in range(4):
                    mk.op(mk.dve, lambda e: e.tensor_tensor(out=hT[:, i * 512:(i + 1) * 512], in0=hT[:, i * 512:(i + 1) * 512], in1=stb[i][:, :], op=ALU.add),
                          reads=[hT, stb[i]], writes=[hT])

            fr = p1_front(order[0])
            for i, t in enumerate(order):
                fr_n = p1_front(order[i + 1]) if i + 1 < len(order) else None
                p1_update(t, fr)
                fr = fr_n

        mk.barrier()
        A.reset(mark0)
        wout = A.alloc("s_wout", [128, 16, D], BF16, dma=True)
        mk.dma(mk.pool, [(wout[:, k, :], self.s_wout[j, :, k, :]) for k in range(16)], None, wout, reads=[W], writes=[wout])
        xsr = Rot([A.alloc(f"p_xs{i}", [128, 2048], BF16, dma=True) for i in range(2)])
        BTr = Rot([A.alloc(f"p_BT{i}", [128, 8, 128], BF16, dma=True) for i in range(2)])
        CTr = Rot([A.alloc(f"p_CT{i}", [128, 8, 128], BF16, dma=True) for i in range(2)])
        zr = Rot([A.alloc(f"p_z{i}", [128, 2048], BF16, dma=True) for i in range(2)])
        hpfr = Rot([A.alloc(f"p_hpf{i}", [128, 2048], BF16, dma=True) for i in range(2)])
        hpbr = Rot([A.alloc(f"p_hpb{i}", [128, 2048], BF16, dma=True) for i in range(2)])
        hresr = Rot([A.alloc(f"p_hres{i}", [128, D], F32, dma=True) for i in range(3)])
        smr = Rot([A.alloc(f"p_sm{i}", [128, 384], F32) for i in range(2)])
        sel = A.alloc("p_sel", [128, 32, 128], BF16, dma=True)
        mk.dma(mk.pool, sel[0:64].rearrange("p a b -> p (a b)"), self.k_sel, sel, reads=[W], writes=[sel], max_dma_last_dim=4096)
        aThr = Rot([A.alloc(f"p_aTh{i}", [128, 128], BF16) for i in range(2)])
        aTlr = Rot([A.alloc(f"p_aTl{i}", [128, 128], BF16) for i in range(2)])
        Er = Rot([A.alloc(f"p_E{i}", [128, 4, 128], BF16) for i in range(2)])
        xpr = Rot([A.alloc(f"p_xp{i}", [128, 4, 128], F32) for i in range(2)])
        Mall = A.alloc("p_M", [128, 16, 4, 128], BF16)
        cbs = A.alloc("p_cbs", [128, 8, 128], BF16)
        dident = A.alloc("p_dident", [128, 32, 128], BF16)
        for h in range(32):
            mk.op(mk.dve, lambda e: e.tensor_scalar(out=dident[:, h, :], in0=self.ident[:], scalar1=drow[:, h:h + 1], scalar2=None, op0=ALU.mult),
                  reads=[self.ident, drow], writes=[dident])
        xdtr = Rot([[A.alloc(f"p_xdt{d}_{i}", [128, 2048], BF16) for d in range(2)] for i in range(2)])
        t1r = Rot([A.alloc(f"p_t1{i}", [128, 256], F32) for i in range(2)])
        t2r = Rot([A.alloc(f"p_t2{i}", [128, 256], F32) for i in range(2)])
        yall = A.alloc("p_yall", [128, 2048], F32)
        t3 = A.alloc("p_t3", [128, 2048], F32, dma=True)
        un = A.alloc("p_un", [128, 2048], BF16)
        ynT = A.alloc("p_ynT", [128, 16, 128], BF16)
        ytr = Rot([A.alloc(f"p_yt{i}", [128, 512], F32) for i in range(2)])
        mmb = Rot(self.banks[0:2])
        ebr = Rot(self.banks[2:4])
        ABr = Rot([(self.banks[4], self.banks[5]), (self.banks[6], self.banks[7])])
        print("pass2 arena bytes", A.off * 4, "of", A.cap * 4)
        tiles = list(range(NT)) if need_ctx else list(range(2, NT))

        def p2_load(t):
            xs_t, BTt, CTt, z_t, hpf, hpb, hres = xsr.next(), BTr.next(), CTr.next(), zr.next(), hpfr.next(), hpbr.next(), hresr.next()
            mk.dma(mk.sp, BTt[:], self.BT_d[:, :, t * 128:(t + 1) * 128].rearrange("g n t -> n g t"), BTt, reads=self.BTT, writes=[BTt])
            mk.dma(mk.sp, CTt[:], self.CT_d[:, :, t * 128:(t + 1) * 128].rearrange("g n t -> n g t"), CTt, reads=self.CTT, writes=[CTt])
            mk.dma(mk.sp, xs_t[:], self.xs_d[t * 128:(t + 1) * 128, :], xs_t, reads=[self.XS[t]], writes=[xs_t])
            mk.dma(mk.sp, hpf[:], self.hp_d[0, t], hpf, reads=[self.HP[0][t]], writes=[hpf])
            mk.dma(mk.sp, hpb[:], self.hp_d[1, t], hpb, reads=[self.HP[1][t]], writes=[hpb])
            mk.dma(mk.sp, z_t[:], self.z_d[t * 128:(t + 1) * 128, :], z_t, reads=[self.ZT[t]], writes=[z_t])
            src, srcT = self.h_src(li, t)
            mk.dma(mk.sp, hres[:], src, hres, reads=[srcT], writes=[hres])
            return xs_t, BTt, CTt, z_t, hpf, hpb, hres

        def p2_front(t, bufs):
            xs_t, BTt, CTt, z_t, hpf, hpb, hres = bufs
            sm = smr.next()
            mk.op(mk.dve, lambda e: e.tensor_tensor(out=sm[:, 0:64], in0=dt_all[:, t, :], in1=arow[:], op=ALU.mult), reads=[dt_all, arow], writes=[sm])
            bk = mmb.next()
            self.mm_group(bk, bk[:, 0:32], [(self.tri[:, 0, :], sm[:, 0:32])], [self.tri, sm])
            self.mm_group(bk, bk[:, 32:64], [(self.tri[:, 1, :], sm[:, 32:64])], [self.tri, sm])
            mk.op(mk.act, lambda e: e.activation(out=sm[:, 64:128], in_=bk[:, 0:64], func=AF.Exp), reads=[bk], writes=[sm])
            mk.op(mk.act, lambda e: e.activation(out=sm[:, 192:256], in_=bk[:, 0:64], func=AF.Identity, scale=-1.0), reads=[bk], writes=[sm])
            mk.op(mk.act, lambda e: e.activation(out=sm[:, 256:320], in_=bk[:, 0:64], func=AF.Identity), reads=[bk], writes=[sm])
            bkT = mmb.next()
            mk.op(mk.pe, lambda e: e.transpose(bkT[0:64, 0:128], sm[:, 256:320], self.identf[:]), reads=[sm, self.identf], writes=[bkT])
            aTh, aTl = aThr.next(), aTlr.next()
            mk.op(mk.dve, lambda e: e.tensor_copy(out=aTh[0:64, :], in_=bkT[0:64, 0:128]), reads=[bkT], writes=[aTh])
            mk.op(mk.dve, lambda e: e.tensor_tensor(out=aTl[0:64, :], in0=bkT[0:64, 0:128], in1=aTh[0:64, :], op=ALU.subtract), reads=[bkT, aTh], writes=[aTl])
            for hf in range(2):
                bk = mmb.next()
                for g4 in range(4):
                    g = hf * 4 + g4
                    self.mm_group(bk, bk[:, g4 * 128:(g4 + 1) * 128], [(BTt[:, g, :], CTt[:, g, :])], [BTt, CTt])
                mk.op(mk.act, lambda e: e.activation(out=cbs[:, hf * 4:(hf + 1) * 4, :].rearrange("p a b -> p (a b)"), in_=bk[:, :], func=AF.Identity),
                      reads=[bk], writes=[cbs])
            return sm, aTh, aTl

        def p2_front_main(t, bufs, pre, steps=()):
            steps = list(steps)
            xs_t, BTt, CTt, z_t, hpf, hpb, hres = bufs
            sm, aTh, aTl = pre
            xdtb = xdtr.next()
            for d in range(2):
                x_ = xdtb[d]
                mk.op(mk.dve, lambda e: e.tensor_tensor(out=x_[:].rearrange("p (h c) -> p h c", h=32),
                                                         in0=xs_t[:].rearrange("p (h c) -> p h c", h=32),
                                                         in1=dt_all[:, t, d * 32:(d + 1) * 32].unsqueeze(2).to_broadcast([128, 32, 64]), op=ALU.mult),
                      reads=[xs_t, dt_all], writes=[x_])
            for g in range(8):
                for d in range(2):
                    it = 2 * g + d
                    if steps and it >= 2 and (it % 2 == 0 or len(steps) > (16 - it) // 2 + 1):
                        steps.pop(0)()
                    eb = ebr.next()
                    msk = self.neghi if d == 0 else self.neglo
                    mk.op(mk.pe, lambda e: e.matmul(eb[:, :], self.ident[:], msk[:], start=True, stop=False),
                          reads=[self.ident, msk], writes=[eb], inc=False)
                    for r in range(4):
                        h = 4 * g + r
                        mk.op(mk.pe, lambda e: e.matmul(eb[:, r * 128:(r + 1) * 128], sel[32 * d:32 * d + 32, h, :], aTh[32 * d:32 * d + 32, :],
                                                        start=False, stop=False),
                              reads=[sel, aTh], writes=[eb], inc=False)
                        mk.op(mk.pe, lambda e: e.matmul(eb[:, r * 128:(r + 1) * 128], sel[32 * d:32 * d + 32, h, :], aTl[32 * d:32 * d + 32, :],
                                                        start=False, stop=(r == 3)),
                              reads=[sel, aTl], writes=[eb], inc=(r == 3))
                    xp = xpr.next()
                    mk.op(mk.dve, lambda e: e.tensor_tensor(out=xp[:], in0=eb[:, :].rearrange("p (a b) -> p a b", a=4),
                                                            in1=sm[:, 192 + 32 * d + 4 * g:192 + 32 * d + 4 * g + 4].unsqueeze(2).to_broadcast([128, 4, 128]),
                                                            op=ALU.add),
                          reads=[eb, sm], writes=[xp])
                    E = Er.next()
                    mk.op(mk.act, lambda e: e.activation(out=E[:].rearrange("p a b -> p (a b)"), in_=xp[:].rearrange("p a b -> p (a b)"), func=AF.Exp),
                          reads=[xp], writes=[E])
                    mk.op(mk.dve, lambda e: e.tensor_tensor(out=Mall[:, 2 * g + d], in0=E[:],
                                                            in1=cbs[:, g, :].unsqueeze(1).to_broadcast([128, 4, 128]), op=ALU.mult),
                          reads=[E, cbs], writes=[Mall])
            while steps:
                steps.pop(0)()
            return sm, xdtb

        def p2_ystage(t, bufs, smx):
            sm, xdtb = smx
            xs_t, BTt, CTt, z_t, hpf, hpb, hres = bufs
            pend = None

            def fin(g_, Ab_, t1_):
                mk.op(mk.dve, lambda e: e.tensor_tensor(out=yall[:, g_ * 256:(g_ + 1) * 256], in0=Ab_[:, 0:256], in1=t1_[:], op=ALU.add),
                      reads=[Ab_, t1_], writes=[yall])

            for g in range(8):
                Ab, Bb = ABr.next()
                self.mm_group(Ab, Ab[:, 256:512], [(CTt[:, g, :], hpf[:, g * 256:(g + 1) * 256])], [CTt, hpf])
                self.mm_group(Bb, Bb[:, 0:256], [(CTt[:, g, :], hpb[:, g * 256:(g + 1) * 256])], [CTt, hpb])
                for r in range(4):
                    h = 4 * g + r
                    self.mm_group(Ab, Ab[:, r * 64:(r + 1) * 64],
                                  [(Mall[:, 2 * g, r, :], xdtb[0][:, h * 64:(h + 1) * 64]), (Mall[:, 2 * g + 1, r, :], xdtb[1][:, h * 64:(h + 1) * 64]),
                                   (dident[:, h, :], xs_t[:, h * 64:(h + 1) * 64])],
                                  [Mall, xdtb[0], xdtb[1], dident, xs_t])
                t1, t2 = t1r.next(), t2r.next()
                mk.op(mk.dve, lambda e: e.tensor_tensor(out=t1[:].rearrange("p (a b) -> p a b", a=4), in0=Ab[:, 256:512].rearrange("p (a b) -> p a b", a=4),
                                                        in1=sm[:, 64 + 4 * g:64 + 4 * g + 4].unsqueeze(2).to_broadcast([128, 4, 64]), op=ALU.mult),
                      reads=[Ab, sm], writes=[t1])
                mk.op(mk.dve, lambda e: e.tensor_tensor(out=t2[:].rearrange("p (a b) -> p a b", a=4), in0=Bb[:, 0:256].rearrange("p (a b) -> p a b", a=4),
                                                        in1=sm[:, 96 + 4 * g:96 + 4 * g + 4].unsqueeze(2).to_broadcast([128, 4, 64]), op=ALU.mult),
                      reads=[Bb, sm], writes=[t2])
                mk.op(mk.dve, lambda e: e.tensor_tensor(out=t1[:], in0=t1[:], in1=t2[:], op=ALU.add), reads=[t1, t2], writes=[t1])
                fin(g, Ab, t1)

        def p2_post_steps(t, bufs, smx):
            sm = smx[0]
            xs_t, BTt, CTt, z_t, hpf, hpb, hres = bufs
            cond = 1 if t < 2 else 0
            steps = []

            def s_gate():
                mk.op(mk.dve, lambda e: e.tensor_tensor(out=t3[:], in0=yall[:], in1=z_t[:], op=ALU.mult), reads=[yall, z_t], writes=[t3])
            steps.append(s_gate)

            def s_sq():
                mk.op(mk.act, lambda e: e.activation(out=yall[:], in_=t3[:], func=AF.Square), reads=[t3], writes=[yall])
            steps.append(s_sq)

            def s_red():
                mk.op(mk.dve, lambda e: e.tensor_reduce(out=sm[:, 128:136], in_=yall[:].rearrange("p (a b) -> p a b", a=8), axis=AX.X, op=ALU.add),
                      reads=[yall], writes=[sm])
            steps.append(s_red)

            def s_sqrt():
                mk.op(mk.act, lambda e: e.activation(out=sm[:, 136:144], in_=sm[:, 128:136], func=AF.Sqrt, scale=1.0 / 256, bias=self.epsc[:, 0:1]),
                      reads=[sm, self.epsc], writes=[sm])
            steps.append(s_sqrt)

            def s_un():
                mk.op(mk.dve, lambda e: e.reciprocal(sm[:, 144:152], sm[:, 136:144]), reads=[sm], writes=[sm])
                mk.op(mk.dve, lambda e: e.tensor_tensor(out=un[:].rearrange("p (a b) -> p a b", a=8), in0=t3[:].rearrange("p (a b) -> p a b", a=8),
                                                        in1=sm[:, 144:152].unsqueeze(2).to_broadcast([128, 8, 256]), op=ALU.mult),
                      reads=[t3, sm], writes=[un])
            steps.append(s_un)

            def mk_tr(hb_):
                def f():
                    bk = mmb.next()
                    bv = bview(bk[:]).rearrange("p (a b) -> p a b", a=8)
                    for k in range(8):
                        kk = hb_ * 8 + k
                        mk.op(mk.pe, lambda e: e.transpose(bv[:, k, :], un[:, kk * 128:(kk + 1) * 128], self.ident[:]),
                              reads=[un, self.ident], writes=[bk], inc=(k == 7))
                    mk.op(mk.dve, lambda e: e.tensor_tensor(out=ynT[:, hb_ * 8:(hb_ + 1) * 8, :], in0=bv,
                                                            in1=nwc[:, hb_ * 8:(hb_ + 1) * 8].unsqueeze(2).to_broadcast([128, 8, 128]), op=ALU.mult),
                          reads=[bk, nwc], writes=[ynT])
                return f
            steps.append(mk_tr(0))
            steps.append(mk_tr(1))

            def mk_out(cg):
                def f():
                    yb = mmb.next()
                    self.mm_group(yb, yb[:, :], [(ynT[:, k, :], wout[:, k, cg * 512:(cg + 1) * 512]) for k in range(16)], [ynT, wout])
                    yt = ytr.next()
                    mk.op(mk.dve, lambda e: e.tensor_tensor(out=yt[:], in0=yb[:, :], in1=gate[cond][:, cg * 512:(cg + 1) * 512], op=ALU.mult),
                          reads=[yb, gate[cond]], writes=[yt])
                    mk.op(mk.dve, lambda e: e.tensor_tensor(out=hres[:, cg * 512:(cg + 1) * 512], in0=hres[:, cg * 512:(cg + 1) * 512], in1=yt[:], op=ALU.add),
                          reads=[yt, hres], writes=[hres])
                return f
            steps.append(mk_out(0))
            steps.append(mk_out(1))

            def s_store():
                if (not last) or self.dump_h:
                    mk.dma(mk.sp, self.h_d[t * 128:(t + 1) * 128, :], hres[:], hres, reads=[hres], writes=[self.HT[t]])
                if last and not self.dump_h:
                    ost = t3
                    mk.op(mk.act, lambda e: e.activation(out=ost[:, 0:D], in_=hres[:], func=AF.Square, accum_out=sm[:, 160:161]), reads=[hres], writes=[ost, sm])
                    mk.op(mk.act, lambda e: e.activation(out=sm[:, 161:162], in_=sm[:, 160:161], func=AF.Sqrt, scale=1.0 / D, bias=self.epsc[:, 0:1]),
                          reads=[sm, self.epsc], writes=[sm])
                    mk.op(mk.dve, lambda e: e.reciprocal(sm[:, 162:163], sm[:, 161:162]), reads=[sm], writes=[sm])
                    mk.op(mk.dve, lambda e: e.tensor_scalar(out=ost[:, 0:D], in0=hres[:], scalar1=sm[:, 162:163], scalar2=None, op0=ALU.mult),
                          reads=[hres, sm], writes=[ost])
                    mk.op(mk.dve, lambda e: e.tensor_tensor(out=ost[:, 0:D], in0=ost[:, 0:D], in1=fnw[:], op=ALU.mult), reads=[ost, fnw], writes=[ost])
                    mk.dma(mk.sp, self.out_d[(t - 2) * 128:(t - 1) * 128, :], ost[:, 0:D], ost, reads=[ost], writes=[self.OUT[t]])
            steps.append(s_store)
            return steps

        cur = p2_load(tiles[0])
        sm_cur = p2_front_main(tiles[0], cur, p2_front(tiles[0], cur))
        for i, t in enumerate(tiles):
            nxt = p2_load(tiles[i + 1]) if i + 1 < len(tiles) else None
            p2_ystage(t, cur, sm_cur)
            steps = p2_post_steps(t, cur, sm_cur)
            if nxt is not None:
                pre = p2_front(tiles[i + 1], nxt)
                sm_nxt = p2_front_main(tiles[i + 1], nxt, pre, steps)
            else:
                sm_nxt = None
                for f in steps:
                    f()
            cur, sm_cur = nxt, sm_nxt


def _consts():
    f32 = np.float32
    k = {}
    k["k_ident"] = np.eye(128, dtype=f32)
    prot = np.zeros((128, 128), f32)
    for base in range(0, 128, 32):
        for d in range(16):
            prot[base + d + 16, base + d] = -1.0
            prot[base + d, base + d + 16] = 1.0
    k["k_prot"] = prot
    c = np.arange(128)[:, None]
    i = np.arange(128)[None, :]
    lo = np.where(c >= i, 0.0, NEG).astype(f32)
    hi = np.where(c <= i, 0.0, NEG).astype(f32)
    k["k_neglo"] = np.tile(lo, (1, 4))
    k["k_neghi"] = np.tile(hi, (1, 4))
    half = 32
    inv_freq = (10000.0 ** (-np.arange(0, half, 2, dtype=f32) / f32(half))).astype(f32)
    pos = np.arange(TL)
    row = (pos // 64).astype(f32)
    col = (pos % 64).astype(f32)
    cosT = np.ones((128, TT), f32)
    sinT = np.zeros((128, TT), f32)
    for d in range(128):
        dl = d % 64
        p = row if dl < 32 else col
        f = (dl % 32) % 16
        ang = (p * inv_freq[f]).astype(f32)
        cosT[d, TC:] = np.cos(ang).astype(f32)
        sinT[d, TC:] = np.sin(ang).astype(f32)
    k["k_cos"] = cosT
    k["k_sin"] = sinT
    t = np.arange(128)[:, None]
    s_ = np.arange(128)[None, :]
    tri = np.stack([(t <= s_), (t >= s_), (t > s_), (t < s_)], axis=1).astype(f32)
    k["k_tri"] = np.ascontiguousarray(tri)
    sel = np.zeros((64, 32, 128), f32)
    for kk in range(64):
        sel[kk, kk % 32, :] = 1.0
    k["k_sel"] = sel.reshape(64, 32 * 128)
    return k


def _pk(w):
    K, N = w.shape
    return np.ascontiguousarray(w.reshape(K // 128, 128, N).transpose(1, 0, 2))


def prep_shared(inputs):
    f32 = np.float32
    g = {}
    g["w_ada"] = np.stack([_pk(inputs["w_ada"][l]) for l in range(4)])
    b = inputs["b_ada"]
    g["b_col"] = np.ascontiguousarray(b[:, :2048].reshape(4, 16, 128).transpose(2, 0, 1))
    g["b_gate"] = np.ascontiguousarray(b[:, 2048:])
    qorder = []
    for cidx in range(8):
        a = cidx if cidx < 4 else 8 + (cidx - 4)
        bb = 4 + cidx if cidx < 4 else 12 + (cidx - 4)
        qorder += list(range(a * 64, a * 64 + 64)) + list(range(bb * 64, bb * 64 + 64))
    cols = np.array(qorder + list(range(1024, 2560)))
    g["attn_w_in"] = np.stack([_pk(inputs["attn_w_in"][j][:, cols]) for j in range(2)])
    g["attn_w_out"] = np.stack([_pk(inputs["attn_w_out"][j]) for j in range(2)])
    g["attn_sink"] = np.ascontiguousarray(inputs["attn_sink"])
    w = inputs["ssd_w_in"]
    xbc = w[:, :, 2048:2048 + 4096]
    g["ssd_w_xbc"] = np.ascontiguousarray(
        xbc.reshape(2, 8, 128, 32, 128).transpose(0, 3, 2, 1, 4).reshape(2, 32, 128, 1024))
    zdt = np.concatenate([w[:, :, :2048], w[:, :, 2048 + 4096:]], axis=2)
    g["ssd_w_zdt"] = np.stack([_pk(zdt[j]) for j in range(2)])
    g["ssd_conv_w"] = np.ascontiguousarray(inputs["ssd_conv_w"].reshape(2, 5, 32, 128).transpose(3, 0, 2, 1))
    g["ssd_conv_b"] = np.ascontiguousarray(inputs["ssd_conv_b"].reshape(2, 32, 128).transpose(2, 0, 1))
    g["ssd_dt_bias"] = np.ascontiguousarray(inputs["ssd_dt_bias"].reshape(2, 64))
    g["ssd_a_log"] = np.ascontiguousarray(inputs["ssd_a_log"].reshape(2, 64))
    g["ssd_d"] = np.ascontiguousarray(inputs["ssd_d"])
    g["ssd_norm_w"] = np.ascontiguousarray(inputs["ssd_norm_w"].reshape(2, 16, 128).transpose(2, 0, 1))
    g["ssd_w_out"] = np.stack([_pk(inputs["ssd_w_out"][j]) for j in range(2)])
    g["final_norm_w"] = np.ascontiguousarray(inputs["final_norm_w"].reshape(1, 1024))
    g.update(_consts())
    return {k_: np.ascontiguousarray(v, dtype=f32) for k_, v in g.items()}


def prep_core(inputs, shared, b):
    m = dict(shared)
    m["x"] = np.ascontiguousarray(inputs["x"][b], dtype=np.float32)
    m["ctx"] = np.ascontiguousarray(inputs["ctx"][b], dtype=np.float32)
    cc = np.stack([inputs["c"][b].reshape(8, 128).T, inputs["c_ctx"].reshape(8, 128).T], axis=2)
    m["c_col2"] = np.ascontiguousarray(cc, dtype=np.float32)
    return m


_NC_CACHE = {}


def kernel(**inputs):
    inputs = {k_: np.asarray(v) for k_, v in inputs.items()}
    if "nc" not in _NC_CACHE:
        _NC_CACHE["nc"] = Prog(n_layers=4).build()
    nc = _NC_CACHE["nc"]
    shared = prep_shared(inputs)
    in_maps = [prep_core(inputs, shared, b) for b in range(8)]
    res = run_bass_kernel_spmd(nc, in_maps, core_ids=list(range(8)))
    return np.stack([r["out"] for r in res.results], axis=0).astype(np.float32)
```

```python
import math
from contextlib import ExitStack

import numpy as np
import concourse.bass as bass
import concourse.mybir as mybir
from concourse.bass_utils import run_bass_kernel_spmd

F32 = mybir.dt.float32
BF16 = mybir.dt.bfloat16
AF = mybir.ActivationFunctionType
ALU = mybir.AluOpType
AX = mybir.AxisListType

SAME_ENGINE_WAITS = True
DBG = {}


class Q:
    def __init__(self, mk, name, eng, is_pe=False):
        self.mk = mk
        self.name = name
        self.eng = eng
        self.sem = mk.new_sem("q_" + name)
        self.count = 0
        self.seen = {}
        self.is_pe = is_pe

    def wait_tokens(self, toks):
        best = {}
        for tok in toks:
            if tok is None:
                continue
            sem, val, q = tok
            if q is self and (self.is_pe or not SAME_ENGINE_WAITS):
                continue
            k = id(sem)
            if self.seen.get(k, 0) >= val:
                continue
            if k not in best or best[k][1] < val:
                best[k] = (sem, val)
        for k, (sem, val) in best.items():
            self.eng.wait_ge(sem, val)
            self.seen[k] = val


class T:
    def __init__(self, name, ap=None, dsem=None):
        self.name = name
        self.ap = ap
        self.w = []
        self.r = []
        self.dsem = dsem
        self.dcount = 0
        self.psum = False

    def __getitem__(self, k):
        return self.ap[k]


class MK:
    def __init__(self, nc, es):
        self.nc = nc
        self.es = es
        self.nsem = 0
        self.pe = Q(self, "pe", nc.tensor, is_pe=True)
        self.act = Q(self, "act", nc.scalar)
        self.dve = Q(self, "dve", nc.vector)
        self.pool = Q(self, "pool", nc.gpsimd)
        self.sp = Q(self, "sp", nc.sync)
        self.n_inst = 0
        self.dts = []

    def new_sem(self, name):
        self.nsem += 1
        return self.es.enter_context(self.nc.semaphore(f"{name}_{self.nsem}"))

    def sbuf(self, name, shape, dtype, dma=False):
        h = self.es.enter_context(self.nc.sbuf_tensor(name, list(shape), dtype))
        t = T(name, h, self.new_sem("d_" + name) if dma else None)
        if dma:
            self.dts.append(t)
        return t

    def psum(self, name, shape, dtype=F32):
        h = self.es.enter_context(self.nc.psum_tensor(name, list(shape), dtype))
        t = T(name, h)
        t.psum = True
        return t

    def dram(self, name, shape, dtype, kind="Internal"):
        h = self.nc.dram_tensor(name, list(shape), dtype, kind=kind)
        return T(name, h.ap() if hasattr(h, "ap") else h)

    def op(self, q, fn, reads=(), writes=(), inc=True):
        toks = []
        for t in reads:
            toks += t.w
            if t.psum:
                toks += [x for x in t.r if x[2] is not q]
        for t in writes:
            toks += t.w
            toks += t.r
        q.wait_tokens(toks)
        ins = fn(q.eng)
        self.n_inst += 1
        if not inc:
            tok = (q.sem, q.count + 1, q)
        else:
            q.count += 1
            ins.then_inc(q.sem, 1)
            tok = (q.sem, q.count, q)
        for t in writes:
            t.w = [tok]
            t.r = []
        for t in reads:
            if t not in writes:
                t.r.append(tok)
                if len(t.r) > 24:
                    t.r = _prune(t.r)
        return ins

    def dma(self, q, out, in_, side, reads=(), writes=(), n_parts=None, **kw):
        pairs = out if isinstance(out, list) else [(out, in_)]
        toks = []
        for t in reads:
            toks += t.w
        for t in writes:
            toks += t.w
            toks += t.r
        if side.dcount:
            toks.append((side.dsem, side.dcount, None))
        q.wait_tokens(toks)
        for (o, i) in pairs:
            q.eng.dma_start(out=o, in_=i, **kw).then_inc(side.dsem, 16)
            side.dcount += 16
            self.n_inst += 1
        tok = (side.dsem, side.dcount, None)
        for t in writes:
            t.w = [tok]
            t.r = []
        for t in reads:
            if t not in writes:
                t.r.append(tok)
                if len(t.r) > 24:
                    t.r = _prune(t.r)

    def barrier(self):
        toks = [(p.sem, p.count, None) for p in self.queues() if p.count]
        toks += [(t.dsem, t.dcount, None) for t in self.dts if t.dcount]
        for q in self.queues():
            q.wait_tokens(toks)

    def queues(self):
        return [self.pe, self.act, self.dve, self.pool, self.sp]

    def finish(self, outs):
        toks = []
        for t in outs:
            toks += t.w
        self.sp.wait_tokens(toks)


def _prune(toks):
    best = {}
    for (sem, val, q) in toks:
        k = id(sem)
        if k not in best or best[k][1] < val:
            best[k] = (sem, val, q)
    return list(best.values())


D = 1024
KD = 8
TC = 256
TL = 4096
TT = TC + TL
NT = TT // 128
EPS = 1e-6
NEG = -30000.0
GROUPS = [(0, 256)] + [(256 + 512 * i, 512) for i in range(8)]
UCOLS = TT + 8


def ucol(tok):
    return tok + (2 if tok < TC else 6)


class Arena:
    def __init__(self, mk, nbytes):
        self.mk = mk
        self.h = mk.es.enter_context(mk.nc.sbuf_tensor("arena", [128, nbytes // 4], F32))
        self.cap = nbytes // 4
        self.off = 0
        self.cache = {}

    def reset(self, off=0):
        self.off = off

    def alloc(self, key, shape, dtype, dma=False):
        nel = 1
        for s_ in shape[1:]:
            nel *= s_
        nb = nel * (4 if dtype == F32 else 2)
        n4 = (nb + 3) // 4
        n4 = (n4 + 7) // 8 * 8
        assert self.off + n4 <= self.cap, f"arena overflow at {key}: {self.off + n4} > {self.cap}"
        ck = (key, self.off, tuple(shape), str(dtype))
        off = self.off
        self.off += n4
        if ck in self.cache:
            return self.cache[ck]
        ap = self.h[:, off:off + n4]
        if dtype != F32:
            ap = ap.bitcast(dtype)
        ap = ap[:, 0:nel]
        if len(shape) == 3:
            ap = ap.rearrange("p (a b) -> p a b", a=shape[1])
        elif len(shape) == 4:
            ap = ap.rearrange("p (a b c) -> p a b c", a=shape[1], b=shape[2])
        t = T(key, ap, self.mk.new_sem("d_" + key) if dma else None)
        if dma:
            self.mk.dts.append(t)
        self.cache[ck] = t
        return t


class Rot:
    def __init__(self, items):
        self.items = items
        self.i = 0

    def next(self):
        t = self.items[self.i % len(self.items)]
        self.i += 1
        return t


def bview(bank_ap):
    return bank_ap.bitcast(BF16)


class Prog:
    def __init__(self, n_layers=4, dump_h=False):
        self.n_layers = n_layers
        self.dump_h = dump_h
        self.nc = bass.Bass("TRN2", target_bir_lowering=False)
        self.es = ExitStack()

    def inp(self, name, shape):
        return self.nc.dram_tensor(name, list(shape), F32, kind="ExternalInput").ap()

    def build(self):
        nc = self.nc
        with self.es:
            mk = self.mk = MK(nc, self.es)
            self.x_d = self.inp("x", [TL, D])
            self.ctx_d = self.inp("ctx", [TC, D])
            self.c_col2 = self.inp("c_col2", [128, 8, 2])
            self.w_ada = self.inp("w_ada", [4, 128, 8, 3072])
            self.b_col = self.inp("b_col", [128, 4, 16])
            self.b_gate = self.inp("b_gate", [4, 1024])
            self.a_win = self.inp("attn_w_in", [2, 128, 8, 2560])
            self.a_wout = self.inp("attn_w_out", [2, 128, 8, 1024])
            self.a_sink = self.inp("attn_sink", [2, 16])
            self.s_wxbc = self.inp("ssd_w_xbc", [2, 32, 128, 1024])
            self.s_wzdt = self.inp("ssd_w_zdt", [2, 128, 8, 2112])
            self.s_convw = self.inp("ssd_conv_w", [128, 2, 32, 5])
            self.s_convb = self.inp("ssd_conv_b", [128, 2, 32])
            self.s_dtb = self.inp("ssd_dt_bias", [2, 64])
            self.s_alog = self.inp("ssd_a_log", [2, 64])
            self.s_d = self.inp("ssd_d", [2, 32])
            self.s_nw = self.inp("ssd_norm_w", [128, 2, 16])
            self.s_wout = self.inp("ssd_w_out", [2, 128, 16, 1024])
            self.fnw = self.inp("final_norm_w", [1, 1024])
            self.k_ident = self.inp("k_ident", [128, 128])
            self.k_prot = self.inp("k_prot", [128, 128])
            self.k_neglo = self.inp("k_neglo", [128, 512])
            self.k_neghi = self.inp("k_neghi", [128, 512])
            self.k_cos = self.inp("k_cos", [128, TT])
            self.k_sin = self.inp("k_sin", [128, TT])
            self.k_tri = self.inp("k_tri", [128, 4, 128])
            self.k_sel = self.inp("k_sel", [64, 32 * 128])
            if self.dump_h:
                self.out_d = nc.dram_tensor("out", [TT, D], F32, kind="ExternalOutput").ap()
            else:
                self.out_d = nc.dram_tensor("out", [TL, D], F32, kind="ExternalOutput").ap()
            self.h_d = nc.dram_tensor("h_scr", [TT, D], F32).ap()
            self.gates_d = nc.dram_tensor("gates_scr", [8, D], F32).ap()
            self.xs_d = nc.dram_tensor("xs_scr", [TT, 2048], BF16).ap()
            self.bt_d = nc.dram_tensor("btok_scr", [TT, 1024], BF16).ap()
            self.BT_d = nc.dram_tensor("BT_scr", [8, 128, TT], BF16).ap()
            self.CT_d = nc.dram_tensor("CT_scr", [8, 128, TT], BF16).ap()
            self.z_d = nc.dram_tensor("z_scr", [TT, 2048], BF16).ap()
            self.hp_d = nc.dram_tensor("hprev_scr", [2, NT, 128, 2048], BF16).ap()
            self.HT = [T(f"h_tile{t}") for t in range(NT)]
            self.GT = T("gates")
            self.XS = [T(f"xs{t}") for t in range(NT)]
            self.BTK = [T(f"btk{t}") for t in range(NT)]
            self.BTT = T("BT")
            self.CTT = T("CT")
            self.ZT = [T(f"z{t}") for t in range(NT)]
            self.HP = [[T(f"hp{d}_{t}") for t in range(NT)] for d in range(2)]
            self.OUT = [T(f"out{t}") for t in range(NT)]
            self.BTT = [T(f"BT{g}") for g in range(8)]
            self.CTT = [T(f"CT{g}") for g in range(8)]
            self.XIN = T("xin")
            self.ident = mk.sbuf("ident", [128, 128], BF16, dma=True)
            self.prot = mk.sbuf("prot", [128, 128], BF16, dma=True)
            self.neglo = mk.sbuf("neglo", [128, 512], BF16, dma=True)
            self.neghi = mk.sbuf("neghi", [128, 512], BF16, dma=True)
            self.tri = mk.sbuf("tri", [128, 4, 128], F32, dma=True)
            self.onesf = mk.sbuf("onesf", [128, 128], F32)
            self.identf = mk.sbuf("identf", [128, 128], F32, dma=True)
            self.modcol = mk.sbuf("modcol", [128, 4, 16, 2], F32)
            self.banks = [mk.psum(f"bank{i}", [128, 512], F32) for i in range(8)]
            self.arena = Arena(mk, 200 * 1024)
            K = T("consts")
            mk.dma(mk.pool, self.ident[:], self.k_ident, self.ident, reads=[K], writes=[self.ident])
            mk.dma(mk.pool, self.prot[:], self.k_prot, self.prot, reads=[K], writes=[self.prot])
            mk.dma(mk.pool, self.neglo[:], self.k_neglo, self.neglo, reads=[K], writes=[self.neglo])
            mk.dma(mk.pool, self.neghi[:], self.k_neghi, self.neghi, reads=[K], writes=[self.neghi])
            mk.dma(mk.sp, self.tri[:], self.k_tri, self.tri, reads=[K], writes=[self.tri])
            mk.dma(mk.sp, self.identf[:], self.k_ident, self.identf, reads=[K], writes=[self.identf])
            mk.op(mk.dve, lambda e: e.memset(self.onesf[:], 1.0), writes=[self.onesf])

            self.prologue()
            for li in range(self.n_layers):
                mk.barrier()
                if li % 2 == 0:
                    self.attn_layer(li)
                else:
                    self.ssd_layer(li)
            if self.dump_h:
                self.dump()
            mk.barrier()
        return nc

    def mm_group(self, bank_t, out_ap, pairs, reads, start=True, stop=True):
        mk = self.mk
        n = len(pairs)
        for i, (l, r) in enumerate(pairs):
            mk.op(mk.pe, lambda e, l=l, r=r, i=i: e.matmul(out_ap, l, r, start=(start and i == 0), stop=(stop and i == n - 1)),
                  reads=reads, writes=[bank_t], inc=(i == n - 1))

    def h_src(self, li, t):
        if li == 0:
            return (self.ctx_d[t * 128:(t + 1) * 128, :] if t < 2 else self.x_d[(t - 2) * 128:(t - 1) * 128, :]), self.XIN
        return self.h_d[t * 128:(t + 1) * 128, :], self.HT[t]

    def prologue(self):
        mk, A = self.mk, self.arena
        A.reset()
        wadar = Rot([A.alloc(f"wada{i}", [128, 8, 3072], BF16, dma=True) for i in range(2)])
        ccol = A.alloc("ccol", [128, 8, 2], F32, dma=True)
        sc2 = A.alloc("sc2", [128, 8, 2], BF16)
        bcol = A.alloc("bcol", [128, 4, 16], F32, dma=True)
        bg = A.alloc("bg", [128, 1024], F32, dma=True)
        grow = A.alloc("grow", [128, 1024], F32, dma=True)
        K = T("pin")
        mk.dma(mk.sp, ccol[:], self.c_col2, ccol, reads=[K], writes=[ccol])
        mk.dma(mk.sp, bcol[:], self.b_col, bcol, reads=[K], writes=[bcol])
        mk.op(mk.act, lambda e: e.activation(out=sc2[:], in_=ccol[:], func=AF.Silu), reads=[ccol], writes=[sc2])
        wadas = []
        for l in range(4):
            wada = wadar.next()
            if l < 2:
                mk.dma(mk.pool, [(wada[:, k, :], self.w_ada[l, :, k, :]) for k in range(8)], None, wada, reads=[K], writes=[wada], max_dma_last_dim=4096)
            wadas.append(wada)
        for l in range(4):
            wada = wadas[l]
            mk.dma(mk.sp, bg[0:2, :], self.b_gate[l:l + 1, :].to_broadcast([2, 1024]), bg, reads=[K], writes=[bg])
            b0 = self.banks[0]
            pscol = b0[:, 0:32].rearrange("p (f c) -> p f c", c=2)
            for f in range(16):
                self.mm_group(b0, pscol[:, f, :], [(wada[:, k, f * 128:(f + 1) * 128], sc2[:, k, :]) for k in range(8)], [wada, sc2])
            mk.op(mk.dve, lambda e: e.tensor_tensor(out=self.modcol[:, l], in0=pscol,
                                                    in1=bcol[:, l, :].unsqueeze(2).to_broadcast([128, 16, 2]), op=ALU.add),
                  reads=[b0, bcol], writes=[self.modcol])
            mk.op(mk.dve, lambda e: e.tensor_scalar(out=self.modcol[:, l, 8:16, :], in0=self.modcol[:, l, 8:16, :],
                                                    scalar1=1.0, scalar2=None, op0=ALU.add),
                  reads=[self.modcol], writes=[self.modcol])
            for cg in range(2):
                bk = self.banks[1 + cg]
                self.mm_group(bk, bk[0:2, :], [(sc2[:, k, :], wada[:, k, 2048 + cg * 512:2048 + (cg + 1) * 512]) for k in range(8)], [wada, sc2])
                mk.op(mk.dve, lambda e, bk=bk, cg=cg: e.tensor_tensor(out=grow[0:2, cg * 512:(cg + 1) * 512], in0=bk[0:2, :],
                                                                      in1=bg[0:2, cg * 512:(cg + 1) * 512], op=ALU.add),
                      reads=[bk, bg], writes=[grow])
            mk.dma(mk.sp, self.gates_d[2 * l:2 * l + 2, :], grow[0:2, :], grow, reads=[grow], writes=[self.GT])
            if l + 2 < 4:
                mk.dma(mk.pool, [(wada[:, k, :], self.w_ada[l + 2, :, k, :]) for k in range(8)], None, wada, reads=[K], writes=[wada], max_dma_last_dim=4096)

    def dump(self):
        mk, A = self.mk, self.arena
        mk.barrier()
        A.reset()
        bufs = Rot([A.alloc(f"dmp{i}", [128, D], F32, dma=True) for i in range(4)])
        for t in range(NT):
            b = bufs.next()
            mk.dma(mk.sp, b[:], self.h_d[t * 128:(t + 1) * 128, :], b, reads=[self.HT[t]], writes=[b])
            mk.dma(mk.sp, self.out_d[t * 128:(t + 1) * 128, :], b[:], b, reads=[b], writes=[self.OUT[t]])

    def norm_tile(self, li, t, hin, hn, st, dst_T, dst_fn):
        mk = self.mk
        cond = 1 if t < 2 else 0
        mk.op(mk.act, lambda e: e.activation(out=hn[:], in_=hin[:], func=AF.Square, accum_out=st[:, 0:1]),
              reads=[hin], writes=[hn, st])
        mk.op(mk.act, lambda e: e.activation(out=st[:, 1:2], in_=st[:, 0:1], func=AF.Ln, scale=1.0 / D, bias=self.epsc[:, 0:1]),
              reads=[st, self.epsc], writes=[st])
        mk.op(mk.act, lambda e: e.activation(out=st[:, 2:3], in_=st[:, 1:2], func=AF.Exp, scale=-0.5), reads=[st], writes=[st])
        mk.op(mk.dve, lambda e: e.tensor_scalar(out=hn[:], in0=hin[:], scalar1=st[:, 2:3], scalar2=None, op0=ALU.mult),
              reads=[hin, st], writes=[hn])
        if DBG.get("nstop", 9) < 2:
            return
        bk = self.trb.next()
        bv = bview(bk[:]).rearrange("p (a b) -> p a b", a=8)
        for k in range(8):
            mk.op(mk.pe, lambda e, k=k: e.transpose(bv[:, k, :], hn[:, k * 128:(k + 1) * 128], self.ident[:]),
                  reads=[hn, self.ident], writes=[bk], inc=(k == 7))
        if DBG.get("nstop", 9) < 3:
            return
        for k in range(8):
            mk.op(mk.dve, lambda e, k=k: e.tensor_scalar(out=dst_fn(k), in0=bv[:, k, :],
                                                         scalar1=self.modcol[:, li, 8 + k, cond:cond + 1],
                                                         scalar2=self.modcol[:, li, k, cond:cond + 1],
                                                         op0=ALU.mult, op1=ALU.add),
                  reads=[bk, self.modcol], writes=[dst_T])

    def attn_layer(self, li):
        mk, A = self.mk, self.arena
        j = li // 2
        A.reset()
        win = A.alloc("a_win", [128, 8, 2560], BF16, dma=True)
        wout = A.alloc("a_wout", [128, 8, 1024], BF16, dma=True)
        gate = [A.alloc(f"a_gate{i}", [128, D], F32, dma=True) for i in range(2)]
        esink = A.alloc("a_esink", [128, 16], F32, dma=True)
        kT = A.alloc("a_kT", [128, 2, TT], BF16)
        vaug = A.alloc("a_vaug", [128, NT, 4, 65], BF16)
        hinr = Rot([A.alloc(f"a_hin{i}", [128, D], F32, dma=True) for i in range(3)])
        hresr = Rot([A.alloc(f"a_hres{i}", [128, D], F32, dma=True) for i in range(2)])
        hnr = Rot([A.alloc(f"a_hn{i}", [128, D], BF16) for i in range(2)])
        str_ = Rot([A.alloc(f"a_st{i}", [128, 4], F32) for i in range(4)])
        uT = A.alloc("a_uT", [128, 8, 512], BF16)
        qTr = Rot([A.alloc(f"a_qT{i}", [128, 8, 512], BF16) for i in range(2)])
        gsr = Rot([A.alloc(f"a_gs{i}", [128, 4, D], BF16) for i in range(2)])
        cosg = A.alloc("a_cos", [128, 512], F32, dma=True)
        sing = A.alloc("a_sin", [128, 512], F32, dma=True)
        qbr = Rot([A.alloc(f"a_qb{i}", [128, 512], BF16) for i in range(2)])
        t1r = Rot([A.alloc(f"a_t1{i}", [128, 512], F32) for i in range(2)])
        t2r = Rot([A.alloc(f"a_t2{i}", [128, 512], F32) for i in range(1)])
        pTr = Rot([A.alloc(f"a_pT{i}", [128, 5, 512], BF16) for i in range(2)])
        denr = Rot([A.alloc(f"a_den{i}", [128, 8], F32) for i in range(2)])
        otr = Rot([A.alloc(f"a_ot{i}", [128, 256], F32) for i in range(2)])
        og = A.alloc("a_og", [128, D], BF16)
        ogT = A.alloc("a_ogT", [128, 8, 128], BF16)
        ytr = Rot([A.alloc(f"a_yt{i}", [128, 512], F32) for i in range(2)])
        self.epsc = A.alloc("a_eps", [128, 1], F32)
        self.trb = Rot(self.banks[0:2])
        mmb = Rot(self.banks[2:4])
        sb = Rot(self.banks[4:7])
        ob = self.banks[7]
        W = T("wsrc")
        mk.op(mk.dve, lambda e: e.memset(self.epsc[:], EPS), writes=[self.epsc])
        mk.op(mk.dve, lambda e: e.memset(vaug[:, :, :, 64:65], 1.0), writes=[vaug])
        mk.dma(mk.pool, [(win[:, k, :], self.a_win[j, :, k, :]) for k in range(8)], None, win, reads=[W], writes=[win], max_dma_last_dim=4096)
        mk.dma(mk.pool, [(wout[:, k, :], self.a_wout[j, :, k, :]) for k in range(8)], None, wout, reads=[W], writes=[wout], max_dma_last_dim=4096)
        mk.dma(mk.sp, gate[0][:], self.gates_d[2 * li:2 * li + 1, :].to_broadcast([128, D]), gate[0], reads=[self.GT], writes=[gate[0]])
        mk.dma(mk.sp, gate[1][:], self.gates_d[2 * li + 1:2 * li + 2, :].to_broadcast([128, D]), gate[1], reads=[self.GT], writes=[gate[1]])
        mk.dma(mk.sp, esink[:], self.a_sink[j:j + 1, :].to_broadcast([128, 16]), esink, reads=[W], writes=[esink])
        mk.op(mk.act, lambda e: e.activation(out=esink[:], in_=esink[:], func=AF.Exp), reads=[esink], writes=[esink])
        scale = 1.0 / 8.0

        hin_of = {}
        nload = [0]

        def ensure_loaded(upto):
            while nload[0] <= min(upto, NT - 1):
                t = nload[0]
                b = hinr.next()
                src, srcT = self.h_src(li, t)
                mk.dma(mk.sp, b[:], src, b, reads=[srcT], writes=[b])
                hin_of[t] = b
                nload[0] += 1

        def rope_chunk(c, ntok, dst_T, dst_ap):
            pb = mmb.next()
            self.mm_group(pb, pb[:, 0:ntok], [(win[:, k, c * 128:(c + 1) * 128], uT[:, k, 0:ntok]) for k in range(8)], [win, uT])
            qb = qbr.next()
            mk.op(mk.act, lambda e: e.activation(out=qb[:, 0:ntok], in_=pb[:, 0:ntok], func=AF.Identity), reads=[pb], writes=[qb])
            rb = mmb.next()
            self.mm_group(rb, rb[:, 0:ntok], [(self.prot[:], qb[:, 0:ntok])], [self.prot, qb])
            t1 = t1r.next()
            t2 = t2r.next()
            mk.op(mk.dve, lambda e: e.tensor_tensor(out=t1[:, 0:ntok], in0=pb[:, 0:ntok], in1=cosg[:, 0:ntok], op=ALU.mult),
                  reads=[pb, cosg], writes=[t1])
            mk.op(mk.dve, lambda e: e.tensor_tensor(out=t2[:, 0:ntok], in0=rb[:, 0:ntok], in1=sing[:, 0:ntok], op=ALU.mult),
                  reads=[rb, sing], writes=[t2])
            mk.op(mk.dve, lambda e: e.tensor_tensor(out=dst_ap, in0=t1[:, 0:ntok], in1=t2[:, 0:ntok], op=ALU.add),
                  reads=[t1, t2], writes=[dst_T])

        def stage_p(n):
            tok0, ntok = GROUPS[n]
            nt = ntok // 128
            qT = qTr.next()
            gs = gsr.next()

            def part0():
                mk.dma(mk.sp, cosg[:, 0:ntok], self.k_cos[:, tok0:tok0 + ntok], cosg, reads=[W], writes=[cosg])
                mk.dma(mk.sp, sing[:, 0:ntok], self.k_sin[:, tok0:tok0 + ntok], sing, reads=[W], writes=[sing])
                for ti in range(nt):
                    t = tok0 // 128 + ti
                    ensure_loaded(t + 2)
                    self.norm_tile(li, t, hin_of.pop(t), hnr.next(), str_.next(), uT,
                                   lambda k, ti=ti: uT[:, k, ti * 128:(ti + 1) * 128])

            def part1():
                for c in (8, 9):
                    rope_chunk(c, ntok, kT, kT[:, c - 8, tok0:tok0 + ntok])
                for c in range(0, 3):
                    rope_chunk(c, ntok, qT, qT[:, c, 0:ntok])

            def part2():
                for c in range(3, 8):
                    rope_chunk(c, ntok, qT, qT[:, c, 0:ntok])

            def part3():
                for ti in range(nt):
                    t = tok0 // 128 + ti
                    vb = mmb.next()
                    self.mm_group(vb, vb[:, 0:256], [(uT[:, k, ti * 128:(ti + 1) * 128], win[:, k, 1280:1536]) for k in range(8)], [win, uT])
                    mk.op(mk.act, lambda e, vb=vb, t=t: e.activation(out=vaug[:, t, :, 0:64],
                                                                     in_=vb[:, 0:256].rearrange("p (a b) -> p a b", a=4), func=AF.Identity),
                          reads=[vb], writes=[vaug])
                    for cg in range(2):
                        gb = mmb.next()
                        self.mm_group(gb, gb[:, :], [(uT[:, k, ti * 128:(ti + 1) * 128], win[:, k, 1536 + cg * 512:1536 + (cg + 1) * 512])
                                                     for k in range(8)], [win, uT])
                        mk.op(mk.act, lambda e, gb=gb, ti=ti, cg=cg: e.activation(out=gs[:, ti, cg * 512:(cg + 1) * 512], in_=gb[:, :], func=AF.Silu),
                              reads=[gb], writes=[gs])

            return qT, gs, [part0, part1, part2, part3]

        def stage_a(n, qT, gs, parts=()):
            parts = list(parts)
            tok0, ntok = GROUPS[n]
            nta = ntok // 128
            for ti in range(nta):
                t = tok0 // 128 + ti
                while parts and len(parts) > (nta - 1 - ti) * (4 // max(nta, 1)) - 0 and (len(parts) > 4 - (ti + 1) * (4 // nta)):
                    parts.pop(0)()
                if t < 2:
                    kbs = [(0, None), (1, None)]
                else:
                    kbs = []
                    if t - 1 >= 2:
                        kbs.append((t - 1, self.neglo))
                    kbs.append((t, None))
                    if t + 1 < NT:
                        kbs.append((t + 1, self.neghi))
                    kbs += [(0, None), (1, None)]
                hres = hresr.next()
                src, srcT = self.h_src(li, t)
                mk.dma(mk.sp, hres[:], src, hres, reads=[srcT], writes=[hres])
                def qk_stage(kh):
                    pbase = 64 * (kh % 2)
                    kc = kh // 2
                    pT = pTr.next()
                    for b, (kt, msk) in enumerate(kbs):
                        sbk = sb.next()
                        prs = [(kT[pbase:pbase + 64, kc, kt * 128:(kt + 1) * 128],
                                qT[pbase:pbase + 64, 4 * kc:4 * kc + 4, ti * 128:(ti + 1) * 128])]
                        rds = [kT, qT]
                        if msk is not None:
                            prs.append((self.ident[:], msk[:]))
                            rds += [self.ident, msk]
                        self.mm_group(sbk, sbk[:, :], prs, rds)
                        mk.op(mk.act, lambda e, sbk=sbk, b=b: e.activation(out=pT[:, b, :], in_=sbk[:, :], func=AF.Exp, scale=scale),
                              reads=[sbk], writes=[pT])
                    return pT

                pT_next = qk_stage(0)
                for kh in range(4):
                    pT = pT_next
                    if kh + 1 < 4:
                        pT_next = qk_stage(kh + 1)
                    ov = ob[:, :].rearrange("p (a b) -> p a b", a=4)
                    for r in range(4):
                        self.mm_group(ob, ov[:, r, 0:65],
                                      [(pT[:, b, r * 128:(r + 1) * 128], vaug[:, kt, kh, :]) for b, (kt, msk) in enumerate(kbs)],
                                      [pT, vaug])
                    den = denr.next()
                    ot = otr.next()
                    mk.op(mk.dve, lambda e, kh=kh: e.tensor_tensor(out=den[:, 0:4], in0=ov[:, :, 64], in1=esink[:, 4 * kh:4 * kh + 4], op=ALU.add),
                          reads=[ob, esink], writes=[den])
                    mk.op(mk.dve, lambda e: e.reciprocal(den[:, 4:8], den[:, 0:4]), reads=[den], writes=[den])
                    otv = ot[:, :].rearrange("p (a b) -> p a b", a=4)
                    mk.op(mk.dve, lambda e: e.tensor_tensor(out=otv, in0=ov[:, :, 0:64],
                                                            in1=den[:, 4:8].unsqueeze(2).to_broadcast([128, 4, 64]), op=ALU.mult),
                          reads=[ob, den], writes=[ot])
                    mk.op(mk.dve, lambda e, kh=kh: e.tensor_tensor(out=og[:, kh * 256:(kh + 1) * 256], in0=ot[:, :],
                                                                    in1=gs[:, ti, kh * 256:(kh + 1) * 256], op=ALU.mult),
                          reads=[ot, gs], writes=[og])
                bk = self.trb.next()
                bv = bview(bk[:]).rearrange("p (a b) -> p a b", a=8)
                for k in range(8):
                    mk.op(mk.pe, lambda e, k=k: e.transpose(bv[:, k, :], og[:, k * 128:(k + 1) * 128], self.ident[:]),
                          reads=[og, self.ident], writes=[bk], inc=(k == 7))
                mk.op(mk.dve, lambda e: e.tensor_copy(out=ogT[:], in_=bv), reads=[bk], writes=[ogT])
                gt = gate[1] if t < 2 else gate[0]
                for cg in range(2):
                    yb = mmb.next()
                    self.mm_group(yb, yb[:, :], [(ogT[:, k, :], wout[:, k, cg * 512:(cg + 1) * 512]) for k in range(8)], [ogT, wout])
                    yt = ytr.next()
                    mk.op(mk.dve, lambda e, yb=yb, cg=cg: e.tensor_tensor(out=yt[:], in0=yb[:, :], in1=gt[:, cg * 512:(cg + 1) * 512], op=ALU.mult),
                          reads=[yb, gt], writes=[yt])
                    mk.op(mk.dve, lambda e, cg=cg: e.tensor_tensor(out=hres[:, cg * 512:(cg + 1) * 512], in0=hres[:, cg * 512:(cg + 1) * 512],
                                                                    in1=yt[:], op=ALU.add),
                          reads=[yt, hres], writes=[hres])
                mk.dma(mk.sp, self.h_d[t * 128:(t + 1) * 128, :], hres[:], hres, reads=[hres], writes=[self.HT[t]])
            while parts:
                parts.pop(0)()

        ng = len(GROUPS)
        cur = stage_p(0)
        for f in cur[2]:
            f()
        for n in range(ng):
            nxt = stage_p(n + 1) if n + 1 < ng else None
            stage_a(n, cur[0], cur[1], nxt[2] if nxt is not None else ())
            cur = nxt

    def ssd_layer(self, li):
        mk, A = self.mk, self.arena
        j = li // 2
        last = (li == 3)
        need_ctx = not last
        A.reset()
        W = T("wsrc_s")
        gate = [A.alloc(f"s_gate{i}", [128, D], F32, dma=True) for i in range(2)]
        convw = A.alloc("s_convw", [128, 32, 5], F32, dma=True)
        convb = A.alloc("s_convb", [128, 32], F32, dma=True)
        nwc = A.alloc("s_nwc", [128, 16], F32, dma=True)
        dtb = A.alloc("s_dtb", [128, 64], F32, dma=True)
        arow = A.alloc("s_arow", [128, 64], F32, dma=True)
        drow = A.alloc("s_drow", [128, 32], F32, dma=True)
        fnw = A.alloc("s_fnw", [128, D], F32, dma=True)
        dt_all = A.alloc("s_dt", [128, NT, 64], F32)
        self.epsc = A.alloc("s_eps", [128, 1], F32)
        mk.op(mk.dve, lambda e: e.memset(self.epsc[:], EPS), writes=[self.epsc])
        mk.dma(mk.sp, gate[0][:], self.gates_d[2 * li:2 * li + 1, :].to_broadcast([128, D]), gate[0], reads=[self.GT], writes=[gate[0]])
        mk.dma(mk.sp, gate[1][:], self.gates_d[2 * li + 1:2 * li + 2, :].to_broadcast([128, D]), gate[1], reads=[self.GT], writes=[gate[1]])
        mk.dma(mk.sp, convw[:], self.s_convw[:, j], convw, reads=[W], writes=[convw])
        mk.dma(mk.sp, convb[:], self.s_convb[:, j], convb, reads=[W], writes=[convb])
        mk.dma(mk.sp, nwc[:], self.s_nw[:, j], nwc, reads=[W], writes=[nwc])
        mk.dma(mk.sp, dtb[:], self.s_dtb[j:j + 1, :].to_broadcast([128, 64]), dtb, reads=[W], writes=[dtb])
        mk.dma(mk.sp, arow[:], self.s_alog[j:j + 1, :].to_broadcast([128, 64]), arow, reads=[W], writes=[arow])
        mk.dma(mk.sp, drow[:], self.s_d[j:j + 1, :].to_broadcast([128, 32]), drow, reads=[W], writes=[drow])
        mk.dma(mk.sp, fnw[:], self.fnw[0:1, :].to_broadcast([128, D]), fnw, reads=[W], writes=[fnw])
        mk.op(mk.act, lambda e: e.activation(out=arow[:], in_=arow[:], func=AF.Exp), reads=[arow], writes=[arow])
        mk.op(mk.dve, lambda e: e.tensor_scalar(out=arow[:], in0=arow[:], scalar1=-1.0, scalar2=None, op0=ALU.mult), reads=[arow], writes=[arow])
        mark0 = A.off

        uT = A.alloc("s_uT", [128, 8, UCOLS], BF16)
        mark1 = A.off
        hinr = Rot([A.alloc(f"s_hin{i}", [128, D], F32, dma=True) for i in range(3)])
        hnr = Rot([A.alloc(f"s_hn{i}", [128, D], BF16) for i in range(2)])
        str_ = Rot([A.alloc(f"s_st{i}", [128, 4], F32) for i in range(4)])
        self.trb = Rot(self.banks[0:2])
        mmb = Rot(self.banks[2:5])
        for (a, b) in ((0, 2), (258, 262), (UCOLS - 2, UCOLS)):
            mk.op(mk.dve, lambda e: e.memset(uT[:, :, a:b], 0.0), writes=[uT])
        hin_of = {}
        nload = [0]

        def ensure_loaded(upto):
            while nload[0] <= min(upto, NT - 1):
                t = nload[0]
                b = hinr.next()
                src, srcT = self.h_src(li, t)
                mk.dma(mk.sp, b[:], src, b, reads=[srcT], writes=[b])
                hin_of[t] = b
                nload[0] += 1

        for t in range(NT):
            ensure_loaded(t + 2)
            c0 = ucol(t * 128)
            self.norm_tile(li, t, hin_of.pop(t), hnr.next(), str_.next(), uT, lambda k: uT[:, k, c0:c0 + 128])

        mk.barrier()
        A.reset(mark1)
        wccr = Rot([A.alloc(f"s_wcc{i}", [128, 1024], BF16, dma=True) for i in range(3)])
        prer = Rot([A.alloc(f"s_pre{i}", [128, UCOLS], F32) for i in range(2)])
        accr = Rot([A.alloc(f"s_acc{i}", [128, UCOLS - 4], F32) for i in range(2)])
        xor_ = Rot([A.alloc(f"s_xo{i}", [128, UCOLS - 4], BF16, dma=True) for i in range(2)])
        tstr = Rot([A.alloc(f"s_tst{i}", [128, 8, 128], BF16, dma=True) for i in range(3)])
        NX = UCOLS - 4
        colgroups = [(c, min(512, UCOLS - c)) for c in range(0, UCOLS, 512)]
        tilegroups = [list(range(a, min(a + 8, NT))) for a in range(0, NT, 8)]

        def xo_idx(tok):
            return tok if tok < TC else tok + 4

        def s0b_proj(cc):
            wcc = wccr.next()
            mk.dma(mk.pool, wcc[:], self.s_wxbc[j, cc], wcc, reads=[W], writes=[wcc], max_dma_last_dim=4096)
            pre = prer.next()
            for (c0, n) in colgroups:
                bk = mmb.next()
                self.mm_group(bk, bk[:, 0:n], [(wcc[:, k * 128:(k + 1) * 128], uT[:, k, c0:c0 + n]) for k in range(8)], [wcc, uT])
                mk.op(mk.act, lambda e: e.activation(out=pre[:, c0:c0 + n], in_=bk[:, 0:n], func=AF.Identity), reads=[bk], writes=[pre])
            return pre

        def s0b_conv(cc, pre):
            acc = accr.next()
            mk.op(mk.dve, lambda e: e.tensor_scalar(out=acc[:], in0=pre[:, 0:NX], scalar1=convw[:, cc, 0:1], scalar2=None, op0=ALU.mult),
                  reads=[pre, convw], writes=[acc])
            for k in range(1, 5):
                mk.op(mk.dve, lambda e: e.scalar_tensor_tensor(out=acc[:], in0=pre[:, k:k + NX], scalar=convw[:, cc, k:k + 1], in1=acc[:],
                                                               op0=ALU.mult, op1=ALU.add),
                      reads=[pre, convw, acc], writes=[acc])
            xo = xor_.next()
            mk.op(mk.act, lambda e: e.activation(out=xo[:], in_=acc[:], func=AF.Silu, bias=convb[:, cc:cc + 1]),
                  reads=[acc, convb], writes=[xo])
            return xo

        def s0b_out(cc, xo):
            if cc < 24:
                dst_d, dstT, coff = (self.xs_d, self.XS, cc * 128) if cc < 16 else (self.bt_d, self.BTK, (cc - 16) * 128)
                for tg in tilegroups:
                    nu = len(tg)
                    bk = self.trb.next()
                    bv = bview(bk[:]).rearrange("p (a b) -> p a b", a=8)
                    for u, t in enumerate(tg):
                        i0 = xo_idx(t * 128)
                        mk.op(mk.pe, lambda e: e.transpose(bv[:, u, :], xo[:, i0:i0 + 128], self.ident[:]),
                              reads=[xo, self.ident], writes=[bk], inc=(u == nu - 1))
                    ts_ = tstr.next()
                    mk.op(mk.act, lambda e: e.activation(out=ts_[:, 0:nu, :], in_=bv[:, 0:nu, :], func=AF.Identity), reads=[bk], writes=[ts_])
                    mk.dma(mk.sp, dst_d[tg[0] * 128:(tg[0] + nu) * 128, coff:coff + 128].rearrange("(u p) c -> p u c", p=128),
                           ts_[:, 0:nu, :], ts_, reads=[ts_], writes=[dstT[t] for t in tg])
            if cc >= 16:
                g = (cc - 16) % 8
                dd, ddT = (self.BT_d, self.BTT) if cc < 24 else (self.CT_d, self.CTT)
                mk.dma(mk.sp, [(dd[g, :, 0:TC], xo[:, 0:TC]), (dd[g, :, TC:TT], xo[:, TC + 4:TT + 4])], None, xo,
                       reads=[xo], writes=[ddT[g]])

        pre_cur = s0b_proj(0)
        for cc in range(32):
            pre_nxt = s0b_proj(cc + 1) if cc + 1 < 32 else None
            xo = s0b_conv(cc, pre_cur)
            s0b_out(cc, xo)
            pre_cur = pre_nxt

        mk.barrier()
        A.reset(mark1)
        wz = A.alloc("s_wz", [128, 8, 2112], BF16, dma=True)
        zstr = Rot([A.alloc(f"s_zst{i}", [128, 2048], BF16, dma=True) for i in range(2)])
        dtmr = Rot([A.alloc(f"s_dtm{i}", [128, 128], F32) for i in range(2)])
        mk.dma(mk.pool, [(wz[:, k, :], self.s_wzdt[j, :, k, :]) for k in range(8)], None, wz, reads=[W], writes=[wz], max_dma_last_dim=4096)
        for t in range(NT):
            c0 = ucol(t * 128)
            zst = zstr.next()
            for cg in range(4):
                bk = mmb.next()
                self.mm_group(bk, bk[:, :], [(uT[:, k, c0:c0 + 128], wz[:, k, cg * 512:(cg + 1) * 512]) for k in range(8)], [uT, wz])
                mk.op(mk.act, lambda e: e.activation(out=zst[:, cg * 512:(cg + 1) * 512], in_=bk[:, :], func=AF.Silu), reads=[bk], writes=[zst])
            mk.dma(mk.sp, self.z_d[t * 128:(t + 1) * 128, :], zst[:], zst, reads=[zst], writes=[self.ZT[t]])
            bk = mmb.next()
            self.mm_group(bk, bk[:, 0:64], [(uT[:, k, c0:c0 + 128], wz[:, k, 2048:2112]) for k in range(8)], [uT, wz])
            dtm = dtmr.next()
            mk.op(mk.dve, lambda e: e.tensor_tensor(out=dtm[:, 0:64], in0=bk[:, 0:64], in1=dtb[:], op=ALU.add), reads=[bk, dtb], writes=[dtm])
            mk.op(mk.act, lambda e: e.activation(out=dtm[:, 64:128], in_=dtm[:, 0:64], func=AF.Exp), reads=[dtm], writes=[dtm])
            mk.op(mk.act, lambda e: e.activation(out=dt_all[:, t, :], in_=dtm[:, 64:128], func=AF.Ln, bias=1.0), reads=[dtm], writes=[dt_all])

        mk.barrier()
        A.reset(mark0)
        xsr = Rot([A.alloc(f"s_xs{i}", [128, 2048], BF16, dma=True) for i in range(3)])
        btr = Rot([A.alloc(f"s_bt{i}", [128, 1024], BF16, dma=True) for i in range(3)])
        smr = Rot([A.alloc(f"s_sm{i}", [128, 256], F32) for i in range(3)])
        xwr = Rot([A.alloc(f"s_xw{i}", [128, 2048], BF16) for i in range(3)])
        hT = A.alloc("s_hT", [128, 2048], F32)
        hbr = Rot([A.alloc(f"s_hb{i}", [128, 2048], BF16, dma=True) for i in range(2)])
        stb = self.banks[4:8]
        mmb = Rot(self.banks[2:4])
        mark2 = A.off
        for d in range(2):
            order = list(range(NT)) if d == 0 else [1, 0] + list(range(NT - 1, 1, -1))
            mk.op(mk.dve, lambda e: e.memset(hT[:], 0.0), writes=[hT])

            def p1_front(t):
                xs_t = xsr.next()
                bt_t = btr.next()
                mk.dma(mk.sp, xs_t[:], self.xs_d[t * 128:(t + 1) * 128, :], xs_t, reads=[self.XS[t]], writes=[xs_t])
                mk.dma(mk.sp, bt_t[:], self.bt_d[t * 128:(t + 1) * 128, :], bt_t, reads=[self.BTK[t]], writes=[bt_t])
                sm = smr.next()
                dts = dt_all[:, t, d * 32:(d + 1) * 32]
                mk.op(mk.dve, lambda e: e.tensor_tensor(out=sm[:, 0:32], in0=dts, in1=arow[:, d * 32:(d + 1) * 32], op=ALU.mult),
                      reads=[dt_all, arow], writes=[sm])
                bk = mmb.next()
                self.mm_group(bk, bk[:, 0:32], [(self.tri[:, d, :], sm[:, 0:32])], [self.tri, sm])
                self.mm_group(bk, bk[:, 32:64], [(self.onesf[:], sm[:, 0:32])], [self.onesf, sm])
                mk.op(mk.act, lambda e: e.activation(out=sm[:, 32:96], in_=bk[:, 0:64], func=AF.Identity), reads=[bk], writes=[sm])
                mk.op(mk.dve, lambda e: e.tensor_tensor(out=sm[:, 96:128], in0=sm[:, 64:96], in1=sm[:, 32:64], op=ALU.subtract), reads=[sm], writes=[sm])
                mk.op(mk.act, lambda e: e.activation(out=sm[:, 96:128], in_=sm[:, 96:128], func=AF.Exp), reads=[sm], writes=[sm])
                mk.op(mk.act, lambda e: e.activation(out=sm[:, 160:192], in_=sm[:, 64:96], func=AF.Exp), reads=[sm], writes=[sm])
                mk.op(mk.dve, lambda e: e.tensor_tensor(out=sm[:, 128:160], in0=sm[:, 96:128], in1=dts, op=ALU.mult), reads=[sm, dt_all], writes=[sm])
                xw = xwr.next()
                mk.op(mk.dve, lambda e: e.tensor_tensor(out=xw[:].rearrange("p (h c) -> p h c", h=32),
                                                        in0=xs_t[:].rearrange("p (h c) -> p h c", h=32),
                                                        in1=sm[:, 128:160].unsqueeze(2).to_broadcast([128, 32, 64]), op=ALU.mult),
                      reads=[xs_t, sm], writes=[xw])
                return sm, bt_t, xw

            def p1_update(t, fr):
                sm, bt_t, xw = fr
                hb = hbr.next()
                mk.op(mk.act, lambda e: e.activation(out=hb[:], in_=hT[:], func=AF.Identity), reads=[hT], writes=[hb])
                mk.dma(mk.sp, self.hp_d[d, t], hb[:], hb, reads=[hb], writes=[self.HP[d][t]])
                for g in range(8):
                    sbk = stb[g // 2]
                    self.mm_group(sbk, sbk[:, (g % 2) * 256:(g % 2 + 1) * 256], [(bt_t[:, g * 128:(g + 1) * 128], xw[:, g * 256:(g + 1) * 256])], [bt_t, xw])
                hv = hT[:].rearrange("p (h c) -> p h c", h=32)
                mk.op(mk.dve, lambda e: e.tensor_tensor(out=hv, in0=hv, in1=sm[:, 160:192].unsqueeze(2).to_broadcast([128, 32, 64]), op=ALU.mult),
                      reads=[hT, sm], writes=[hT])
                for i in range(4):
                    mk.op(mk.dve, lambda e: e.tensor_tensor(out=hT[:, i * 512:(i + 1) * 512], in0=hT[:, i * 512:(i + 1) * 512], in1=stb[i][:, :], op=ALU.add),
                          reads=[hT, stb[i]], writes=[hT])

            fr = p1_front(order[0])
            for i, t in enumerate(order):
                fr_n = p1_front(order[i + 1]) if i + 1 < len(order) else None
                p1_update(t, fr)
                fr = fr_n

        mk.barrier()
        A.reset(mark0)
        wout = A.alloc("s_wout", [128, 16, D], BF16, dma=True)
        mk.dma(mk.pool, [(wout[:, k, :], self.s_wout[j, :, k, :]) for k in range(16)], None, wout, reads=[W], writes=[wout])
        xsr = Rot([A.alloc(f"p_xs{i}", [128, 2048], BF16, dma=True) for i in range(2)])
        BTr = Rot([A.alloc(f"p_BT{i}", [128, 8, 128], BF16, dma=True) for i in range(2)])
        CTr = Rot([A.alloc(f"p_CT{i}", [128, 8, 128], BF16, dma=True) for i in range(2)])
        zr = Rot([A.alloc(f"p_z{i}", [128, 2048], BF16, dma=True) for i in range(2)])
        hpfr = Rot([A.alloc(f"p_hpf{i}", [128, 2048], BF16, dma=True) for i in range(2)])
        hpbr = Rot([A.alloc(f"p_hpb{i}", [128, 2048], BF16, dma=True) for i in range(2)])
        hresr = Rot([A.alloc(f"p_hres{i}", [128, D], F32, dma=True) for i in range(3)])
        smr = Rot([A.alloc(f"p_sm{i}", [128, 384], F32) for i in range(2)])
        sel = A.alloc("p_sel", [128, 32, 128], BF16, dma=True)
        mk.dma(mk.pool, sel[0:64].rearrange("p a b -> p (a b)"), self.k_sel, sel, reads=[W], writes=[sel], max_dma_last_dim=4096)
        aThr = Rot([A.alloc(f"p_aTh{i}", [128, 128], BF16) for i in range(2)])
        aTlr = Rot([A.alloc(f"p_aTl{i}", [128, 128], BF16) for i in range(2)])
        Er = Rot([A.alloc(f"p_E{i}", [128, 4, 128], BF16) for i in range(2)])
        xpr = Rot([A.alloc(f"p_xp{i}", [128, 4, 128], F32) for i in range(2)])
        Mall = A.alloc("p_M", [128, 16, 4, 128], BF16)
        cbs = A.alloc("p_cbs", [128, 8, 128], BF16)
        dident = A.alloc("p_dident", [128, 32, 128], BF16)
        for h in range(32):
            mk.op(mk.dve, lambda e: e.tensor_scalar(out=dident[:, h, :], in0=self.ident[:], scalar1=drow[:, h:h + 1], scalar2=None, op0=ALU.mult),
                  reads=[self.ident, drow], writes=[dident])
        xdtr = Rot([[A.alloc(f"p_xdt{d}_{i}", [128, 2048], BF16) for d in range(2)] for i in range(2)])
        t1r = Rot([A.alloc(f"p_t1{i}", [128, 256], F32) for i in range(2)])
        t2r = Rot([A.alloc(f"p_t2{i}", [128, 256], F32) for i in range(2)])
        yall = A.alloc("p_yall", [128, 2048], F32)
        t3 = A.alloc("p_t3", [128, 2048], F32, dma=True)
        un = A.alloc("p_un", [128, 2048], BF16)
        ynT = A.alloc("p_ynT", [128, 16, 128], BF16)
        ytr = Rot([A.alloc(f"p_yt{i}", [128, 512], F32) for i in range(2)])
        mmb = Rot(self.banks[0:2])
        ebr = Rot(self.banks[2:4])
        ABr = Rot([(self.banks[4], self.banks[5]), (self.banks[6], self.banks[7])])
        print("pass2 arena bytes", A.off * 4, "of", A.cap * 4)
        tiles = list(range(NT)) if need_ctx else list(range(2, NT))

        def p2_load(t):
            xs_t, BTt, CTt, z_t, hpf, hpb, hres = xsr.next(), BTr.next(), CTr.next(), zr.next(), hpfr.next(), hpbr.next(), hresr.next()
            mk.dma(mk.sp, BTt[:], self.BT_d[:, :, t * 128:(t + 1) * 128].rearrange("g n t -> n g t"), BTt, reads=self.BTT, writes=[BTt])
            mk.dma(mk.sp, CTt[:], self.CT_d[:, :, t * 128:(t + 1) * 128].rearrange("g n t -> n g t"), CTt, reads=self.CTT, writes=[CTt])
            mk.dma(mk.sp, xs_t[:], self.xs_d[t * 128:(t + 1) * 128, :], xs_t, reads=[self.XS[t]], writes=[xs_t])
            mk.dma(mk.sp, hpf[:], self.hp_d[0, t], hpf, reads=[self.HP[0][t]], writes=[hpf])
            mk.dma(mk.sp, hpb[:], self.hp_d[1, t], hpb, reads=[self.HP[1][t]], writes=[hpb])
            mk.dma(mk.sp, z_t[:], self.z_d[t * 128:(t + 1) * 128, :], z_t, reads=[self.ZT[t]], writes=[z_t])
            src, srcT = self.h_src(li, t)
            mk.dma(mk.sp, hres[:], src, hres, reads=[srcT], writes=[hres])
            return xs_t, BTt, CTt, z_t, hpf, hpb, hres

        def p2_front(t, bufs):
            xs_t, BTt, CTt, z_t, hpf, hpb, hres = bufs
            sm = smr.next()
            mk.op(mk.dve, lambda e: e.tensor_tensor(out=sm[:, 0:64], in0=dt_all[:, t, :], in1=arow[:], op=ALU.mult), reads=[dt_all, arow], writes=[sm])
            bk = mmb.next()
            self.mm_group(bk, bk[:, 0:32], [(self.tri[:, 0, :], sm[:, 0:32])], [self.tri, sm])
            self.mm_group(bk, bk[:, 32:64], [(self.tri[:, 1, :], sm[:, 32:64])], [self.tri, sm])
            mk.op(mk.act, lambda e: e.activation(out=sm[:, 64:128], in_=bk[:, 0:64], func=AF.Exp), reads=[bk], writes=[sm])
            mk.op(mk.act, lambda e: e.activation(out=sm[:, 192:256], in_=bk[:, 0:64], func=AF.Identity, scale=-1.0), reads=[bk], writes=[sm])
            mk.op(mk.act, lambda e: e.activation(out=sm[:, 256:320], in_=bk[:, 0:64], func=AF.Identity), reads=[bk], writes=[sm])
            bkT = mmb.next()
            mk.op(mk.pe, lambda e: e.transpose(bkT[0:64, 0:128], sm[:, 256:320], self.identf[:]), reads=[sm, self.identf], writes=[bkT])
            aTh, aTl = aThr.next(), aTlr.next()
            mk.op(mk.dve, lambda e: e.tensor_copy(out=aTh[0:64, :], in_=bkT[0:64, 0:128]), reads=[bkT], writes=[aTh])
            mk.op(mk.dve, lambda e: e.tensor_tensor(out=aTl[0:64, :], in0=bkT[0:64, 0:128], in1=aTh[0:64, :], op=ALU.subtract), reads=[bkT, aTh], writes=[aTl])
            for hf in range(2):
                bk = mmb.next()
                for g4 in range(4):
                    g = hf * 4 + g4
                    self.mm_group(bk, bk[:, g4 * 128:(g4 + 1) * 128], [(BTt[:, g, :], CTt[:, g, :])], [BTt, CTt])
                mk.op(mk.act, lambda e: e.activation(out=cbs[:, hf * 4:(hf + 1) * 4, :].rearrange("p a b -> p (a b)"), in_=bk[:, :], func=AF.Identity),
                      reads=[bk], writes=[cbs])
            return sm, aTh, aTl

        def p2_front_main(t, bufs, pre, steps=()):
            steps = list(steps)
            xs_t, BTt, CTt, z_t, hpf, hpb, hres = bufs
            sm, aTh, aTl = pre
            xdtb = xdtr.next()
            for d in range(2):
                x_ = xdtb[d]
                mk.op(mk.dve, lambda e: e.tensor_tensor(out=x_[:].rearrange("p (h c) -> p h c", h=32),
                                                         in0=xs_t[:].rearrange("p (h c) -> p h c", h=32),
                                                         in1=dt_all[:, t, d * 32:(d + 1) * 32].unsqueeze(2).to_broadcast([128, 32, 64]), op=ALU.mult),
                      reads=[xs_t, dt_all], writes=[x_])
            for g in range(8):
                for d in range(2):
                    it = 2 * g + d
                    if steps and it >= 2 and (it % 2 == 0 or len(steps) > (16 - it) // 2 + 1):
                        steps.pop(0)()
                    eb = ebr.next()
                    msk = self.neghi if d == 0 else self.neglo
                    mk.op(mk.pe, lambda e: e.matmul(eb[:, :], self.ident[:], msk[:], start=True, stop=False),
                          reads=[self.ident, msk], writes=[eb], inc=False)
                    for r in range(4):
                        h = 4 * g + r
                        mk.op(mk.pe, lambda e: e.matmul(eb[:, r * 128:(r + 1) * 128], sel[32 * d:32 * d + 32, h, :], aTh[32 * d:32 * d + 32, :],
                                                        start=False, stop=False),
                              reads=[sel, aTh], writes=[eb], inc=False)
                        mk.op(mk.pe, lambda e: e.matmul(eb[:, r * 128:(r + 1) * 128], sel[32 * d:32 * d + 32, h, :], aTl[32 * d:32 * d + 32, :],
                                                        start=False, stop=(r == 3)),
                              reads=[sel, aTl], writes=[eb], inc=(r == 3))
                    xp = xpr.next()
                    mk.op(mk.dve, lambda e: e.tensor_tensor(out=xp[:], in0=eb[:, :].rearrange("p (a b) -> p a b", a=4),
                                                            in1=sm[:, 192 + 32 * d + 4 * g:192 + 32 * d + 4 * g + 4].unsqueeze(2).to_broadcast([128, 4, 128]),
                                                            op=ALU.add),
                          reads=[eb, sm], writes=[xp])
                    E = Er.next()
                    mk.op(mk.act, lambda e: e.activation(out=E[:].rearrange("p a b -> p (a b)"), in_=xp[:].rearrange("p a b -> p (a b)"), func=AF.Exp),
                          reads=[xp], writes=[E])
                    mk.op(mk.dve, lambda e: e.tensor_tensor(out=Mall[:, 2 * g + d], in0=E[:],
                                                            in1=cbs[:, g, :].unsqueeze(1).to_broadcast([128, 4, 128]), op=ALU.mult),
                          reads=[E, cbs], writes=[Mall])
            while steps:
                steps.pop(0)()
            return sm, xdtb

        def p2_ystage(t, bufs, smx):
            sm, xdtb = smx
            xs_t, BTt, CTt, z_t, hpf, hpb, hres = bufs
            pend = None

            def fin(g_, Ab_, t1_):
                mk.op(mk.dve, lambda e: e.tensor_tensor(out=yall[:, g_ * 256:(g_ + 1) * 256], in0=Ab_[:, 0:256], in1=t1_[:], op=ALU.add),
                      reads=[Ab_, t1_], writes=[yall])

            for g in range(8):
                Ab, Bb = ABr.next()
                self.mm_group(Ab, Ab[:, 256:512], [(CTt[:, g, :], hpf[:, g * 256:(g + 1) * 256])], [CTt, hpf])
                self.mm_group(Bb, Bb[:, 0:256], [(CTt[:, g, :], hpb[:, g * 256:(g + 1) * 256])], [CTt, hpb])
                for r in range(4):
                    h = 4 * g + r
                    self.mm_group(Ab, Ab[:, r * 64:(r + 1) * 64],
                                  [(Mall[:, 2 * g, r, :], xdtb[0][:, h * 64:(h + 1) * 64]), (Mall[:, 2 * g + 1, r, :], xdtb[1][:, h * 64:(h + 1) * 64]),
                                   (dident[:, h, :], xs_t[:, h * 64:(h + 1) * 64])],
                                  [Mall, xdtb[0], xdtb[1], dident, xs_t])
                t1, t2 = t1r.next(), t2r.next()
                mk.op(mk.dve, lambda e: e.tensor_tensor(out=t1[:].rearrange("p (a b) -> p a b", a=4), in0=Ab[:, 256:512].rearrange("p (a b) -> p a b", a=4),
                                                        in1=sm[:, 64 + 4 * g:64 + 4 * g + 4].unsqueeze(2).to_broadcast([128, 4, 64]), op=ALU.mult),
                      reads=[Ab, sm], writes=[t1])
                mk.op(mk.dve, lambda e: e.tensor_tensor(out=t2[:].rearrange("p (a b) -> p a b", a=4), in0=Bb[:, 0:256].rearrange("p (a b) -> p a b", a=4),
                                                        in1=sm[:, 96 + 4 * g:96 + 4 * g + 4].unsqueeze(2).to_broadcast([128, 4, 64]), op=ALU.mult),
                      reads=[Bb, sm], writes=[t2])
                mk.op(mk.dve, lambda e: e.tensor_tensor(out=t1[:], in0=t1[:], in1=t2[:], op=ALU.add), reads=[t1, t2], writes=[t1])
                fin(g, Ab, t1)

        def p2_post_steps(t, bufs, smx):
            sm = smx[0]
            xs_t, BTt, CTt, z_t, hpf, hpb, hres = bufs
            cond = 1 if t < 2 else 0
            steps = []

            def s_gate():
                mk.op(mk.dve, lambda e: e.tensor_tensor(out=t3[:], in0=yall[:], in1=z_t[:], op=ALU.mult), reads=[yall, z_t], writes=[t3])
            steps.append(s_gate)

            def s_sq():
                mk.op(mk.act, lambda e: e.activation(out=yall[:], in_=t3[:], func=AF.Square), reads=[t3], writes=[yall])
            steps.append(s_sq)

            def s_red():
                mk.op(mk.dve, lambda e: e.tensor_reduce(out=sm[:, 128:136], in_=yall[:].rearrange("p (a b) -> p a b", a=8), axis=AX.X, op=ALU.add),
                      reads=[yall], writes=[sm])
            steps.append(s_red)

            def s_sqrt():
                mk.op(mk.act, lambda e: e.activation(out=sm[:, 136:144], in_=sm[:, 128:136], func=AF.Ln, scale=1.0 / 256, bias=self.epsc[:, 0:1]),
                      reads=[sm, self.epsc], writes=[sm])
            steps.append(s_sqrt)

            def s_un():
                mk.op(mk.act, lambda e: e.activation(out=sm[:, 144:152], in_=sm[:, 136:144], func=AF.Exp, scale=-0.5), reads=[sm], writes=[sm])
                mk.op(mk.dve, lambda e: e.tensor_tensor(out=un[:].rearrange("p (a b) -> p a b", a=8), in0=t3[:].rearrange("p (a b) -> p a b", a=8),
                                                        in1=sm[:, 144:152].unsqueeze(2).to_broadcast([128, 8, 256]), op=ALU.mult),
                      reads=[t3, sm], writes=[un])
            steps.append(s_un)

            def mk_tr(hb_):
                def f():
                    bk = mmb.next()
                    bv = bview(bk[:]).rearrange("p (a b) -> p a b", a=8)
                    for k in range(8):
                        kk = hb_ * 8 + k
                        mk.op(mk.pe, lambda e: e.transpose(bv[:, k, :], un[:, kk * 128:(kk + 1) * 128], self.ident[:]),
                              reads=[un, self.ident], writes=[bk], inc=(k == 7))
                    mk.op(mk.dve, lambda e: e.tensor_tensor(out=ynT[:, hb_ * 8:(hb_ + 1) * 8, :], in0=bv,
                                                            in1=nwc[:, hb_ * 8:(hb_ + 1) * 8].unsqueeze(2).to_broadcast([128, 8, 128]), op=ALU.mult),
                          reads=[bk, nwc], writes=[ynT])
                return f
            steps.append(mk_tr(0))
            steps.append(mk_tr(1))

            def mk_out(cg):
                def f():
                    yb = mmb.next()
                    self.mm_group(yb, yb[:, :], [(ynT[:, k, :], wout[:, k, cg * 512:(cg + 1) * 512]) for k in range(16)], [ynT, wout])
                    yt = ytr.next()
                    mk.op(mk.dve, lambda e: e.tensor_tensor(out=yt[:], in0=yb[:, :], in1=gate[cond][:, cg * 512:(cg + 1) * 512], op=ALU.mult),
                          reads=[yb, gate[cond]], writes=[yt])
                    mk.op(mk.dve, lambda e: e.tensor_tensor(out=hres[:, cg * 512:(cg + 1) * 512], in0=hres[:, cg * 512:(cg + 1) * 512], in1=yt[:], op=ALU.add),
                          reads=[yt, hres], writes=[hres])
                return f
            steps.append(mk_out(0))
            steps.append(mk_out(1))

            def s_store():
                if (not last) or self.dump_h:
                    mk.dma(mk.sp, self.h_d[t * 128:(t + 1) * 128, :], hres[:], hres, reads=[hres], writes=[self.HT[t]])
                if last and not self.dump_h:
                    ost = t3
                    mk.op(mk.act, lambda e: e.activation(out=ost[:, 0:D], in_=hres[:], func=AF.Square, accum_out=sm[:, 160:161]), reads=[hres], writes=[ost, sm])
                    mk.op(mk.act, lambda e: e.activation(out=sm[:, 161:162], in_=sm[:, 160:161], func=AF.Sqrt, scale=1.0 / D, bias=self.epsc[:, 0:1]),
                          reads=[sm, self.epsc], writes=[sm])
                    mk.op(mk.dve, lambda e: e.reciprocal(sm[:, 162:163], sm[:, 161:162]), reads=[sm], writes=[sm])
                    mk.op(mk.dve, lambda e: e.tensor_scalar(out=ost[:, 0:D], in0=hres[:], scalar1=sm[:, 162:163], scalar2=None, op0=ALU.mult),
                          reads=[hres, sm], writes=[ost])
                    mk.op(mk.dve, lambda e: e.tensor_tensor(out=ost[:, 0:D], in0=ost[:, 0:D], in1=fnw[:], op=ALU.mult), reads=[ost, fnw], writes=[ost])
                    mk.dma(mk.sp, self.out_d[(t - 2) * 128:(t - 1) * 128, :], ost[:, 0:D], ost, reads=[ost], writes=[self.OUT[t]])
            steps.append(s_store)
            return steps

        cur = p2_load(tiles[0])
        sm_cur = p2_front_main(tiles[0], cur, p2_front(tiles[0], cur))
        for i, t in enumerate(tiles):
            nxt = p2_load(tiles[i + 1]) if i + 1 < len(tiles) else None
            p2_ystage(t, cur, sm_cur)
            steps = p2_post_steps(t, cur, sm_cur)
            if nxt is not None:
                pre = p2_front(tiles[i + 1], nxt)
                sm_nxt = p2_front_main(tiles[i + 1], nxt, pre, steps)
            else:
                sm_nxt = None
                for f in steps:
                    f()
            cur, sm_cur = nxt, sm_nxt


def _consts():
    f32 = np.float32
    k = {}
    k["k_ident"] = np.eye(128, dtype=f32)
    prot = np.zeros((128, 128), f32)
    for base in range(0, 128, 32):
        for d in range(16):
            prot[base + d + 16, base + d] = -1.0
            prot[base + d, base + d + 16] = 1.0
    k["k_prot"] = prot
    c = np.arange(128)[:, None]
    i = np.arange(128)[None, :]
    lo = np.where(c >= i, 0.0, NEG).astype(f32)
    hi = np.where(c <= i, 0.0, NEG).astype(f32)
    k["k_neglo"] = np.tile(lo, (1, 4))
    k["k_neghi"] = np.tile(hi, (1, 4))
    half = 32
    inv_freq = (10000.0 ** (-np.arange(0, half, 2, dtype=f32) / f32(half))).astype(f32)
    pos = np.arange(TL)
    row = (pos // 64).astype(f32)
    col = (pos % 64).astype(f32)
    cosT = np.ones((128, TT), f32)
    sinT = np.zeros((128, TT), f32)
    for d in range(128):
        dl = d % 64
        p = row if dl < 32 else col
        f = (dl % 32) % 16
        ang = (p * inv_freq[f]).astype(f32)
        cosT[d, TC:] = np.cos(ang).astype(f32)
        sinT[d, TC:] = np.sin(ang).astype(f32)
    k["k_cos"] = cosT
    k["k_sin"] = sinT
    t = np.arange(128)[:, None]
    s_ = np.arange(128)[None, :]
    tri = np.stack([(t <= s_), (t >= s_), (t > s_), (t < s_)], axis=1).astype(f32)
    k["k_tri"] = np.ascontiguousarray(tri)
    sel = np.zeros((64, 32, 128), f32)
    for kk in range(64):
        sel[kk, kk % 32, :] = 1.0
    k["k_sel"] = sel.reshape(64, 32 * 128)
    return k


def _pk(w):
    K, N = w.shape
    return np.ascontiguousarray(w.reshape(K // 128, 128, N).transpose(1, 0, 2))


def prep_shared(inputs):
    f32 = np.float32
    g = {}
    g["w_ada"] = np.stack([_pk(inputs["w_ada"][l]) for l in range(4)])
    b = inputs["b_ada"]
    g["b_col"] = np.ascontiguousarray(b[:, :2048].reshape(4, 16, 128).transpose(2, 0, 1))
    g["b_gate"] = np.ascontiguousarray(b[:, 2048:])
    qorder = []
    for cidx in range(8):
        a = cidx if cidx < 4 else 8 + (cidx - 4)
        bb = 4 + cidx if cidx < 4 else 12 + (cidx - 4)
        qorder += list(range(a * 64, a * 64 + 64)) + list(range(bb * 64, bb * 64 + 64))
    cols = np.array(qorder + list(range(1024, 2560)))
    g["attn_w_in"] = np.stack([_pk(inputs["attn_w_in"][j][:, cols]) for j in range(2)])
    g["attn_w_out"] = np.stack([_pk(inputs["attn_w_out"][j]) for j in range(2)])
    g["attn_sink"] = np.ascontiguousarray(inputs["attn_sink"])
    w = inputs["ssd_w_in"]
    xbc = w[:, :, 2048:2048 + 4096]
    g["ssd_w_xbc"] = np.ascontiguousarray(
        xbc.reshape(2, 8, 128, 32, 128).transpose(0, 3, 2, 1, 4).reshape(2, 32, 128, 1024))
    zdt = np.concatenate([w[:, :, :2048], w[:, :, 2048 + 4096:]], axis=2)
    g["ssd_w_zdt"] = np.stack([_pk(zdt[j]) for j in range(2)])
    g["ssd_conv_w"] = np.ascontiguousarray(inputs["ssd_conv_w"].reshape(2, 5, 32, 128).transpose(3, 0, 2, 1))
    g["ssd_conv_b"] = np.ascontiguousarray(inputs["ssd_conv_b"].reshape(2, 32, 128).transpose(2, 0, 1))
    g["ssd_dt_bias"] = np.ascontiguousarray(inputs["ssd_dt_bias"].reshape(2, 64))
    g["ssd_a_log"] = np.ascontiguousarray(inputs["ssd_a_log"].reshape(2, 64))
    g["ssd_d"] = np.ascontiguousarray(inputs["ssd_d"])
    g["ssd_norm_w"] = np.ascontiguousarray(inputs["ssd_norm_w"].reshape(2, 16, 128).transpose(2, 0, 1))
    g["ssd_w_out"] = np.stack([_pk(inputs["ssd_w_out"][j]) for j in range(2)])
    g["final_norm_w"] = np.ascontiguousarray(inputs["final_norm_w"].reshape(1, 1024))
    g.update(_consts())
    return {k_: np.ascontiguousarray(v, dtype=f32) for k_, v in g.items()}


def prep_core(inputs, shared, b):
    m = dict(shared)
    m["x"] = np.ascontiguousarray(inputs["x"][b], dtype=np.float32)
    m["ctx"] = np.ascontiguousarray(inputs["ctx"][b], dtype=np.float32)
    cc = np.stack([inputs["c"][b].reshape(8, 128).T, inputs["c_ctx"].reshape(8, 128).T], axis=2)
    m["c_col2"] = np.ascontiguousarray(cc, dtype=np.float32)
    return m


_NC_CACHE = {}


def kernel(**inputs):
    inputs = {k_: np.asarray(v) for k_, v in inputs.items()}
    if "nc" not in _NC_CACHE:
        _NC_CACHE["nc"] = Prog(n_layers=4).build()
    nc = _NC_CACHE["nc"]
    shared = prep_shared(inputs)
    in_maps = [prep_core(inputs, shared, b) for b in range(8)]
    res = run_bass_kernel_spmd(nc, in_maps, core_ids=list(range(8)))
    return np.stack([r["out"] for r in res.results], axis=0).astype(np.float32)
```

```python
import math
from contextlib import ExitStack

import numpy as np
import concourse.bass as bass
import concourse.mybir as mybir
from concourse.bass_utils import run_bass_kernel_spmd

F32 = mybir.dt.float32
BF16 = mybir.dt.bfloat16
AF = mybir.ActivationFunctionType
ALU = mybir.AluOpType
AX = mybir.AxisListType

SAME_ENGINE_WAITS = True
DBG = {}


class Q:
    def __init__(self, mk, name, eng, is_pe=False):
        self.mk = mk
        self.name = name
        self.eng = eng
        self.sem = mk.new_sem("q_" + name)
        self.count = 0
        self.seen = {}
        self.is_pe = is_pe

    def wait_tokens(self, toks):
        best = {}
        for tok in toks:
            if tok is None:
                continue
            sem, val, q = tok
            if q is self and (self.is_pe or not SAME_ENGINE_WAITS):
                continue
            k = id(sem)
            if self.seen.get(k, 0) >= val:
                continue
            if k not in best or best[k][1] < val:
                best[k] = (sem, val)
        for k, (sem, val) in best.items():
            self.eng.wait_ge(sem, val)
            self.seen[k] = val


class T:
    def __init__(self, name, ap=None, dsem=None):
        self.name = name
        self.ap = ap
        self.w = []
        self.r = []
        self.dsem = dsem
        self.dcount = 0
        self.psum = False

    def __getitem__(self, k):
        return self.ap[k]


class MK:
    def __init__(self, nc, es):
        self.nc = nc
        self.es = es
        self.nsem = 0
        self.pe = Q(self, "pe", nc.tensor, is_pe=True)
        self.act = Q(self, "act", nc.scalar)
        self.dve = Q(self, "dve", nc.vector)
        self.pool = Q(self, "pool", nc.gpsimd)
        self.sp = Q(self, "sp", nc.sync)
        self.n_inst = 0
        self.dts = []

    def new_sem(self, name):
        self.nsem += 1
        return self.es.enter_context(self.nc.semaphore(f"{name}_{self.nsem}"))

    def sbuf(self, name, shape, dtype, dma=False):
        h = self.es.enter_context(self.nc.sbuf_tensor(name, list(shape), dtype))
        t = T(name, h, self.new_sem("d_" + name) if dma else None)
        if dma:
            self.dts.append(t)
        return t

    def psum(self, name, shape, dtype=F32):
        h = self.es.enter_context(self.nc.psum_tensor(name, list(shape), dtype))
        t = T(name, h)
        t.psum = True
        return t

    def dram(self, name, shape, dtype, kind="Internal"):
        h = self.nc.dram_tensor(name, list(shape), dtype, kind=kind)
        return T(name, h.ap() if hasattr(h, "ap") else h)

    def op(self, q, fn, reads=(), writes=(), inc=True):
        toks = []
        for t in reads:
            toks += t.w
            if t.psum:
                toks += [x for x in t.r if x[2] is not q]
        for t in writes:
            toks += t.w
            toks += t.r
        q.wait_tokens(toks)
        ins = fn(q.eng)
        self.n_inst += 1
        if not inc:
            tok = (q.sem, q.count + 1, q)
        else:
            q.count += 1
            ins.then_inc(q.sem, 1)
            tok = (q.sem, q.count, q)
        for t in writes:
            t.w = [tok]
            t.r = []
        for t in reads:
            if t not in writes:
                t.r.append(tok)
                if len(t.r) > 24:
                    t.r = _prune(t.r)
        return ins

    def dma(self, q, out, in_, side, reads=(), writes=(), n_parts=None, **kw):
        pairs = out if isinstance(out, list) else [(out, in_)]
        toks = []
        for t in reads:
            toks += t.w
        for t in writes:
            toks += t.w
            toks += t.r
        if side.dcount:
            toks.append((side.dsem, side.dcount, None))
        q.wait_tokens(toks)
        for (o, i) in pairs:
            q.eng.dma_start(out=o, in_=i, **kw).then_inc(side.dsem, 16)
            side.dcount += 16
            self.n_inst += 1
        tok = (side.dsem, side.dcount, None)
        for t in writes:
            t.w = [tok]
            t.r = []
        for t in reads:
            if t not in writes:
                t.r.append(tok)
                if len(t.r) > 24:
                    t.r = _prune(t.r)

    def barrier(self):
        toks = [(p.sem, p.count, None) for p in self.queues() if p.count]
        toks += [(t.dsem, t.dcount, None) for t in self.dts if t.dcount]
        for q in self.queues():
            q.wait_tokens(toks)

    def queues(self):
        return [self.pe, self.act, self.dve, self.pool, self.sp]

    def finish(self, outs):
        toks = []
        for t in outs:
            toks += t.w
        self.sp.wait_tokens(toks)


def _prune(toks):
    best = {}
    for (sem, val, q) in toks:
        k = id(sem)
        if k not in best or best[k][1] < val:
            best[k] = (sem, val, q)
    return list(best.values())


D = 1024
KD = 8
TC = 256
TL = 4096
TT = TC + TL
NT = TT // 128
EPS = 1e-6
NEG = -30000.0
GROUPS = [(0, 256)] + [(256 + 512 * i, 512) for i in range(8)]
UCOLS = TT + 8


def ucol(tok):
    return tok + (2 if tok < TC else 6)


class Arena:
    def __init__(self, mk, nbytes):
        self.mk = mk
        self.h = mk.es.enter_context(mk.nc.sbuf_tensor("arena", [128, nbytes // 4], F32))
        self.cap = nbytes // 4
        self.off = 0
        self.cache = {}

    def reset(self, off=0):
        self.off = off

    def alloc(self, key, shape, dtype, dma=False):
        nel = 1
        for s_ in shape[1:]:
            nel *= s_
        nb = nel * (4 if dtype == F32 else 2)
        n4 = (nb + 3) // 4
        n4 = (n4 + 7) // 8 * 8
        assert self.off + n4 <= self.cap, f"arena overflow at {key}: {self.off + n4} > {self.cap}"
        ck = (key, self.off, tuple(shape), str(dtype))
        off = self.off
        self.off += n4
        if ck in self.cache:
            return self.cache[ck]
        ap = self.h[:, off:off + n4]
        if dtype != F32:
            ap = ap.bitcast(dtype)
        ap = ap[:, 0:nel]
        if len(shape) == 3:
            ap = ap.rearrange("p (a b) -> p a b", a=shape[1])
        elif len(shape) == 4:
            ap = ap.rearrange("p (a b c) -> p a b c", a=shape[1], b=shape[2])
        t = T(key, ap, self.mk.new_sem("d_" + key) if dma else None)
        if dma:
            self.mk.dts.append(t)
        self.cache[ck] = t
        return t


class Rot:
    def __init__(self, items):
        self.items = items
        self.i = 0

    def next(self):
        t = self.items[self.i % len(self.items)]
        self.i += 1
        return t


def bview(bank_ap):
    return bank_ap.bitcast(BF16)


class Prog:
    def __init__(self, n_layers=4, dump_h=False):
        self.n_layers = n_layers
        self.dump_h = dump_h
        self.nc = bass.Bass("TRN2", target_bir_lowering=False)
        self.es = ExitStack()

    def inp(self, name, shape):
        return self.nc.dram_tensor(name, list(shape), F32, kind="ExternalInput").ap()

    def build(self):
        nc = self.nc
        with self.es:
            mk = self.mk = MK(nc, self.es)
            self.x_d = self.inp("x", [TL, D])
            self.ctx_d = self.inp("ctx", [TC, D])
            self.c_col2 = self.inp("c_col2", [128, 8, 2])
            self.w_ada = self.inp("w_ada", [4, 128, 8, 3072])
            self.b_col = self.inp("b_col", [128, 4, 16])
            self.b_gate = self.inp("b_gate", [4, 1024])
            self.a_win = self.inp("attn_w_in", [2, 128, 8, 2560])
            self.a_wout = self.inp("attn_w_out", [2, 128, 8, 1024])
            self.a_sink = self.inp("attn_sink", [2, 16])
            self.s_wxbc = self.inp("ssd_w_xbc", [2, 32, 128, 1024])
            self.s_wzdt = self.inp("ssd_w_zdt", [2, 128, 8, 2112])
            self.s_convw = self.inp("ssd_conv_w", [128, 2, 32, 5])
            self.s_convb = self.inp("ssd_conv_b", [128, 2, 32])
            self.s_dtb = self.inp("ssd_dt_bias", [2, 64])
            self.s_alog = self.inp("ssd_a_log", [2, 64])
            self.s_d = self.inp("ssd_d", [2, 32])
            self.s_nw = self.inp("ssd_norm_w", [128, 2, 16])
            self.s_wout = self.inp("ssd_w_out", [2, 128, 16, 1024])
            self.fnw = self.inp("final_norm_w", [1, 1024])
            self.k_ident = self.inp("k_ident", [128, 128])
            self.k_prot = self.inp("k_prot", [128, 128])
            self.k_neglo = self.inp("k_neglo", [128, 512])
            self.k_neghi = self.inp("k_neghi", [128, 512])
            self.k_cos = self.inp("k_cos", [128, TT])
            self.k_sin = self.inp("k_sin", [128, TT])
            self.k_tri = self.inp("k_tri", [128, 4, 128])
            self.k_sel = self.inp("k_sel", [64, 32 * 128])
            if self.dump_h:
                self.out_d = nc.dram_tensor("out", [TT, D], F32, kind="ExternalOutput").ap()
            else:
                self.out_d = nc.dram_tensor("out", [TL, D], F32, kind="ExternalOutput").ap()
            self.h_d = nc.dram_tensor("h_scr", [TT, D], F32).ap()
            self.gates_d = nc.dram_tensor("gates_scr", [8, D], F32).ap()
            self.xs_d = nc.dram_tensor("xs_scr", [TT, 2048], BF16).ap()
            self.bt_d = nc.dram_tensor("btok_scr", [TT, 1024], BF16).ap()
            self.BT_d = nc.dram_tensor("BT_scr", [8, 128, TT], BF16).ap()
            self.CT_d = nc.dram_tensor("CT_scr", [8, 128, TT], BF16).ap()
            self.z_d = nc.dram_tensor("z_scr", [TT, 2048], BF16).ap()
            self.hp_d = nc.dram_tensor("hprev_scr", [2, NT, 128, 2048], BF16).ap()
            self.HT = [T(f"h_tile{t}") for t in range(NT)]
            self.GT = T("gates")
            self.XS = [T(f"xs{t}") for t in range(NT)]
            self.BTK = [T(f"btk{t}") for t in range(NT)]
            self.BTT = T("BT")
            self.CTT = T("CT")
            self.ZT = [T(f"z{t}") for t in range(NT)]
            self.HP = [[T(f"hp{d}_{t}") for t in range(NT)] for d in range(2)]
            self.OUT = [T(f"out{t}") for t in range(NT)]
            self.BTT = [T(f"BT{g}") for g in range(8)]
            self.CTT = [T(f"CT{g}") for g in range(8)]
            self.XIN = T("xin")
            self.ident = mk.sbuf("ident", [128, 128], BF16, dma=True)
            self.prot = mk.sbuf("prot", [128, 128], BF16, dma=True)
            self.neglo = mk.sbuf("neglo", [128, 512], BF16, dma=True)
            self.neghi = mk.sbuf("neghi", [128, 512], BF16, dma=True)
            self.tri = mk.sbuf("tri", [128, 4, 128], F32, dma=True)
            self.onesf = mk.sbuf("onesf", [128, 128], F32)
            self.identf = mk.sbuf("identf", [128, 128], F32, dma=True)
            self.modcol = mk.sbuf("modcol", [128, 4, 16, 2], F32)
            self.banks = [mk.psum(f"bank{i}", [128, 512], F32) for i in range(8)]
            self.arena = Arena(mk, 200 * 1024)
            K = T("consts")
            mk.dma(mk.pool, self.ident[:], self.k_ident, self.ident, reads=[K], writes=[self.ident])
            mk.dma(mk.pool, self.prot[:], self.k_prot, self.prot, reads=[K], writes=[self.prot])
            mk.dma(mk.pool, self.neglo[:], self.k_neglo, self.neglo, reads=[K], writes=[self.neglo])
            mk.dma(mk.pool, self.neghi[:], self.k_neghi, self.neghi, reads=[K], writes=[self.neghi])
            mk.dma(mk.sp, self.tri[:], self.k_tri, self.tri, reads=[K], writes=[self.tri])
            mk.dma(mk.sp, self.identf[:], self.k_ident, self.identf, reads=[K], writes=[self.identf])
            mk.op(mk.dve, lambda e: e.memset(self.onesf[:], 1.0), writes=[self.onesf])

            self.prologue()
            for li in range(self.n_layers):
                mk.barrier()
                if li % 2 == 0:
                    self.attn_layer(li)
                else:
                    self.ssd_layer(li)
            if self.dump_h:
                self.dump()
            mk.barrier()
        return nc

    def mm_group(self, bank_t, out_ap, pairs, reads, start=True, stop=True):
        mk = self.mk
        n = len(pairs)
        for i, (l, r) in enumerate(pairs):
            mk.op(mk.pe, lambda e, l=l, r=r, i=i: e.matmul(out_ap, l, r, start=(start and i == 0), stop=(stop and i == n - 1)),
                  reads=reads, writes=[bank_t], inc=(i == n - 1))

    def h_src(self, li, t):
        if li == 0:
            return (self.ctx_d[t * 128:(t + 1) * 128, :] if t < 2 else self.x_d[(t - 2) * 128:(t - 1) * 128, :]), self.XIN
        return self.h_d[t * 128:(t + 1) * 128, :], self.HT[t]

    def prologue(self):
        mk, A = self.mk, self.arena
        A.reset()
        wadar = Rot([A.alloc(f"wada{i}", [128, 8, 3072], BF16, dma=True) for i in range(2)])
        ccol = A.alloc("ccol", [128, 8, 2], F32, dma=True)
        sc2 = A.alloc("sc2", [128, 8, 2], BF16)
        bcol = A.alloc("bcol", [128, 4, 16], F32, dma=True)
        bg = A.alloc("bg", [128, 1024], F32, dma=True)
        grow = A.alloc("grow", [128, 1024], F32, dma=True)
        K = T("pin")
        mk.dma(mk.sp, ccol[:], self.c_col2, ccol, reads=[K], writes=[ccol])
        mk.dma(mk.sp, bcol[:], self.b_col, bcol, reads=[K], writes=[bcol])
        mk.op(mk.act, lambda e: e.activation(out=sc2[:], in_=ccol[:], func=AF.Silu), reads=[ccol], writes=[sc2])
        wadas = []
        for l in range(4):
            wada = wadar.next()
            if l < 2:
                mk.dma(mk.pool, [(wada[:, k, :], self.w_ada[l, :, k, :]) for k in range(8)], None, wada, reads=[K], writes=[wada], max_dma_last_dim=4096)
            wadas.append(wada)
        for l in range(4):
            wada = wadas[l]
            mk.dma(mk.sp, bg[0:2, :], self.b_gate[l:l + 1, :].to_broadcast([2, 1024]), bg, reads=[K], writes=[bg])
            b0 = self.banks[0]
            pscol = b0[:, 0:32].rearrange("p (f c) -> p f c", c=2)
            for f in range(16):
                self.mm_group(b0, pscol[:, f, :], [(wada[:, k, f * 128:(f + 1) * 128], sc2[:, k, :]) for k in range(8)], [wada, sc2])
            mk.op(mk.dve, lambda e: e.tensor_tensor(out=self.modcol[:, l], in0=pscol,
                                                    in1=bcol[:, l, :].unsqueeze(2).to_broadcast([128, 16, 2]), op=ALU.add),
                  reads=[b0, bcol], writes=[self.modcol])
            mk.op(mk.dve, lambda e: e.tensor_scalar(out=self.modcol[:, l, 8:16, :], in0=self.modcol[:, l, 8:16, :],
                                                    scalar1=1.0, scalar2=None, op0=ALU.add),
                  reads=[self.modcol], writes=[self.modcol])
            for cg in range(2):
                bk = self.banks[1 + cg]
                self.mm_group(bk, bk[0:2, :], [(sc2[:, k, :], wada[:, k, 2048 + cg * 512:2048 + (cg + 1) * 512]) for k in range(8)], [wada, sc2])
                mk.op(mk.dve, lambda e, bk=bk, cg=cg: e.tensor_tensor(out=grow[0:2, cg * 512:(cg + 1) * 512], in0=bk[0:2, :],
                                                                      in1=bg[0:2, cg * 512:(cg + 1) * 512], op=ALU.add),
                      reads=[bk, bg], writes=[grow])
            mk.dma(mk.sp, self.gates_d[2 * l:2 * l + 2, :], grow[0:2, :], grow, reads=[grow], writes=[self.GT])
            if l + 2 < 4:
                mk.dma(mk.pool, [(wada[:, k, :], self.w_ada[l + 2, :, k, :]) for k in range(8)], None, wada, reads=[K], writes=[wada], max_dma_last_dim=4096)

    def dump(self):
        mk, A = self.mk, self.arena
        mk.barrier()
        A.reset()
        bufs = Rot([A.alloc(f"dmp{i}", [128, D], F32, dma=True) for i in range(4)])
        for t in range(NT):
            b = bufs.next()
            mk.dma(mk.sp, b[:], self.h_d[t * 128:(t + 1) * 128, :], b, reads=[self.HT[t]], writes=[b])
            mk.dma(mk.sp, self.out_d[t * 128:(t + 1) * 128, :], b[:], b, reads=[b], writes=[self.OUT[t]])

    def norm_tile(self, li, t, hin, hn, st, dst_T, dst_fn):
        mk = self.mk
        cond = 1 if t < 2 else 0
        mk.op(mk.act, lambda e: e.activation(out=hn[:], in_=hin[:], func=AF.Square, accum_out=st[:, 0:1]),
              reads=[hin], writes=[hn, st])
        mk.op(mk.act, lambda e: e.activation(out=st[:, 1:2], in_=st[:, 0:1], func=AF.Ln, scale=1.0 / D, bias=self.epsc[:, 0:1]),
              reads=[st, self.epsc], writes=[st])
        mk.op(mk.act, lambda e: e.activation(out=st[:, 2:3], in_=st[:, 1:2], func=AF.Exp, scale=-0.5), reads=[st], writes=[st])
        mk.op(mk.dve, lambda e: e.tensor_scalar(out=hn[:], in0=hin[:], scalar1=st[:, 2:3], scalar2=None, op0=ALU.mult),
              reads=[hin, st], writes=[hn])
        if DBG.get("nstop", 9) < 2:
            return
        bk = self.trb.next()
        bv = bview(bk[:]).rearrange("p (a b) -> p a b", a=8)
        for k in range(8):
            mk.op(mk.pe, lambda e, k=k: e.transpose(bv[:, k, :], hn[:, k * 128:(k + 1) * 128], self.ident[:]),
                  reads=[hn, self.ident], writes=[bk], inc=(k == 7))
        if DBG.get("nstop", 9) < 3:
            return
        for k in range(8):
            mk.op(mk.dve, lambda e, k=k: e.tensor_scalar(out=dst_fn(k), in0=bv[:, k, :],
                                                         scalar1=self.modcol[:, li, 8 + k, cond:cond + 1],
                                                         scalar2=self.modcol[:, li, k, cond:cond + 1],
                                                         op0=ALU.mult, op1=ALU.add),
                  reads=[bk, self.modcol], writes=[dst_T])

    def attn_layer(self, li):
        mk, A = self.mk, self.arena
        j = li // 2
        A.reset()
        win = A.alloc("a_win", [128, 8, 2560], BF16, dma=True)
        wout = A.alloc("a_wout", [128, 8, 1024], BF16, dma=True)
        gate = [A.alloc(f"a_gate{i}", [128, D], F32, dma=True) for i in range(2)]
        esink = A.alloc("a_esink", [128, 16], F32, dma=True)
        kT = A.alloc("a_kT", [128, 2, TT], BF16)
        vaug = A.alloc("a_vaug", [128, NT, 4, 65], BF16)
        hinr = Rot([A.alloc(f"a_hin{i}", [128, D], F32, dma=True) for i in range(3)])
        hresr = Rot([A.alloc(f"a_hres{i}", [128, D], F32, dma=True) for i in range(2)])
        hnr = Rot([A.alloc(f"a_hn{i}", [128, D], BF16) for i in range(2)])
        str_ = Rot([A.alloc(f"a_st{i}", [128, 4], F32) for i in range(4)])
        uT = A.alloc("a_uT", [128, 8, 512], BF16)
        qTr = Rot([A.alloc(f"a_qT{i}", [128, 8, 512], BF16) for i in range(2)])
        gsr = Rot([A.alloc(f"a_gs{i}", [128, 4, D], BF16) for i in range(2)])
        cosg = A.alloc("a_cos", [128, 512], F32, dma=True)
        sing = A.alloc("a_sin", [128, 512], F32, dma=True)
        qbr = Rot([A.alloc(f"a_qb{i}", [128, 512], BF16) for i in range(2)])
        t1r = Rot([A.alloc(f"a_t1{i}", [128, 512], F32) for i in range(2)])
        t2r = Rot([A.alloc(f"a_t2{i}", [128, 512], F32) for i in range(1)])
        pTr = Rot([A.alloc(f"a_pT{i}", [128, 5, 512], BF16) for i in range(2)])
        denr = Rot([A.alloc(f"a_den{i}", [128, 8], F32) for i in range(2)])
        otr = Rot([A.alloc(f"a_ot{i}", [128, 256], F32) for i in range(2)])
        og = A.alloc("a_og", [128, D], BF16)
        ogT = A.alloc("a_ogT", [128, 8, 128], BF16)
        ytr = Rot([A.alloc(f"a_yt{i}", [128, 512], F32) for i in range(2)])
        self.epsc = A.alloc("a_eps", [128, 1], F32)
        self.trb = Rot(self.banks[0:2])
        mmb = Rot(self.banks[2:4])
        sb = Rot(self.banks[4:7])
        ob = self.banks[7]
        W = T("wsrc")
        mk.op(mk.dve, lambda e: e.memset(self.epsc[:], EPS), writes=[self.epsc])
        mk.op(mk.dve, lambda e: e.memset(vaug[:, :, :, 64:65], 1.0), writes=[vaug])
        mk.dma(mk.pool, [(win[:, k, :], self.a_win[j, :, k, :]) for k in range(8)], None, win, reads=[W], writes=[win], max_dma_last_dim=4096)
        mk.dma(mk.pool, [(wout[:, k, :], self.a_wout[j, :, k, :]) for k in range(8)], None, wout, reads=[W], writes=[wout], max_dma_last_dim=4096)
        mk.dma(mk.sp, gate[0][:], self.gates_d[2 * li:2 * li + 1, :].to_broadcast([128, D]), gate[0], reads=[self.GT], writes=[gate[0]])
        mk.dma(mk.sp, gate[1][:], self.gates_d[2 * li + 1:2 * li + 2, :].to_broadcast([128, D]), gate[1], reads=[self.GT], writes=[gate[1]])
        mk.dma(mk.sp, esink[:], self.a_sink[j:j + 1, :].to_broadcast([128, 16]), esink, reads=[W], writes=[esink])
        mk.op(mk.act, lambda e: e.activation(out=esink[:], in_=esink[:], func=AF.Exp), reads=[esink], writes=[esink])
        scale = 1.0 / 8.0

        hin_of = {}
        nload = [0]

        def ensure_loaded(upto):
            while nload[0] <= min(upto, NT - 1):
                t = nload[0]
                b = hinr.next()
                src, srcT = self.h_src(li, t)
                mk.dma(mk.sp, b[:], src, b, reads=[srcT], writes=[b])
                hin_of[t] = b
                nload[0] += 1

        def rope_chunk(c, ntok, dst_T, dst_ap):
            pb = mmb.next()
            self.mm_group(pb, pb[:, 0:ntok], [(win[:, k, c * 128:(c + 1) * 128], uT[:, k, 0:ntok]) for k in range(8)], [win, uT])
            qb = qbr.next()
            mk.op(mk.act, lambda e: e.activation(out=qb[:, 0:ntok], in_=pb[:, 0:ntok], func=AF.Identity), reads=[pb], writes=[qb])
            rb = mmb.next()
            self.mm_group(rb, rb[:, 0:ntok], [(self.prot[:], qb[:, 0:ntok])], [self.prot, qb])
            t1 = t1r.next()
            t2 = t2r.next()
            mk.op(mk.dve, lambda e: e.tensor_tensor(out=t1[:, 0:ntok], in0=pb[:, 0:ntok], in1=cosg[:, 0:ntok], op=ALU.mult),
                  reads=[pb, cosg], writes=[t1])
            mk.op(mk.dve, lambda e: e.tensor_tensor(out=t2[:, 0:ntok], in0=rb[:, 0:ntok], in1=sing[:, 0:ntok], op=ALU.mult),
                  reads=[rb, sing], writes=[t2])
            mk.op(mk.dve, lambda e: e.tensor_tensor(out=dst_ap, in0=t1[:, 0:ntok], in1=t2[:, 0:ntok], op=ALU.add),
                  reads=[t1, t2], writes=[dst_T])

        def stage_p(n):
            tok0, ntok = GROUPS[n]
            nt = ntok // 128
            qT = qTr.next()
            gs = gsr.next()

            def part0():
                mk.dma(mk.sp, cosg[:, 0:ntok], self.k_cos[:, tok0:tok0 + ntok], cosg, reads=[W], writes=[cosg])
                mk.dma(mk.sp, sing[:, 0:ntok], self.k_sin[:, tok0:tok0 + ntok], sing, reads=[W], writes=[sing])
                for ti in range(nt):
                    t = tok0 // 128 + ti
                    ensure_loaded(t + 2)
                    self.norm_tile(li, t, hin_of.pop(t), hnr.next(), str_.next(), uT,
                                   lambda k, ti=ti: uT[:, k, ti * 128:(ti + 1) * 128])

            def part1():
                for c in (8, 9):
                    rope_chunk(c, ntok, kT, kT[:, c - 8, tok0:tok0 + ntok])
                for c in range(0, 3):
                    rope_chunk(c, ntok, qT, qT[:, c, 0:ntok])

            def part2():
                for c in range(3, 8):
                    rope_chunk(c, ntok, qT, qT[:, c, 0:ntok])

            def part3():
                for ti in range(nt):
                    t = tok0 // 128 + ti
                    vb = mmb.next()
                    self.mm_group(vb, vb[:, 0:256], [(uT[:, k, ti * 128:(ti + 1) * 128], win[:, k, 1280:1536]) for k in range(8)], [win, uT])
                    mk.op(mk.act, lambda e, vb=vb, t=t: e.activation(out=vaug[:, t, :, 0:64],
                                                                     in_=vb[:, 0:256].rearrange("p (a b) -> p a b", a=4), func=AF.Identity),
                          reads=[vb], writes=[vaug])
                    for cg in range(2):
                        gb = mmb.next()
                        self.mm_group(gb, gb[:, :], [(uT[:, k, ti * 128:(ti + 1) * 128], win[:, k, 1536 + cg * 512:1536 + (cg + 1) * 512])
                                                     for k in range(8)], [win, uT])
                        mk.op(mk.act, lambda e, gb=gb, ti=ti, cg=cg: e.activation(out=gs[:, ti, cg * 512:(cg + 1) * 512], in_=gb[:, :], func=AF.Silu),
                              reads=[gb], writes=[gs])

            return qT, gs, [part0, part1, part2, part3]

        def stage_a(n, qT, gs, parts=()):
            parts = list(parts)
            tok0, ntok = GROUPS[n]
            nta = ntok // 128
            for ti in range(nta):
                t = tok0 // 128 + ti
                while parts and len(parts) > (nta - 1 - ti) * (4 // max(nta, 1)) - 0 and (len(parts) > 4 - (ti + 1) * (4 // nta)):
                    parts.pop(0)()
                if t < 2:
                    kbs = [(0, None), (1, None)]
                else:
                    kbs = []
                    if t - 1 >= 2:
                        kbs.append((t - 1, self.neglo))
                    kbs.append((t, None))
                    if t + 1 < NT:
                        kbs.append((t + 1, self.neghi))
                    kbs += [(0, None), (1, None)]
                hres = hresr.next()
                src, srcT = self.h_src(li, t)
                mk.dma(mk.sp, hres[:], src, hres, reads=[srcT], writes=[hres])
                def qk_stage(kh):
                    pbase = 64 * (kh % 2)
                    kc = kh // 2
                    pT = pTr.next()
                    for b, (kt, msk) in enumerate(kbs):
                        sbk = sb.next()
                        prs = [(kT[pbase:pbase + 64, kc, kt * 128:(kt + 1) * 128],
                                qT[pbase:pbase + 64, 4 * kc:4 * kc + 4, ti * 128:(ti + 1) * 128])]
                        rds = [kT, qT]
                        if msk is not None:
                            prs.append((self.ident[:], msk[:]))
                            rds += [self.ident, msk]
                        self.mm_group(sbk, sbk[:, :], prs, rds)
                        mk.op(mk.act, lambda e, sbk=sbk, b=b: e.activation(out=pT[:, b, :], in_=sbk[:, :], func=AF.Exp, scale=scale),
                              reads=[sbk], writes=[pT])
                    return pT

                pT_next = qk_stage(0)
                for kh in range(4):
                    pT = pT_next
                    if kh + 1 < 4:
                        pT_next = qk_stage(kh + 1)
                    ov = ob[:, :].rearrange("p (a b) -> p a b", a=4)
                    for r in range(4):
                        self.mm_group(ob, ov[:, r, 0:65],
                                      [(pT[:, b, r * 128:(r + 1) * 128], vaug[:, kt, kh, :]) for b, (kt, msk) in enumerate(kbs)],
                                      [pT, vaug])
                    den = denr.next()
                    ot = otr.next()
                    mk.op(mk.dve, lambda e, kh=kh: e.tensor_tensor(out=den[:, 0:4], in0=ov[:, :, 64], in1=esink[:, 4 * kh:4 * kh + 4], op=ALU.add),
                          reads=[ob, esink], writes=[den])
                    mk.op(mk.dve, lambda e: e.reciprocal(den[:, 4:8], den[:, 0:4]), reads=[den], writes=[den])
                    otv = ot[:, :].rearrange("p (a b) -> p a b", a=4)
                    mk.op(mk.dve, lambda e: e.tensor_tensor(out=otv, in0=ov[:, :, 0:64],
                                                            in1=den[:, 4:8].unsqueeze(2).to_broadcast([128, 4, 64]), op=ALU.mult),
                          reads=[ob, den], writes=[ot])
                    mk.op(mk.dve, lambda e, kh=kh: e.tensor_tensor(out=og[:, kh * 256:(kh + 1) * 256], in0=ot[:, :],
                                                                    in1=gs[:, ti, kh * 256:(kh + 1) * 256], op=ALU.mult),
                          reads=[ot, gs], writes=[og])
                bk = self.trb.next()
                bv = bview(bk[:]).rearrange("p (a b) -> p a b", a=8)
                for k in range(8):
                    mk.op(mk.pe, lambda e, k=k: e.transpose(bv[:, k, :], og[:, k * 128:(k + 1) * 128], self.ident[:]),
                          reads=[og, self.ident], writes=[bk], inc=(k == 7))
                mk.op(mk.dve, lambda e: e.tensor_copy(out=ogT[:], in_=bv), reads=[bk], writes=[ogT])
                gt = gate[1] if t < 2 else gate[0]
                for cg in range(2):
                    yb = mmb.next()
                    self.mm_group(yb, yb[:, :], [(ogT[:, k, :], wout[:, k, cg * 512:(cg + 1) * 512]) for k in range(8)], [ogT, wout])
                    yt = ytr.next()
                    mk.op(mk.dve, lambda e, yb=yb, cg=cg: e.tensor_tensor(out=yt[:], in0=yb[:, :], in1=gt[:, cg * 512:(cg + 1) * 512], op=ALU.mult),
                          reads=[yb, gt], writes=[yt])
                    mk.op(mk.dve, lambda e, cg=cg: e.tensor_tensor(out=hres[:, cg * 512:(cg + 1) * 512], in0=hres[:, cg * 512:(cg + 1) * 512],
                                                                    in1=yt[:], op=ALU.add),
                          reads=[yt, hres], writes=[hres])
                mk.dma(mk.sp, self.h_d[t * 128:(t + 1) * 128, :], hres[:], hres, reads=[hres], writes=[self.HT[t]])
            while parts:
                parts.pop(0)()

        ng = len(GROUPS)
        cur = stage_p(0)
        for f in cur[2]:
            f()
        for n in range(ng):
            nxt = stage_p(n + 1) if n + 1 < ng else None
            stage_a(n, cur[0], cur[1], nxt[2] if nxt is not None else ())
            cur = nxt

    def ssd_layer(self, li):
        mk, A = self.mk, self.arena
        j = li // 2
        last = (li == 3)
        need_ctx = not last
        A.reset()
        W = T("wsrc_s")
        gate = [A.alloc(f"s_gate{i}", [128, D], F32, dma=True) for i in range(2)]
        convw = A.alloc("s_convw", [128, 32, 5], F32, dma=True)
        convb = A.alloc("s_convb", [128, 32], F32, dma=True)
        nwc = A.alloc("s_nwc", [128, 16], F32, dma=True)
        dtb = A.alloc("s_dtb", [128, 64], F32, dma=True)
        arow = A.alloc("s_arow", [128, 64], F32, dma=True)
        drow = A.alloc("s_drow", [128, 32], F32, dma=True)
        fnw = A.alloc("s_fnw", [128, D], F32, dma=True)
        dt_all = A.alloc("s_dt", [128, NT, 64], F32)
        self.epsc = A.alloc("s_eps", [128, 1], F32)
        mk.op(mk.dve, lambda e: e.memset(self.epsc[:], EPS), writes=[self.epsc])
        mk.dma(mk.sp, gate[0][:], self.gates_d[2 * li:2 * li + 1, :].to_broadcast([128, D]), gate[0], reads=[self.GT], writes=[gate[0]])
        mk.dma(mk.sp, gate[1][:], self.gates_d[2 * li + 1:2 * li + 2, :].to_broadcast([128, D]), gate[1], reads=[self.GT], writes=[gate[1]])
        mk.dma(mk.sp, convw[:], self.s_convw[:, j], convw, reads=[W], writes=[convw])
        mk.dma(mk.sp, convb[:], self.s_convb[:, j], convb, reads=[W], writes=[convb])
        mk.dma(mk.sp, nwc[:], self.s_nw[:, j], nwc, reads=[W], writes=[nwc])
        mk.dma(mk.sp, dtb[:], self.s_dtb[j:j + 1, :].to_broadcast([128, 64]), dtb, reads=[W], writes=[dtb])
        mk.dma(mk.sp, arow[:], self.s_alog[j:j + 1, :].to_broadcast([128, 64]), arow, reads=[W], writes=[arow])
        mk.dma(mk.sp, drow[:], self.s_d[j:j + 1, :].to_broadcast([128, 32]), drow, reads=[W], writes=[drow])
        mk.dma(mk.sp, fnw[:], self.fnw[0:1, :].to_broadcast([128, D]), fnw, reads=[W], writes=[fnw])
        mk.op(mk.act, lambda e: e.activation(out=arow[:], in_=arow[:], func=AF.Exp), reads=[arow], writes=[arow])
        mk.op(mk.dve, lambda e: e.tensor_scalar(out=arow[:], in0=arow[:], scalar1=-1.0, scalar2=None, op0=ALU.mult), reads=[arow], writes=[arow])
        mark0 = A.off

        uT = A.alloc("s_uT", [128, 8, UCOLS], BF16)
        mark1 = A.off
        hinr = Rot([A.alloc(f"s_hin{i}", [128, D], F32, dma=True) for i in range(3)])
        hnr = Rot([A.alloc(f"s_hn{i}", [128, D], BF16) for i in range(2)])
        str_ = Rot([A.alloc(f"s_st{i}", [128, 4], F32) for i in range(4)])
        self.trb = Rot(self.banks[0:2])
        mmb = Rot(self.banks[2:5])
        for (a, b) in ((0, 2), (258, 262), (UCOLS - 2, UCOLS)):
            mk.op(mk.dve, lambda e: e.memset(uT[:, :, a:b], 0.0), writes=[uT])
        hin_of = {}
        nload = [0]

        def ensure_loaded(upto):
            while nload[0] <= min(upto, NT - 1):
                t = nload[0]
                b = hinr.next()
                src, srcT = self.h_src(li, t)
                mk.dma(mk.sp, b[:], src, b, reads=[srcT], writes=[b])
                hin_of[t] = b
                nload[0] += 1

        for t in range(NT):
            ensure_loaded(t + 2)
            c0 = ucol(t * 128)
            self.norm_tile(li, t, hin_of.pop(t), hnr.next(), str_.next(), uT, lambda k: uT[:, k, c0:c0 + 128])

        mk.barrier()
        A.reset(mark1)
        wccr = Rot([A.alloc(f"s_wcc{i}", [128, 1024], BF16, dma=True) for i in range(3)])
        prer = Rot([A.alloc(f"s_pre{i}", [128, UCOLS], F32) for i in range(2)])
        accr = Rot([A.alloc(f"s_acc{i}", [128, UCOLS - 4], F32) for i in range(2)])
        xor_ = Rot([A.alloc(f"s_xo{i}", [128, UCOLS - 4], BF16, dma=True) for i in range(2)])
        tstr = Rot([A.alloc(f"s_tst{i}", [128, 8, 128], BF16, dma=True) for i in range(3)])
        NX = UCOLS - 4
        colgroups = [(c, min(512, UCOLS - c)) for c in range(0, UCOLS, 512)]
        tilegroups = [list(range(a, min(a + 8, NT))) for a in range(0, NT, 8)]

        def xo_idx(tok):
            return tok if tok < TC else tok + 4

        def s0b_proj(cc):
            wcc = wccr.next()
            mk.dma(mk.pool, wcc[:], self.s_wxbc[j, cc], wcc, reads=[W], writes=[wcc], max_dma_last_dim=4096)
            pre = prer.next()
            for (c0, n) in colgroups:
                bk = mmb.next()
                self.mm_group(bk, bk[:, 0:n], [(wcc[:, k * 128:(k + 1) * 128], uT[:, k, c0:c0 + n]) for k in range(8)], [wcc, uT])
                mk.op(mk.act, lambda e: e.activation(out=pre[:, c0:c0 + n], in_=bk[:, 0:n], func=AF.Identity), reads=[bk], writes=[pre])
            return pre

        def s0b_conv(cc, pre):
            acc = accr.next()
            mk.op(mk.dve, lambda e: e.tensor_scalar(out=acc[:], in0=pre[:, 0:NX], scalar1=convw[:, cc, 0:1], scalar2=None, op0=ALU.mult),
                  reads=[pre, convw], writes=[acc])
            for k in range(1, 5):
                mk.op(mk.dve, lambda e: e.scalar_tensor_tensor(out=acc[:], in0=pre[:, k:k + NX], scalar=convw[:, cc, k:k + 1], in1=acc[:],
                                                               op0=ALU.mult, op1=ALU.add),
                      reads=[pre, convw, acc], writes=[acc])
            xo = xor_.next()
            mk.op(mk.act, lambda e: e.activation(out=xo[:], in_=acc[:], func=AF.Silu, bias=convb[:, cc:cc + 1]),
                  reads=[acc, convb], writes=[xo])
            return xo

        def s0b_out(cc, xo):
            if cc < 24:
                dst_d, dstT, coff = (self.xs_d, self.XS, cc * 128) if cc < 16 else (self.bt_d, self.BTK, (cc - 16) * 128)
                for tg in tilegroups:
                    nu = len(tg)
                    bk = self.trb.next()
                    bv = bview(bk[:]).rearrange("p (a b) -> p a b", a=8)
                    for u, t in enumerate(tg):
                        i0 = xo_idx(t * 128)
                        mk.op(mk.pe, lambda e: e.transpose(bv[:, u, :], xo[:, i0:i0 + 128], self.ident[:]),
                              reads=[xo, self.ident], writes=[bk], inc=(u == nu - 1))
                    ts_ = tstr.next()
                    mk.op(mk.act, lambda e: e.activation(out=ts_[:, 0:nu, :], in_=bv[:, 0:nu, :], func=AF.Identity), reads=[bk], writes=[ts_])
                    mk.dma(mk.sp, dst_d[tg[0] * 128:(tg[0] + nu) * 128, coff:coff + 128].rearrange("(u p) c -> p u c", p=128),
                           ts_[:, 0:nu, :], ts_, reads=[ts_], writes=[dstT[t] for t in tg])
            if cc >= 16:
                g = (cc - 16) % 8
                dd, ddT = (self.BT_d, self.BTT) if cc < 24 else (self.CT_d, self.CTT)
                mk.dma(mk.sp, [(dd[g, :, 0:TC], xo[:, 0:TC]), (dd[g, :, TC:TT], xo[:, TC + 4:TT + 4])], None, xo,
                       reads=[xo], writes=[ddT[g]])

        pre_cur = s0b_proj(0)
        for cc in range(32):
            pre_nxt = s0b_proj(cc + 1) if cc + 1 < 32 else None
            xo = s0b_conv(cc, pre_cur)
            s0b_out(cc, xo)
            pre_cur = pre_nxt

        mk.barrier()
        A.reset(mark1)
        wz = A.alloc("s_wz", [128, 8, 2112], BF16, dma=True)
        zstr = Rot([A.alloc(f"s_zst{i}", [128, 2048], BF16, dma=True) for i in range(2)])
        dtmr = Rot([A.alloc(f"s_dtm{i}", [128, 128], F32) for i in range(2)])
        mk.dma(mk.pool, [(wz[:, k, :], self.s_wzdt[j, :, k, :]) for k in range(8)], None, wz, reads=[W], writes=[wz], max_dma_last_dim=4096)
        for t in range(NT):
            c0 = ucol(t * 128)
            zst = zstr.next()
            for cg in range(4):
                bk = mmb.next()
                self.mm_group(bk, bk[:, :], [(uT[:, k, c0:c0 + 128], wz[:, k, cg * 512:(cg + 1) * 512]) for k in range(8)], [uT, wz])
                mk.op(mk.act, lambda e: e.activation(out=zst[:, cg * 512:(cg + 1) * 512], in_=bk[:, :], func=AF.Silu), reads=[bk], writes=[zst])
            mk.dma(mk.sp, self.z_d[t * 128:(t + 1) * 128, :], zst[:], zst, reads=[zst], writes=[self.ZT[t]])
            bk = mmb.next()
            self.mm_group(bk, bk[:, 0:64], [(uT[:, k, c0:c0 + 128], wz[:, k, 2048:2112]) for k in range(8)], [uT, wz])
            mk.op(mk.dve, lambda e: e.tensor_tensor(out=dt_all[:, t, :], in0=bk[:, 0:64], in1=dtb[:], op=ALU.add), reads=[bk, dtb], writes=[dt_all])

        dtf = dt_all[:].rearrange("p a b -> p (a b)")
        mk.op(mk.act, lambda e: e.activation(out=dtf, in_=dtf, func=AF.Exp), reads=[dt_all], writes=[dt_all])
        mk.op(mk.act, lambda e: e.activation(out=dtf, in_=dtf, func=AF.Ln, bias=1.0), reads=[dt_all], writes=[dt_all])

        mk.barrier()
        A.reset(mark0)
        xsr = Rot([A.alloc(f"s_xs{i}", [128, 2048], BF16, dma=True) for i in range(3)])
        btr = Rot([A.alloc(f"s_bt{i}", [128, 1024], BF16, dma=True) for i in range(3)])
        smr = Rot([A.alloc(f"s_sm{i}", [128, 256], F32) for i in range(3)])
        xwr = Rot([A.alloc(f"s_xw{i}", [128, 2048], BF16) for i in range(3)])
        hT = A.alloc("s_hT", [128, 2048], F32)
        hbr = Rot([A.alloc(f"s_hb{i}", [128, 2048], BF16, dma=True) for i in range(2)])
        stb = self.banks[4:8]
        mmb = Rot(self.banks[2:4])
        mark2 = A.off
        for d in range(2):
            order = list(range(NT)) if d == 0 else [1, 0] + list(range(NT - 1, 1, -1))
            mk.op(mk.dve, lambda e: e.memset(hT[:], 0.0), writes=[hT])

            def p1_front(t):
                xs_t = xsr.next()
                bt_t = btr.next()
                mk.dma(mk.sp, xs_t[:], self.xs_d[t * 128:(t + 1) * 128, :], xs_t, reads=[self.XS[t]], writes=[xs_t])
                mk.dma(mk.sp, bt_t[:], self.bt_d[t * 128:(t + 1) * 128, :], bt_t, reads=[self.BTK[t]], writes=[bt_t])
                sm = smr.next()
                dts = dt_all[:, t, d * 32:(d + 1) * 32]
                mk.op(mk.dve, lambda e: e.tensor_tensor(out=sm[:, 0:32], in0=dts, in1=arow[:, d * 32:(d + 1) * 32], op=ALU.mult),
                      reads=[dt_all, arow], writes=[sm])
                bk = mmb.next()
                self.mm_group(bk, bk[:, 0:32], [(self.tri[:, d, :], sm[:, 0:32])], [self.tri, sm])
                self.mm_group(bk, bk[:, 32:64], [(self.onesf[:], sm[:, 0:32])], [self.onesf, sm])
                mk.op(mk.act, lambda e: e.activation(out=sm[:, 32:96], in_=bk[:, 0:64], func=AF.Identity), reads=[bk], writes=[sm])
                mk.op(mk.dve, lambda e: e.tensor_tensor(out=sm[:, 96:128], in0=sm[:, 64:96], in1=sm[:, 32:64], op=ALU.subtract), reads=[sm], writes=[sm])
                mk.op(mk.act, lambda e: e.activation(out=sm[:, 96:128], in_=sm[:, 96:128], func=AF.Exp), reads=[sm], writes=[sm])
                mk.op(mk.act, lambda e: e.activation(out=sm[:, 160:192], in_=sm[:, 64:96], func=AF.Exp), reads=[sm], writes=[sm])
                mk.op(mk.dve, lambda e: e.tensor_tensor(out=sm[:, 128:160], in0=sm[:, 96:128], in1=dts, op=ALU.mult), reads=[sm, dt_all], writes=[sm])
                xw = xwr.next()
                mk.op(mk.dve, lambda e: e.tensor_tensor(out=xw[:].rearrange("p (h c) -> p h c", h=32),
                                                        in0=xs_t[:].rearrange("p (h c) -> p h c", h=32),
                                                        in1=sm[:, 128:160].unsqueeze(2).to_broadcast([128, 32, 64]), op=ALU.mult),
                      reads=[xs_t, sm], writes=[xw])
                return sm, bt_t, xw

            def p1_update(t, fr):
                sm, bt_t, xw = fr
                hb = hbr.next()
                mk.op(mk.act, lambda e: e.activation(out=hb[:], in_=hT[:], func=AF.Identity), reads=[hT], writes=[hb])
                mk.dma(mk.sp, self.hp_d[d, t], hb[:], hb, reads=[hb], writes=[self.HP[d][t]])
                for g in range(8):
                    sbk = stb[g // 2]
                    self.mm_group(sbk, sbk[:, (g % 2) * 256:(g % 2 + 1) * 256], [(bt_t[:, g * 128:(g + 1) * 128], xw[:, g * 256:(g + 1) * 256])], [bt_t, xw])
                hv = hT[:].rearrange("p (h c) -> p h c", h=32)
                mk.op(mk.dve, lambda e: e.tensor_tensor(out=hv, in0=hv, in1=sm[:, 160:192].unsqueeze(2).to_broadcast([128, 32, 64]), op=ALU.mult),
                      reads=[hT, sm], writes=[hT])
                for i in range(4):
                    mk.op(mk.dve, lambda e: e.tensor_tensor(out=hT[:, i * 512:(i + 1) * 512], in0=hT[:, i * 512:(i + 1) * 512], in1=stb[i][:, :], op=ALU.add),
                          reads=[hT, stb[i]], writes=[hT])

            fr = p1_front(order[0])
            for i, t in enumerate(order):
                fr_n = p1_front(order[i + 1]) if i + 1 < len(order) else None
                p1_update(t, fr)
                fr = fr_n

        mk.barrier()
        A.reset(mark0)
        wout = A.alloc("s_wout", [128, 16, D], BF16, dma=True)
        mk.dma(mk.pool, [(wout[:, k, :], self.s_wout[j, :, k, :]) for k in range(16)], None, wout, reads=[W], writes=[wout])
        xsr = Rot([A.alloc(f"p_xs{i}", [128, 2048], BF16, dma=True) for i in range(2)])
        BTr = Rot([A.alloc(f"p_BT{i}", [128, 8, 128], BF16, dma=True) for i in range(2)])
        CTr = Rot([A.alloc(f"p_CT{i}", [128, 8, 128], BF16, dma=True) for i in range(2)])
        zr = Rot([A.alloc(f"p_z{i}", [128, 2048], BF16, dma=True) for i in range(2)])
        hpfr = Rot([A.alloc(f"p_hpf{i}", [128, 2048], BF16, dma=True) for i in range(2)])
        hpbr = Rot([A.alloc(f"p_hpb{i}", [128, 2048], BF16, dma=True) for i in range(2)])
        hresr = Rot([A.alloc(f"p_hres{i}", [128, D], F32, dma=True) for i in range(3)])
        smr = Rot([A.alloc(f"p_sm{i}", [128, 384], F32) for i in range(2)])
        sel = A.alloc("p_sel", [128, 32, 128], BF16, dma=True)
        mk.dma(mk.pool, sel[0:64].rearrange("p a b -> p (a b)"), self.k_sel, sel, reads=[W], writes=[sel], max_dma_last_dim=4096)
        aThr = Rot([A.alloc(f"p_aTh{i}", [128, 128], BF16) for i in range(2)])
        aTlr = Rot([A.alloc(f"p_aTl{i}", [128, 128], BF16) for i in range(2)])
        Er = Rot([A.alloc(f"p_E{i}", [128, 4, 128], BF16) for i in range(2)])
        xpr = Rot([A.alloc(f"p_xp{i}", [128, 4, 128], F32) for i in range(2)])
        Mall = A.alloc("p_M", [128, 16, 4, 128], BF16)
        cbs = A.alloc("p_cbs", [128, 8, 128], BF16)
        dident = A.alloc("p_dident", [128, 32, 128], BF16)
        for h in range(32):
            mk.op(mk.dve, lambda e: e.tensor_scalar(out=dident[:, h, :], in0=self.ident[:], scalar1=drow[:, h:h + 1], scalar2=None, op0=ALU.mult),
                  reads=[self.ident, drow], writes=[dident])
        xdtr = Rot([[A.alloc(f"p_xdt{d}_{i}", [128, 2048], BF16) for d in range(2)] for i in range(2)])
        t1r = Rot([A.alloc(f"p_t1{i}", [128, 256], F32) for i in range(2)])
        t2r = Rot([A.alloc(f"p_t2{i}", [128, 256], F32) for i in range(2)])
        yall = A.alloc("p_yall", [128, 2048], F32)
        t3 = A.alloc("p_t3", [128, 2048], F32, dma=True)
        un = A.alloc("p_un", [128, 2048], BF16)
        ynT = A.alloc("p_ynT", [128, 16, 128], BF16)
        ytr = Rot([A.alloc(f"p_yt{i}", [128, 512], F32) for i in range(2)])
        mmb = Rot(self.banks[0:2])
        ebr = Rot(self.banks[2:4])
        ABr = Rot([(self.banks[4], self.banks[5]), (self.banks[6], self.banks[7])])
        print("pass2 arena bytes", A.off * 4, "of", A.cap * 4)
        tiles = list(range(NT)) if need_ctx else list(range(2, NT))

        def p2_load(t):
            xs_t, BTt, CTt, z_t, hpf, hpb, hres = xsr.next(), BTr.next(), CTr.next(), zr.next(), hpfr.next(), hpbr.next(), hresr.next()
            mk.dma(mk.sp, BTt[:], self.BT_d[:, :, t * 128:(t + 1) * 128].rearrange("g n t -> n g t"), BTt, reads=self.BTT, writes=[BTt])
            mk.dma(mk.sp, CTt[:], self.CT_d[:, :, t * 128:(t + 1) * 128].rearrange("g n t -> n g t"), CTt, reads=self.CTT, writes=[CTt])
            mk.dma(mk.sp, xs_t[:], self.xs_d[t * 128:(t + 1) * 128, :], xs_t, reads=[self.XS[t]], writes=[xs_t])
            mk.dma(mk.sp, hpf[:], self.hp_d[0, t], hpf, reads=[self.HP[0][t]], writes=[hpf])
            mk.dma(mk.sp, hpb[:], self.hp_d[1, t], hpb, reads=[self.HP[1][t]], writes=[hpb])
            mk.dma(mk.sp, z_t[:], self.z_d[t * 128:(t + 1) * 128, :], z_t, reads=[self.ZT[t]], writes=[z_t])
            src, srcT = self.h_src(li, t)
            mk.dma(mk.sp, hres[:], src, hres, reads=[srcT], writes=[hres])
            return xs_t, BTt, CTt, z_t, hpf, hpb, hres

        def p2_front(t, bufs):
            xs_t, BTt, CTt, z_t, hpf, hpb, hres = bufs
            sm = smr.next()
            mk.op(mk.dve, lambda e: e.tensor_tensor(out=sm[:, 0:64], in0=dt_all[:, t, :], in1=arow[:], op=ALU.mult), reads=[dt_all, arow], writes=[sm])
            bk = mmb.next()
            self.mm_group(bk, bk[:, 0:32], [(self.tri[:, 0, :], sm[:, 0:32])], [self.tri, sm])
            self.mm_group(bk, bk[:, 32:64], [(self.tri[:, 1, :], sm[:, 32:64])], [self.tri, sm])
            mk.op(mk.act, lambda e: e.activation(out=sm[:, 64:128], in_=bk[:, 0:64], func=AF.Exp), reads=[bk], writes=[sm])
            mk.op(mk.act, lambda e: e.activation(out=sm[:, 192:256], in_=bk[:, 0:64], func=AF.Identity, scale=-1.0), reads=[bk], writes=[sm])
            mk.op(mk.act, lambda e: e.activation(out=sm[:, 256:320], in_=bk[:, 0:64], func=AF.Identity), reads=[bk], writes=[sm])
            bkT = mmb.next()
            mk.op(mk.pe, lambda e: e.transpose(bkT[0:64, 0:128], sm[:, 256:320], self.identf[:]), reads=[sm, self.identf], writes=[bkT])
            aTh, aTl = aThr.next(), aTlr.next()
            mk.op(mk.dve, lambda e: e.tensor_copy(out=aTh[0:64, :], in_=bkT[0:64, 0:128]), reads=[bkT], writes=[aTh])
            mk.op(mk.dve, lambda e: e.tensor_tensor(out=aTl[0:64, :], in0=bkT[0:64, 0:128], in1=aTh[0:64, :], op=ALU.subtract), reads=[bkT, aTh], writes=[aTl])
            for hf in range(2):
                bk = mmb.next()
                for g4 in range(4):
                    g = hf * 4 + g4
                    self.mm_group(bk, bk[:, g4 * 128:(g4 + 1) * 128], [(BTt[:, g, :], CTt[:, g, :])], [BTt, CTt])
                mk.op(mk.act, lambda e: e.activation(out=cbs[:, hf * 4:(hf + 1) * 4, :].rearrange("p a b -> p (a b)"), in_=bk[:, :], func=AF.Identity),
                      reads=[bk], writes=[cbs])
            return sm, aTh, aTl

        def p2_front_main(t, bufs, pre, steps=()):
            steps = list(steps)
            xs_t, BTt, CTt, z_t, hpf, hpb, hres = bufs
            sm, aTh, aTl = pre
            xdtb = xdtr.next()
            for d in range(2):
                x_ = xdtb[d]
                mk.op(mk.dve, lambda e: e.tensor_tensor(out=x_[:].rearrange("p (h c) -> p h c", h=32),
                                                         in0=xs_t[:].rearrange("p (h c) -> p h c", h=32),
                                                         in1=dt_all[:, t, d * 32:(d + 1) * 32].unsqueeze(2).to_broadcast([128, 32, 64]), op=ALU.mult),
                      reads=[xs_t, dt_all], writes=[x_])
            for g in range(8):
                for d in range(2):
                    it = 2 * g + d
                    if steps and it >= 2 and (it % 2 == 0 or len(steps) > (16 - it) // 2 + 1):
                        steps.pop(0)()
                    eb = ebr.next()
                    msk = self.neghi if d == 0 else self.neglo
                    mk.op(mk.pe, lambda e: e.matmul(eb[:, :], self.ident[:], msk[:], start=True, stop=False),
                          reads=[self.ident, msk], writes=[eb], inc=False)
                    for r in range(4):
                        h = 4 * g + r
                        mk.op(mk.pe, lambda e: e.matmul(eb[:, r * 128:(r + 1) * 128], sel[32 * d:32 * d + 32, h, :], aTh[32 * d:32 * d + 32, :],
                                                        start=False, stop=False),
                              reads=[sel, aTh], writes=[eb], inc=False)
                        mk.op(mk.pe, lambda e: e.matmul(eb[:, r * 128:(r + 1) * 128], sel[32 * d:32 * d + 32, h, :], aTl[32 * d:32 * d + 32, :],
                                                        start=False, stop=(r == 3)),
                              reads=[sel, aTl], writes=[eb], inc=(r == 3))
                    xp = xpr.next()
                    mk.op(mk.dve, lambda e: e.tensor_tensor(out=xp[:], in0=eb[:, :].rearrange("p (a b) -> p a b", a=4),
                                                            in1=sm[:, 192 + 32 * d + 4 * g:192 + 32 * d + 4 * g + 4].unsqueeze(2).to_broadcast([128, 4, 128]),
                                                            op=ALU.add),
                          reads=[eb, sm], writes=[xp])
                    E = Er.next()
                    mk.op(mk.act, lambda e: e.activation(out=E[:].rearrange("p a b -> p (a b)"), in_=xp[:].rearrange("p a b -> p (a b)"), func=AF.Exp),
                          reads=[xp], writes=[E])
                    mk.op(mk.dve, lambda e: e.tensor_tensor(out=Mall[:, 2 * g + d], in0=E[:],
                                                            in1=cbs[:, g, :].unsqueeze(1).to_broadcast([128, 4, 128]), op=ALU.mult),
                          reads=[E, cbs], writes=[Mall])
            while steps:
                steps.pop(0)()
            return sm, xdtb

        def p2_ystage(t, bufs, smx):
            sm, xdtb = smx
            xs_t, BTt, CTt, z_t, hpf, hpb, hres = bufs
            pend = None

            def fin(g_, Ab_, t1_):
                mk.op(mk.dve, lambda e: e.tensor_tensor(out=yall[:, g_ * 256:(g_ + 1) * 256], in0=Ab_[:, 0:256], in1=t1_[:], op=ALU.add),
                      reads=[Ab_, t1_], writes=[yall])

            for g in range(8):
                Ab, Bb = ABr.next()
                self.mm_group(Ab, Ab[:, 256:512], [(CTt[:, g, :], hpf[:, g * 256:(g + 1) * 256])], [CTt, hpf])
                self.mm_group(Bb, Bb[:, 0:256], [(CTt[:, g, :], hpb[:, g * 256:(g + 1) * 256])], [CTt, hpb])
                for r in range(4):
                    h = 4 * g + r
                    self.mm_group(Ab, Ab[:, r * 64:(r + 1) * 64],
                                  [(Mall[:, 2 * g, r, :], xdtb[0][:, h * 64:(h + 1) * 64]), (Mall[:, 2 * g + 1, r, :], xdtb[1][:, h * 64:(h + 1) * 64]),
                                   (dident[:, h, :], xs_t[:, h * 64:(h + 1) * 64])],
                                  [Mall, xdtb[0], xdtb[1], dident, xs_t])
                t1, t2 = t1r.next(), t2r.next()
                mk.op(mk.dve, lambda e: e.tensor_tensor(out=t1[:].rearrange("p (a b) -> p a b", a=4), in0=Ab[:, 256:512].rearrange("p (a b) -> p a b", a=4),
                                                        in1=sm[:, 64 + 4 * g:64 + 4 * g + 4].unsqueeze(2).to_broadcast([128, 4, 64]), op=ALU.mult),
                      reads=[Ab, sm], writes=[t1])
                mk.op(mk.dve, lambda e: e.tensor_tensor(out=t2[:].rearrange("p (a b) -> p a b", a=4), in0=Bb[:, 0:256].rearrange("p (a b) -> p a b", a=4),
                                                        in1=sm[:, 96 + 4 * g:96 + 4 * g + 4].unsqueeze(2).to_broadcast([128, 4, 64]), op=ALU.mult),
                      reads=[Bb, sm], writes=[t2])
                mk.op(mk.dve, lambda e: e.tensor_tensor(out=t1[:], in0=t1[:], in1=t2[:], op=ALU.add), reads=[t1, t2], writes=[t1])
                fin(g, Ab, t1)

        def p2_post_steps(t, bufs, smx):
            sm = smx[0]
            xs_t, BTt, CTt, z_t, hpf, hpb, hres = bufs
            cond = 1 if t < 2 else 0
            steps = []

            def s_gate():
                mk.op(mk.dve, lambda e: e.tensor_tensor(out=t3[:], in0=yall[:], in1=z_t[:], op=ALU.mult), reads=[yall, z_t], writes=[t3])
            steps.append(s_gate)

            def s_sq():
                mk.op(mk.act, lambda e: e.activation(out=yall[:], in_=t3[:], func=AF.Square), reads=[t3], writes=[yall])
            steps.append(s_sq)

            def s_red():
                mk.op(mk.dve, lambda e: e.tensor_reduce(out=sm[:, 128:136], in_=yall[:].rearrange("p (a b) -> p a b", a=8), axis=AX.X, op=ALU.add),
                      reads=[yall], writes=[sm])
            steps.append(s_red)

            def s_sqrt():
                mk.op(mk.act, lambda e: e.activation(out=sm[:, 136:144], in_=sm[:, 128:136], func=AF.Ln, scale=1.0 / 256, bias=self.epsc[:, 0:1]),
                      reads=[sm, self.epsc], writes=[sm])
            steps.append(s_sqrt)

            def s_un():
                mk.op(mk.act, lambda e: e.activation(out=sm[:, 144:152], in_=sm[:, 136:144], func=AF.Exp, scale=-0.5), reads=[sm], writes=[sm])
                mk.op(mk.dve, lambda e: e.tensor_tensor(out=un[:].rearrange("p (a b) -> p a b", a=8), in0=t3[:].rearrange("p (a b) -> p a b", a=8),
                                                        in1=sm[:, 144:152].unsqueeze(2).to_broadcast([128, 8, 256]), op=ALU.mult),
                      reads=[t3, sm], writes=[un])
            steps.append(s_un)

            def mk_tr(hb_):
                def f():
                    bk = mmb.next()
                    bv = bview(bk[:]).rearrange("p (a b) -> p a b", a=8)
                    for k in range(8):
                        kk = hb_ * 8 + k
                        mk.op(mk.pe, lambda e: e.transpose(bv[:, k, :], un[:, kk * 128:(kk + 1) * 128], self.ident[:]),
                              reads=[un, self.ident], writes=[bk], inc=(k == 7))
                    mk.op(mk.dve, lambda e: e.tensor_tensor(out=ynT[:, hb_ * 8:(hb_ + 1) * 8, :], in0=bv,
                                                            in1=nwc[:, hb_ * 8:(hb_ + 1) * 8].unsqueeze(2).to_broadcast([128, 8, 128]), op=ALU.mult),
                          reads=[bk, nwc], writes=[ynT])
                return f
            steps.append(mk_tr(0))
            steps.append(mk_tr(1))

            def mk_out(cg):
                def f():
                    yb = mmb.next()
                    self.mm_group(yb, yb[:, :], [(ynT[:, k, :], wout[:, k, cg * 512:(cg + 1) * 512]) for k in range(16)], [ynT, wout])
                    yt = ytr.next()
                    mk.op(mk.dve, lambda e: e.tensor_tensor(out=yt[:], in0=yb[:, :], in1=gate[cond][:, cg * 512:(cg + 1) * 512], op=ALU.mult),
                          reads=[yb, gate[cond]], writes=[yt])
                    mk.op(mk.dve, lambda e: e.tensor_tensor(out=hres[:, cg * 512:(cg + 1) * 512], in0=hres[:, cg * 512:(cg + 1) * 512], in1=yt[:], op=ALU.add),
                          reads=[yt, hres], writes=[hres])
                return f
            steps.append(mk_out(0))
            steps.append(mk_out(1))

            def s_store():
                if (not last) or self.dump_h:
                    mk.dma(mk.sp, self.h_d[t * 128:(t + 1) * 128, :], hres[:], hres, reads=[hres], writes=[self.HT[t]])
                if last and not self.dump_h:
                    ost = t3
                    mk.op(mk.act, lambda e: e.activation(out=ost[:, 0:D], in_=hres[:], func=AF.Square, accum_out=sm[:, 160:161]), reads=[hres], writes=[ost, sm])
                    mk.op(mk.act, lambda e: e.activation(out=sm[:, 161:162], in_=sm[:, 160:161], func=AF.Sqrt, scale=1.0 / D, bias=self.epsc[:, 0:1]),
                          reads=[sm, self.epsc], writes=[sm])
                    mk.op(mk.dve, lambda e: e.reciprocal(sm[:, 162:163], sm[:, 161:162]), reads=[sm], writes=[sm])
                    mk.op(mk.dve, lambda e: e.tensor_scalar(out=ost[:, 0:D], in0=hres[:], scalar1=sm[:, 162:163], scalar2=None, op0=ALU.mult),
                          reads=[hres, sm], writes=[ost])
                    mk.op(mk.dve, lambda e: e.tensor_tensor(out=ost[:, 0:D], in0=ost[:, 0:D], in1=fnw[:], op=ALU.mult), reads=[ost, fnw], writes=[ost])
                    mk.dma(mk.sp, self.out_d[(t - 2) * 128:(t - 1) * 128, :], ost[:, 0:D], ost, reads=[ost], writes=[self.OUT[t]])
            steps.append(s_store)
            return steps

        cur = p2_load(tiles[0])
        sm_cur = p2_front_main(tiles[0], cur, p2_front(tiles[0], cur))
        for i, t in enumerate(tiles):
            nxt = p2_load(tiles[i + 1]) if i + 1 < len(tiles) else None
            p2_ystage(t, cur, sm_cur)
            steps = p2_post_steps(t, cur, sm_cur)
            if nxt is not None:
                pre = p2_front(tiles[i + 1], nxt)
                sm_nxt = p2_front_main(tiles[i + 1], nxt, pre, steps)
            else:
                sm_nxt = None
                for f in steps:
                    f()
            cur, sm_cur = nxt, sm_nxt


def _consts():
    f32 = np.float32
    k = {}
    k["k_ident"] = np.eye(128, dtype=f32)
    prot = np.zeros((128, 128), f32)
    for base in range(0, 128, 32):
        for d in range(16):
            prot[base + d + 16, base + d] = -1.0
            prot[base + d, base + d + 16] = 1.0
    k["k_prot"] = prot
    c = np.arange(128)[:, None]
    i = np.arange(128)[None, :]
    lo = np.where(c >= i, 0.0, NEG).astype(f32)
    hi = np.where(c <= i, 0.0, NEG).astype(f32)
    k["k_neglo"] = np.tile(lo, (1, 4))
    k["k_neghi"] = np.tile(hi, (1, 4))
    half = 32
    inv_freq = (10000.0 ** (-np.arange(0, half, 2, dtype=f32) / f32(half))).astype(f32)
    pos = np.arange(TL)
    row = (pos // 64).astype(f32)
    col = (pos % 64).astype(f32)
    cosT = np.ones((128, TT), f32)
    sinT = np.zeros((128, TT), f32)
    for d in range(128):
        dl = d % 64
        p = row if dl < 32 else col
        f = (dl % 32) % 16
        ang = (p * inv_freq[f]).astype(f32)
        cosT[d, TC:] = np.cos(ang).astype(f32)
        sinT[d, TC:] = np.sin(ang).astype(f32)
    k["k_cos"] = cosT
    k["k_sin"] = sinT
    t = np.arange(128)[:, None]
    s_ = np.arange(128)[None, :]
    tri = np.stack([(t <= s_), (t >= s_), (t > s_), (t < s_)], axis=1).astype(f32)
    k["k_tri"] = np.ascontiguousarray(tri)
    sel = np.zeros((64, 32, 128), f32)
    for kk in range(64):
        sel[kk, kk % 32, :] = 1.0
    k["k_sel"] = sel.reshape(64, 32 * 128)
    return k


def _pk(w):
    K, N = w.shape
    return np.ascontiguousarray(w.reshape(K // 128, 128, N).transpose(1, 0, 2))


def prep_shared(inputs):
    f32 = np.float32
    g = {}
    g["w_ada"] = np.stack([_pk(inputs["w_ada"][l]) for l in range(4)])
    b = inputs["b_ada"]
    g["b_col"] = np.ascontiguousarray(b[:, :2048].reshape(4, 16, 128).transpose(2, 0, 1))
    g["b_gate"] = np.ascontiguousarray(b[:, 2048:])
    qorder = []
    for cidx in range(8):
        a = cidx if cidx < 4 else 8 + (cidx - 4)
        bb = 4 + cidx if cidx < 4 else 12 + (cidx - 4)
        qorder += list(range(a * 64, a * 64 + 64)) + list(range(bb * 64, bb * 64 + 64))
    cols = np.array(qorder + list(range(1024, 2560)))
    g["attn_w_in"] = np.stack([_pk(inputs["attn_w_in"][j][:, cols]) for j in range(2)])
    g["attn_w_out"] = np.stack([_pk(inputs["attn_w_out"][j]) for j in range(2)])
    g["attn_sink"] = np.ascontiguousarray(inputs["attn_sink"])
    w = inputs["ssd_w_in"]
    xbc = w[:, :, 2048:2048 + 4096]
    g["ssd_w_xbc"] = np.ascontiguousarray(
        xbc.reshape(2, 8, 128, 32, 128).transpose(0, 3, 2, 1, 4).reshape(2, 32, 128, 1024))
    zdt = np.concatenate([w[:, :, :2048], w[:, :, 2048 + 4096:]], axis=2)
    g["ssd_w_zdt"] = np.stack([_pk(zdt[j]) for j in range(2)])
    g["ssd_conv_w"] = np.ascontiguousarray(inputs["ssd_conv_w"].reshape(2, 5, 32, 128).transpose(3, 0, 2, 1))
    g["ssd_conv_b"] = np.ascontiguousarray(inputs["ssd_conv_b"].reshape(2, 32, 128).transpose(2, 0, 1))
    g["ssd_dt_bias"] = np.ascontiguousarray(inputs["ssd_dt_bias"].reshape(2, 64))
    g["ssd_a_log"] = np.ascontiguousarray(inputs["ssd_a_log"].reshape(2, 64))
    g["ssd_d"] = np.ascontiguousarray(inputs["ssd_d"])
    g["ssd_norm_w"] = np.ascontiguousarray(inputs["ssd_norm_w"].reshape(2, 16, 128).transpose(2, 0, 1))
    g["ssd_w_out"] = np.stack([_pk(inputs["ssd_w_out"][j]) for j in range(2)])
    g["final_norm_w"] = np.ascontiguousarray(inputs["final_norm_w"].reshape(1, 1024))
    g.update(_consts())
    return {k_: np.ascontiguousarray(v, dtype=f32) for k_, v in g.items()}


def prep_core(inputs, shared, b):
    m = dict(shared)
    m["x"] = np.ascontiguousarray(inputs["x"][b], dtype=np.float32)
    m["ctx"] = np.ascontiguousarray(inputs["ctx"][b], dtype=np.float32)
    cc = np.stack([inputs["c"][b].reshape(8, 128).T, inputs["c_ctx"].reshape(8, 128).T], axis=2)
    m["c_col2"] = np.ascontiguousarray(cc, dtype=np.float32)
    return m


_NC_CACHE = {}


def kernel(**inputs):
    inputs = {k_: np.asarray(v) for k_, v in inputs.items()}
    if "nc" not in _NC_CACHE:
        _NC_CACHE["nc"] = Prog(n_layers=4).build()
    nc = _NC_CACHE["nc"]
    shared = prep_shared(inputs)
    in_maps = [prep_core(inputs, shared, b) for b in range(8)]
    res = run_bass_kernel_spmd(nc, in_maps, core_ids=list(range(8)))
    return np.stack([r["out"] for r in res.results], axis=0).astype(np.float32)
```
